# Optimizing a Trainium2 kernel written in Bass

```python
import jax
import jax.numpy as jnp
from jax import lax
import numpy as np

D_MODEL = 2048
BATCH = 4
SEQ = 4096
DEPTH = 4

GRID_W = 64
CTX_LEN = 256
N_EVEN = (DEPTH + 1) // 2
N_ODD = DEPTH // 2
EPS_RMS = 1e-6
D_A = D_MODEL // 2
HEAD_A = 64
H_A = D_A // HEAD_A
R_W = 64
R_A = 64
GN_EPS = 64e-5
D_B = D_MODEL // 2
HEAD_B = 64
H_B = D_B // HEAD_B
NA_WIN_H = 8
NA_WIN_W = 16
RG_BLOCKS = 16
D_C = (4 * D_MODEL // 3) // (RG_BLOCKS * 16) * (RG_BLOCKS * 16)
RG_BS = D_C // RG_BLOCKS
CONV_W = 4
CONV_LEFT = 2
RGLRU_C = 8.0
SCAN_DIRECTIONS = (False, True)

A_SHIFTED = 3 * D_A + 2 * R_W + 2 * R_A
EV_IN = A_SHIFTED + D_A + 4 * D_B
EV_SPLITS = (A_SHIFTED, A_SHIFTED + D_A, A_SHIFTED + D_A + D_B, A_SHIFTED + D_A + 2 * D_B, A_SHIFTED + D_A + 3 * D_B)
RW_SPLITS = (D_A, 2 * D_A, 3 * D_A, 3 * D_A + 2 * R_W)

kernel_name = "hybrid_rwkv7_natten_rglru_prefix_dit"


def rmsnorm(x, g):
    xf = x.astype(jnp.float32)
    y = xf * lax.rsqrt(jnp.mean(xf * xf, axis=-1, keepdims=True) + EPS_RMS)
    return (y * g).astype(x.dtype)


def heads(t, dh):
    return t.reshape(t.shape[0], t.shape[1], -1, dh)


def token_shift(f, mu):
    prev = jnp.pad(f[:, :-1], ((0, 0), (1, 0), (0, 0)))
    nxt = jnp.pad(f[:, 1:], ((0, 0), (0, 1), (0, 0)))
    return f + mu[0] * (prev - f) + mu[1] * (nxt - f)


def rwkv7_prepare(f, mu, w0, w_up, a0, a_up, k_k, k_a):
    f = token_shift(f.astype(jnp.float32), mu)
    r, k, v, cw, ca = jnp.split(f, RW_SPLITS, axis=-1)
    B, T = f.shape[:2]
    cw = cw.reshape(B, T, 2, R_W)
    ca = ca.reshape(B, T, 2, R_A)
    w_log = -jax.nn.softplus(-(w0 + jnp.einsum('btdr,drc->btdc', jnp.tanh(cw), w_up))) - 0.5
    decay = jnp.exp(-jnp.exp(w_log))
    a = jax.nn.sigmoid(a0 + jnp.einsum('btdr,drc->btdc', ca, a_up))
    kk = heads(k * k_k, HEAD_A)
    kk = kk * lax.rsqrt(jnp.sum(kk * kk, axis=-1, keepdims=True) + 1e-12)
    k_dir = k[:, :, None] * (1.0 + (a - 1.0) * k_a)
    split_heads = lambda t: t.reshape(B, T, 2, H_A, HEAD_A)
    return (heads(r, HEAD_A), heads(v, HEAD_A), kk, split_heads(decay), split_heads(k_dir), split_heads(a))


def rwkv7_scan(S0, r, decay, k, v, kk, a, reverse):
    def step(S, inp):
        r_t, w_t, k_t, v_t, kk_t, a_t = inp
        S = (S * w_t[:, :, None, :]
             - jnp.einsum('bhvk,bhk->bhv', S, kk_t)[..., None] * (kk_t * a_t)[:, :, None, :]
             + v_t[..., None] * k_t[:, :, None, :])
        return S, jnp.einsum('bhvk,bhk->bhv', S, r_t)
    xs = tuple(jnp.moveaxis(t, 1, 0) for t in (r, decay, k, v, kk, a))
    S, o = lax.scan(step, S0, xs, reverse=reverse)
    return S, jnp.moveaxis(o, 0, 1)


def rwkv7_readout(o, r, k_dir, v, r_k, gn_w, gn_b, g):
    B, T = o.shape[:2]
    mean = jnp.mean(o, axis=-1, keepdims=True)
    var = jnp.mean(jnp.square(o - mean), axis=-1, keepdims=True)
    on = ((o - mean) * lax.rsqrt(var + GN_EPS)).reshape(B, T, D_A) * gn_w + gn_b
    bonus = jnp.einsum('bthn,btdhn,hn->bth', r, k_dir, r_k)[..., None] * v
    return ((on + bonus.reshape(B, T, D_A)) * jax.nn.silu(g.astype(jnp.float32))).astype(g.dtype)


def rwkv7_mixer(f, f_c, g, g_c, mu, w0, w_up, a0, a_up, k_k, k_a, r_k, gn_w, gn_b, ctx_out):
    r, v, kk, decay, k_dir, a = rwkv7_prepare(f, mu, w0, w_up, a0, a_up, k_k, k_a)
    r_c, v_c, kk_c, decay_c, k_dir_c, a_c = rwkv7_prepare(f_c, mu, w0, w_up, a0, a_up, k_k, k_a)
    B = r.shape[0]
    outs, outs_c = [], []
    for d, rev in enumerate(SCAN_DIRECTIONS):
        S0 = jnp.zeros((B, H_A, HEAD_A, HEAD_A), jnp.float32)
        S_c, o_c = rwkv7_scan(S0, r_c, decay_c[:, :, d], k_dir_c[:, :, d], v_c, kk_c, a_c[:, :, d], rev)
        _, o = rwkv7_scan(S_c, r, decay[:, :, d], k_dir[:, :, d], v, kk, a[:, :, d], rev)
        outs.append(o)
        outs_c.append(o_c)
    y = rwkv7_readout(outs[0] + outs[1], r, k_dir, v, r_k, gn_w, gn_b, g)
    y_c = rwkv7_readout(outs_c[0] + outs_c[1], r_c, k_dir_c, v_c, r_k, gn_w, gn_b, g_c) if ctx_out else None
    return y, y_c


def neighbourhood_attention(q, k, v, kc, vc, rpb):
    B, S, H, Dh = q.shape
    rows = S // GRID_W
    kh = min(NA_WIN_H, rows)
    n_loc = kh * NA_WIN_W
    scale = Dh ** -0.5
    qg = q.reshape(B, rows, GRID_W, H, Dh)
    kg = k.reshape(B, rows, GRID_W, H, Dh)
    vg = v.reshape(B, rows, GRID_W, H, Dh)
    cols = jnp.arange(GRID_W)
    col_start = jnp.clip(cols - NA_WIN_W // 2, 0, GRID_W - NA_WIN_W)
    col_idx = col_start[:, None] + jnp.arange(NA_WIN_W)[None, :]
    dx = col_idx - cols[:, None] + (NA_WIN_W - 1)

    def row_block(y):
        y0 = jnp.clip(y - kh // 2, 0, rows - kh)
        k_nb = lax.dynamic_slice_in_dim(kg, y0, kh, axis=1)[:, :, col_idx]
        v_nb = lax.dynamic_slice_in_dim(vg, y0, kh, axis=1)[:, :, col_idx]
        qy = lax.dynamic_index_in_dim(qg, y, axis=1, keepdims=False)
        dy = y0 + jnp.arange(kh) - y + (NA_WIN_H - 1)
        bias = rpb[:, dy[None, :, None], dx[:, None, :]].reshape(H, GRID_W, n_loc)
        s_loc = jnp.einsum('bwhd,brwkhd->bhwrk', qy, k_nb).reshape(B, H, GRID_W, n_loc)
        s_ctx = jnp.einsum('bwhd,bchd->bhwc', qy, kc)
        s = jnp.concatenate([s_loc.astype(jnp.float32) * scale + bias.astype(jnp.float32),
                             s_ctx.astype(jnp.float32) * scale], axis=-1)
        p = jax.nn.softmax(s, axis=-1).astype(v.dtype)
        p_loc = p[..., :n_loc].reshape(B, H, GRID_W, kh, NA_WIN_W)
        return (jnp.einsum('bhwrk,brwkhd->bwhd', p_loc, v_nb)
                + jnp.einsum('bhwc,bchd->bwhd', p[..., n_loc:], vc))

    o = lax.map(row_block, jnp.arange(rows))
    return jnp.moveaxis(o, 0, 1).reshape(B, S, H, Dh)


def ctx_attention(qc, kc, vc):
    s = jnp.einsum('bqhd,bkhd->bhqk', qc, kc).astype(jnp.float32) * (qc.shape[-1] ** -0.5)
    p = jax.nn.softmax(s, axis=-1).astype(vc.dtype)
    return jnp.einsum('bhqk,bkhd->bqhd', p, vc)


def even_mixer(h, hc, w_in, mu, w0, w_up, a0, a_up, k_k, k_a, r_k, gn_w, gn_b, rpb, ctx_out):
    fa, ga, qb, kb, vb, gb = jnp.split(h @ w_in, EV_SPLITS, axis=-1)
    fa_c, ga_c, qb_c, kb_c, vb_c, gb_c = jnp.split(hc @ w_in, EV_SPLITS, axis=-1)
    ya, ya_c = rwkv7_mixer(fa, fa_c, ga, ga_c, mu, w0, w_up, a0, a_up, k_k, k_a, r_k, gn_w, gn_b, ctx_out)
    kc, vc = heads(kb_c, HEAD_B), heads(vb_c, HEAD_B)
    ob = neighbourhood_attention(heads(qb, HEAD_B), heads(kb, HEAD_B), heads(vb, HEAD_B), kc, vc, rpb)
    yb = ob.reshape(h.shape[0], h.shape[1], D_B) * jax.nn.silu(gb)
    y = jnp.concatenate([ya, yb.astype(ya.dtype)], axis=-1)
    if not ctx_out:
        return y, None
    ob_c = ctx_attention(heads(qb_c, HEAD_B), kc, vc)
    yb_c = ob_c.reshape(hc.shape[0], hc.shape[1], D_B) * jax.nn.silu(gb_c)
    return y, jnp.concatenate([ya_c, yb_c.astype(ya_c.dtype)], axis=-1)


def centred_depthwise_conv(x, w, b):
    out = lax.conv_general_dilated(x, w[:, None, :].astype(x.dtype), window_strides=(1,),
                                   padding=[(CONV_LEFT, CONV_W - 1 - CONV_LEFT)],
                                   dimension_numbers=('NWC', 'WIO', 'NWC'),
                                   feature_group_count=x.shape[-1])
    return out + b


def rglru_coeffs(u, wa, ba, wx, bx, lam):
    B, T, _ = u.shape
    ub = u.reshape(B, T, RG_BLOCKS, RG_BS)
    gate_r = jax.nn.sigmoid(jnp.einsum('bthi,hij->bthj', ub, wa).reshape(B, T, D_C) + ba)
    gate_i = jax.nn.sigmoid(jnp.einsum('bthi,hij->bthj', ub, wx).reshape(B, T, D_C) + bx)
    log_a = -RGLRU_C * gate_r * jax.nn.softplus(-lam)
    a = jnp.exp(log_a)
    b = jnp.sqrt(-jnp.expm1(2.0 * log_a)) * (gate_i * u)
    return a, b


def linear_scan(a, b, h0, reverse):
    def combine(e1, e2):
        a1, b1 = e1
        a2, b2 = e2
        return a1 * a2, a2 * b1 + b2
    a_cum, h = lax.associative_scan(combine, (a, b), reverse=reverse, axis=1)
    return h + a_cum * h0[:, None]


def rglru_mixer(h, hc, w_in, conv_w, conv_b, ga_w, ga_b, gx_w, gx_b, lam, ctx_out):
    xr, g = jnp.split(h @ w_in, 2, axis=-1)
    xr_c, g_c = jnp.split(hc @ w_in, 2, axis=-1)
    u = centred_depthwise_conv(xr, conv_w, conv_b).astype(jnp.float32)
    u_c = centred_depthwise_conv(xr_c, conv_w, conv_b).astype(jnp.float32)
    ys, ys_c = [], []
    for d, rev in enumerate(SCAN_DIRECTIONS):
        a_c, b_c = rglru_coeffs(u_c, ga_w[d], ga_b[d], gx_w[d], gx_b[d], lam[d])
        h_c = linear_scan(a_c, b_c, jnp.zeros_like(u_c[:, 0]), rev)
        h0 = h_c[:, 0] if rev else h_c[:, -1]
        a, b = rglru_coeffs(u, ga_w[d], ga_b[d], gx_w[d], gx_b[d], lam[d])
        ys.append(linear_scan(a, b, h0, rev))
        ys_c.append(h_c)
    y = ((ys[0] + ys[1]) * jax.nn.silu(g.astype(jnp.float32))).astype(h.dtype)
    if not ctx_out:
        return y, None
    y_c = ((ys_c[0] + ys_c[1]) * jax.nn.silu(g_c.astype(jnp.float32))).astype(hc.dtype)
    return y, y_c


def setup_inputs(seed: int = 0) -> dict:
    key = jax.random.key(seed)
    ks = iter(jax.random.split(key, 40))
    nrm = lambda shape, s: s * jax.random.normal(next(ks), shape, jnp.float32)
    uni = lambda shape, lo, hi: jax.random.uniform(next(ks), shape, jnp.float32, lo, hi)
    D = D_MODEL
    lam_u = uni((N_ODD, 2, D_C), 0.9, 0.999) ** (1.0 / RGLRU_C)
    return {
        "x": nrm((BATCH, SEQ, D), 1.0),
        "c": nrm((BATCH, D), 1.0),
        "ctx": nrm((BATCH, CTX_LEN, D), 1.0),
        "c_ctx": nrm((D,), 1.0),
        "mod_w": nrm((DEPTH, D, 3 * D), 0.5 * D ** -0.5),
        "mod_b": nrm((DEPTH, 3 * D), 0.02),
        "norm_pre": 1.0 + nrm((DEPTH, D), 0.05),
        "norm_post": 1.0 + nrm((DEPTH, D), 0.05),
        "ev_w_in": nrm((N_EVEN, D, EV_IN), D ** -0.5),
        "ev_mu": uni((N_EVEN, 2, A_SHIFTED), 0.0, 0.5),
        "ev_w0": jnp.linspace(-6.0, -0.5, D_A)[None, None, :] + nrm((N_EVEN, 2, D_A), 0.3),
        "ev_w_up": nrm((N_EVEN, 2, R_W, D_A), 0.5 * R_W ** -0.5),
        "ev_a0": nrm((N_EVEN, 2, D_A), 0.3),
        "ev_a_up": nrm((N_EVEN, 2, R_A, D_A), 0.3 * R_A ** -0.5),
        "ev_k_k": 0.85 + nrm((N_EVEN, D_A), 0.05),
        "ev_k_a": 1.0 + nrm((N_EVEN, D_A), 0.05),
        "ev_r_k": nrm((N_EVEN, H_A, HEAD_A), 0.1),
        "ev_gn_w": 1.0 + nrm((N_EVEN, D_A), 0.05),
        "ev_gn_b": nrm((N_EVEN, D_A), 0.02),
        "ev_rpb": nrm((N_EVEN, H_B, 2 * NA_WIN_H - 1, 2 * NA_WIN_W - 1), 0.5),
        "ev_w_out": nrm((N_EVEN, D_A + D_B, D), (D_A + D_B) ** -0.5),
        "od_w_in": nrm((N_ODD, D, 2 * D_C), D ** -0.5),
        "od_conv_w": nrm((N_ODD, CONV_W, D_C), CONV_W ** -0.5),
        "od_conv_b": nrm((N_ODD, D_C), 0.02),
        "od_gate_a_w": nrm((N_ODD, 2, RG_BLOCKS, RG_BS, RG_BS), RG_BS ** -0.5),
        "od_gate_a_b": nrm((N_ODD, 2, D_C), 0.02),
        "od_gate_x_w": nrm((N_ODD, 2, RG_BLOCKS, RG_BS, RG_BS), RG_BS ** -0.5),
        "od_gate_x_b": nrm((N_ODD, 2, D_C), 0.02),
        "od_lambda": jnp.log(lam_u) - jnp.log1p(-lam_u),
        "od_w_out": nrm((N_ODD, D_C, D), D_C ** -0.5),
    }


def reference(x, c, ctx, c_ctx, mod_w, mod_b, norm_pre, norm_post, ev_w_in, ev_mu, ev_w0, ev_w_up,
              ev_a0, ev_a_up, ev_k_k, ev_k_a, ev_r_k, ev_gn_w, ev_gn_b, ev_rpb, ev_w_out, od_w_in,
              od_conv_w, od_conv_b, od_gate_a_w, od_gate_a_b, od_gate_x_w, od_gate_x_b, od_lambda,
              od_w_out):
    xc = ctx
    for layer in range(DEPTH):
        ctx_out = layer < DEPTH - 1
        i = layer // 2
        shift, scale, gate = jnp.split(jax.nn.silu(c) @ mod_w[layer] + mod_b[layer], 3, axis=-1)
        shift_c, scale_c, gate_c = jnp.split(jax.nn.silu(c_ctx) @ mod_w[layer] + mod_b[layer], 3, axis=-1)
        h = rmsnorm(x, norm_pre[layer]) * (1.0 + scale[:, None]) + shift[:, None]
        hc = rmsnorm(xc, norm_pre[layer]) * (1.0 + scale_c) + shift_c
        if layer % 2 == 0:
            y, yc = even_mixer(h, hc, ev_w_in[i], ev_mu[i], ev_w0[i], ev_w_up[i], ev_a0[i], ev_a_up[i],
                               ev_k_k[i], ev_k_a[i], ev_r_k[i], ev_gn_w[i], ev_gn_b[i], ev_rpb[i], ctx_out)
            w_out = ev_w_out[i]
        else:
            y, yc = rglru_mixer(h, hc, od_w_in[i], od_conv_w[i], od_conv_b[i], od_gate_a_w[i], od_gate_a_b[i],
                                od_gate_x_w[i], od_gate_x_b[i], od_lambda[i], ctx_out)
            w_out = od_w_out[i]
        x = x + gate[:, None] * rmsnorm(y @ w_out, norm_post[layer])
        if ctx_out:
            xc = xc + gate_c * rmsnorm(yc @ w_out, norm_post[layer])
    return x
```

```python
import contextlib
import numpy as np
import concourse.bass as bass
import concourse.mybir as mybir
from concourse.bass_utils import run_bass_kernel_spmd

F32 = mybir.dt.float32
BF16 = mybir.dt.bfloat16
ALU = mybir.AluOpType
AF = mybir.ActivationFunctionType
AX = mybir.AxisListType

ENGS = ("pe", "dve", "act", "pool", "sp")

D = 2048
CTX = 256
SEQ = 4096
T = CTX + SEQ
NTILE = T // 128
DEPTH = 4
A_SH = 3328
EV_IN = 8448
DA = 1024
DC = 2560
NQ = DC // 128
GRID_W = 64
EPS_RMS = 1e-6
GN_EPS = 64e-5


class Sched:
    def __init__(self, nc, stack, n_dma_sems=12):
        self.nc = nc
        self.stack = stack
        self.stream = {e: [] for e in ENGS}
        self.cnt = {e: 0 for e in ENGS}
        self.sem = {e: stack.enter_context(nc.semaphore("s_" + e)) for e in ENGS}
        self.seen = {e: {} for e in ENGS}
        self.lastw = {}
        self.readers = {}
        self.dsem = {}
        self.drr = {}
        for q in ("sp", "act", "pool"):
            self.dsem[q] = [[stack.enter_context(nc.semaphore("d_%s%d" % (q, i))), 0]
                            for i in range(n_dma_sems)]
            self.drr[q] = 0
        self.semkey = {}
        self.uid = 0
        self.ninst = 0

    def sb(self, st, name, shape, dt=F32):
        self.uid += 1
        return st.enter_context(self.nc.sbuf_tensor("%s_%d" % (name, self.uid), list(shape), dt))

    def ps(self, st, name, shape, dt=F32):
        self.uid += 1
        return st.enter_context(self.nc.psum_tensor("%s_%d" % (name, self.uid), list(shape), dt))

    def _deps(self, reads, writes):
        need = {}

        def add(sv):
            s, v = sv
            k = id(s)
            self.semkey[k] = s
            if need.get(k, 0) < v:
                need[k] = v
        for t in reads:
            if t in self.lastw:
                add(self.lastw[t])
        for t in writes:
            if t in self.lastw:
                add(self.lastw[t])
            for sv in self.readers.get(t, {}).values():
                add(sv)
        return need

    def _commit_waits(self, e, need, skip_own=False):
        waits = []
        seen = self.seen[e]
        for k, v in need.items():
            if skip_own and k == id(self.sem[e]):
                continue
            if seen.get(k, 0) >= v:
                continue
            seen[k] = v
            waits.append((self.semkey[k], v))
        return waits

    def _mark(self, reads, writes, sv):
        s, v = sv
        for t in writes:
            self.lastw[t] = sv
            self.readers[t] = {}
        for t in reads:
            if t in writes:
                continue
            r = self.readers.setdefault(t, {})
            k = id(s)
            if k not in r or r[k][1] < v:
                r[k] = sv

    def op(self, e, fn, reads=(), writes=(), signal=True):
        need = self._deps(reads, writes)
        waits = self._commit_waits(e, need, skip_own=(e == "pe"))
        if signal:
            self.cnt[e] += 1
            v = self.cnt[e]
            inc = (self.sem[e], 1)
        else:
            v = self.cnt[e] + 1
            inc = None
        self.stream[e].append((waits, fn, inc))
        self.ninst += 1 + len(waits)
        self._mark(reads, writes, (self.sem[e], v))

    def dma(self, q, out, in_, reads=(), writes=(), **kw):
        need = self._deps(reads, writes)
        k = self.drr[q]
        self.drr[q] = (k + 1) % len(self.dsem[q])
        ent = self.dsem[q][k]
        s = ent[0]
        self.semkey[id(s)] = s
        if ent[1] > 0 and need.get(id(s), 0) < 16 * ent[1]:
            need[id(s)] = 16 * ent[1]
        waits = self._commit_waits(q, need)
        ent[1] += 1
        v = 16 * ent[1]
        self.stream[q].append((waits, (lambda eng: eng.dma_start(out=out, in_=in_, **kw)), (s, 16)))
        self.ninst += 1 + len(waits)
        self._mark(reads, writes, (s, v))

    def barrier(self):
        need = {}
        for q in self.dsem:
            for s, c in self.dsem[q]:
                if c > 0:
                    self.semkey[id(s)] = s
                    need[id(s)] = 16 * c
        for e in ENGS:
            if self.cnt[e] > 0:
                self.semkey[id(self.sem[e])] = self.sem[e]
                need[id(self.sem[e])] = self.cnt[e]
        for e in ENGS:
            waits = self._commit_waits(e, dict(need))
            if waits:
                self.stream[e].append((waits, None, None))
        self.lastw = {}
        self.readers = {}

    def emit(self):
        nc = self.nc
        st = self.stream

        def run(name, eng):
            for waits, fn, inc in st[name]:
                for s, v in waits:
                    eng.wait_ge(s, v)
                if fn is None:
                    continue
                ins = fn(eng)
                if inc is not None:
                    ins.then_inc(inc[0], inc[1])
        with nc.Block() as block:
            @block.tensor
            def _(e):
                run("pe", e)

            @block.vector
            def _(e):
                run("dve", e)

            @block.scalar
            def _(e):
                run("act", e)

            @block.gpsimd
            def _(e):
                run("pool", e)

            @block.sync
            def _(e):
                run("sp", e)

    def mm(self, out, lhsT, rhs, start, stop, reads, writes, signal=None, tp=None):
        if signal is None:
            signal = bool(stop)
        kw = {}
        if tp is not None:
            kw["tile_position"] = tp
        self.op("pe", lambda e: e.matmul(out, lhsT=lhsT, rhs=rhs, start=start, stop=stop, **kw),
                reads, writes, signal=signal)

    def tr(self, out, in_, ident, reads, writes, signal=True, tp=None):
        kw = {}
        if tp is not None:
            kw["tile_position"] = tp
        self.op("pe", lambda e: e.transpose(out, in_, ident, **kw), reads, writes, signal=signal)

    def tt(self, eng, out, in0, in1, op, reads, writes):
        self.op(eng, lambda e: e.tensor_tensor(out=out, in0=in0, in1=in1, op=op), reads, writes)

    def ts(self, eng, out, in0, s1, s2, op0, op1, reads, writes):
        if op1 is None:
            self.op(eng, lambda e: e.tensor_scalar(out=out, in0=in0, scalar1=s1, scalar2=None, op0=op0), reads, writes)
        else:
            self.op(eng, lambda e: e.tensor_scalar(out=out, in0=in0, scalar1=s1, scalar2=s2, op0=op0, op1=op1), reads, writes)

    def stt(self, eng, out, in0, scalar, in1, op0, op1, reads, writes):
        eng = "dve"
        self.op(eng, lambda e: e.scalar_tensor_tensor(out=out, in0=in0, scalar=scalar, in1=in1, op0=op0, op1=op1),
                reads, writes)

    def cp(self, eng, out, in_, reads, writes):
        if eng == "act":
            self.op(eng, lambda e: e.copy(out=out, in_=in_), reads, writes)
        else:
            self.op(eng, lambda e: e.tensor_copy(out=out, in_=in_), reads, writes)

    def act(self, out, in_, func, reads, writes, bias=None, scale=None, accum_out=None):
        kw = {}
        if bias is not None:
            kw["bias"] = bias
        if scale is not None:
            kw["scale"] = scale
        if accum_out is not None:
            kw["accum_out"] = accum_out
        self.op("act", lambda e: e.activation(out=out, in_=in_, func=func, **kw), reads, writes)

    def memset(self, eng, ap, val, writes):
        self.op(eng, lambda e: e.memset(ap, val), (), writes)


def _blocks():
    return [(0, 9), (9, 9), (18, 8), (26, 8)]


class Ctx:
    pass


def build_program(layers, final_out=True, debug_taps=(), stop_after=None):
    nc = bass.Bass("TRN2", target_bir_lowering=False)
    K = Ctx()
    K.nc = nc

    def din(name, shape, dt=F32):
        return nc.dram_tensor(name, list(shape), dt, kind="ExternalInput").ap()

    def dscr(name, shape, dt=F32):
        return nc.dram_tensor(name, list(shape), dt, kind="Internal").ap()

    I = {}
    I["xin"] = din("xin", [T, D])
    I["cc"] = din("cc", [128, 16, 2])
    I["mod_w"] = din("mod_w", [DEPTH, D, 3 * D])
    I["mod_b"] = din("mod_b", [DEPTH, 3 * D])
    I["norm_pre"] = din("norm_pre", [DEPTH, D])
    I["norm_post"] = din("norm_post", [DEPTH, D])
    I["ev_w_in"] = din("ev_w_in", [2, D, EV_IN])
    I["ev_w_out"] = din("ev_w_out", [2, D, D])
    I["od_w_in"] = din("od_w_in", [2, D, 2 * DC])
    I["od_w_out"] = din("od_w_out", [2, DC, D])
    I["od_gw"] = din("od_gw", [2, 2, 2, 16, 160, 160])
    I["od_pc"] = din("od_pc", [2, 128, 11, NQ])
    I["ev_pc"] = din("ev_pc", [2, 128, EVPC_N])
    I["ev_wup"] = din("ev_wup", [2, 2, 128, DA])
    I["ev_tb"] = din("ev_tb", [2, 16, 128, 14, 64])
    I["c_ident"] = din("c_ident", [128, 128])
    I["c_identb"] = din("c_identb", [128, 128], BF16)
    I["c_masks"] = din("c_masks", [128, 5, 64])
    I["c_bones"] = din("c_bones", [128, 128])
    I["c_scanm"] = din("c_scanm", [128, 2, 256])
    out = nc.dram_tensor("out", [SEQ, D], F32, kind="ExternalOutput").ap()
    K.I = I
    K.out = out
    K.X = dscr("X", [T, D])
    K.modrow = dscr("modrow", [2, 3 * D])
    K.FT = dscr("FT", [5376, T])
    K.QK = dscr("QK", [2048, T], BF16)
    K.VB = dscr("VB", [T, 1024], BF16)
    K.YT = dscr("YT", [DC, T], BF16)
    K.HF = dscr("HF", [DC, T])
    K.OT = dscr("OT", [2, 2, DA, T])
    K.Wb_in = {}
    K.Wb_out = {}
    for l in layers:
        i = l // 2
        if l % 2 == 0:
            K.Wb_in[l] = dscr("wbin%d" % l, [EV_IN // 256, 128, 16, 256], BF16)
            K.Wb_out[l] = dscr("wbout%d" % l, [D, D], BF16)
        else:
            K.Wb_in[l] = dscr("wbin%d" % l, [2 * DC // 256, 128, 16, 256], BF16)
            K.Wb_out[l] = dscr("wbout%d" % l, [DC, D], BF16)
    taps = {}
    for name, shape in debug_taps:
        taps[name] = nc.dram_tensor("tap_" + name, list(shape), F32, kind="ExternalOutput").ap()
    K.taps = taps

    with contextlib.ExitStack() as st:
        S = Sched(nc, st)
        K.S = S
        for l in layers:
            i = l // 2
            win = I["ev_w_in"][i] if l % 2 == 0 else I["od_w_in"][i]
            wout = I["ev_w_out"][i] if l % 2 == 0 else I["od_w_out"][i]
            winv = win.rearrange("(k p) c -> p k c", p=128)
            for g in range(K.Wb_in[l].shape[0]):
                S.dma("pool", K.Wb_in[l][g], winv[:, :, g * 256:(g + 1) * 256], writes=[("wbin", l)])
            nrow = D if l % 2 == 0 else DC
            for r in range(0, nrow, 256):
                S.dma("pool", K.Wb_out[l][r:r + 256, :], wout[r:r + 256, :], writes=[("wbout", l)])
        for r in range(0, T, 544):
            S.dma("sp", K.X[r:r + 544, :], I["xin"][r:r + 544, :], writes=[("X", "all")])
        S.barrier()
        stopped = stop_after == "init"
        for li, l in enumerate(layers):
            if stopped:
                break
            last = (li == len(layers) - 1) and final_out
            import os
            skip = os.environ.get("SKIP_PH", "").split(",")
            if "mod" not in skip:
                phase_mod(K, l)
                S.barrier()
            if stop_after == "mod":
                stopped = True
                break
            if "proj" not in skip:
                phase_proj(K, l)
                S.barrier()
            if stop_after == "proj":
                stopped = True
                break
            if l % 2 == 0:
                phase_rwkv(K, l)
                S.barrier()
                phase_na(K, l)
                S.barrier()
            else:
                phase_rglru(K, l)
                S.barrier()
            if stop_after == "mixer":
                stopped = True
                break
            phase_out(K, l, last)
            S.barrier()
        if stopped:
            final_out = False
        if not final_out:
            for r in range(0, SEQ, 512):
                S.dma("sp", out[r:r + 512, :], K.X[CTX + r:CTX + r + 512, :])
        S.barrier()
        print("instructions (incl waits):", S.ninst, {e: len(S.stream[e]) for e in ENGS})
        S.emit()
    return nc


def phase_mod(K, l):
    S, I = K.S, K.I
    with contextlib.ExitStack() as st:
        cc = S.sb(st, "cc", [128, 16, 2])
        sc = S.sb(st, "sc", [128, 16, 2])
        S.dma("sp", cc[:], I["cc"], writes=["cc"])
        S.act(sc[:], cc[:], AF.Sigmoid, ["cc"], ["sc"])
        S.tt("dve", sc[:], sc[:], cc[:], ALU.mult, ["sc", "cc"], ["sc"])
        mw = [S.sb(st, "mw%d" % i, [128, 16, 512]) for i in range(2)]
        pm = [S.ps(st, "pm%d" % i, [2, 512]) for i in range(2)]
        msb = S.sb(st, "msb", [2, 3 * D])
        mb = S.sb(st, "mb", [2, 3 * D])
        S.dma("act", mb[:], I["mod_b"][l:l + 1, :].partition_broadcast(2), writes=["mb"])
        src = I["mod_w"][l].rearrange("(k p) c -> p k c", p=128)
        for g in range(12):
            b = g % 2
            for h in range(2):
                S.dma("sp" if h == 0 else "act", mw[b][:, 8 * h:8 * h + 8, :], src[:, 8 * h:8 * h + 8, g * 512:(g + 1) * 512],
                      writes=[("mw", b, h)])
            for k in range(16):
                S.mm(pm[b][:], sc[:, k, :], mw[b][:, k, :], k == 0, k == 15,
                     ["sc", ("mw", b, k // 8)], [("pm", b)])
            S.tt("dve", msb[:, g * 512:(g + 1) * 512], pm[b][:], mb[:, g * 512:(g + 1) * 512], ALU.add,
                 [("pm", b), "mb"], ["msb"])
        S.dma("sp", K.modrow, msb[:], reads=["msb"], writes=["modrow"])


def load_bcast_rows(K, st, l, which):
    S, I = K.S, K.I
    res = {}
    tmp = S.sb(st, "bt_n", [128, D])
    if which == "pre":
        S.dma("act", tmp[:], I["norm_pre"][l:l + 1, :].partition_broadcast(128), writes=["bt_n"])
        for s, nm in ((0, "x"), (1, "c")):
            G = S.sb(st, "G" + nm, [128, D])
            Sh = S.sb(st, "S" + nm, [128, D])
            S.dma("sp", G[:], K.modrow[s:s + 1, D:2 * D].partition_broadcast(128), reads=["modrow"], writes=["G" + nm])
            S.dma("act", Sh[:], K.modrow[s:s + 1, 0:D].partition_broadcast(128), reads=["modrow"], writes=["S" + nm])
            S.stt("dve", G[:], G[:], 1.0, tmp[:], ALU.add, ALU.mult, ["G" + nm, "bt_n"], ["G" + nm])
            res[nm] = (G, Sh)
    else:
        S.dma("act", tmp[:], I["norm_post"][l:l + 1, :].partition_broadcast(128), writes=["bt_n"])
        for s, nm in ((0, "x"), (1, "c")):
            G = S.sb(st, "GP" + nm, [128, D])
            S.dma("sp", G[:], K.modrow[s:s + 1, 2 * D:3 * D].partition_broadcast(128), reads=["modrow"], writes=["GP" + nm])
            S.tt("dve", G[:], G[:], tmp[:], ALU.mult, ["GP" + nm, "bt_n"], ["GP" + nm])
            res[nm] = G
    return res


def proj_spec(l):
    if l % 2 == 0:
        return [(0, 4352, "fm32", 0), (4352, 6400, "fmbf", 0), (6400, 7424, "tm", 0), (7424, 8448, "fm32", 4352)]
    return [(0, 2 * DC, "fm32", 0)]


def phase_proj(K, l):
    S, I = K.S, K.I
    spec = proj_spec(l)
    Wb = K.Wb_in[l]
    with contextlib.ExitStack() as st:
        rows = load_bcast_rows(K, st, l, "pre")
        identb = S.sb(st, "identb", [128, 128], BF16)
        S.dma("sp", identb[:], I["c_identb"], writes=["identb"])
        xt = [S.sb(st, "xt%d" % i, [128, D]) for i in range(2)]
        junk = S.sb(st, "junk", [128, D], BF16)
        ss = [S.sb(st, "ss%d" % i, [128, 1]) for i in range(2)]
        hn = S.sb(st, "hn", [128, D])
        hb = [S.sb(st, "hb%d" % i, [128, D], BF16) for i in range(2)]
        hT2 = [S.sb(st, "hT%d" % i, [128, 16, 9 * 128], BF16) for i in range(2)]
        ptr = [S.ps(st, "ptr%d" % i, [128, 1024], BF16) for i in range(2)]
        pp = [S.ps(st, "pp%d" % i, [128, 512]) for i in range(4)]
        wt = [S.sb(st, "wt%d" % i, [128, 16, 256], BF16) for i in range(3)]
        stg = [S.sb(st, "stg%d" % i, [128, 9 * 128]) for i in range(3)]
        stgb = [S.sb(st, "stgb%d" % i, [128, 9 * 128], BF16) for i in range(2)]
        stgt = [S.sb(st, "stgt%d" % i, [128, 256], BF16) for i in range(3)]
        cnt = {"x": 0, "pp": 0, "w": 0, "stg": 0, "stgb": 0, "stgt": 0, "ev": 0}

        def a_tile(tile_i, hbuf, ti):
            hT = hT2[hbuf]
            b = cnt["x"] % 2
            cnt["x"] += 1
            G, Sh = rows["c"] if tile_i < 2 else rows["x"]
            gn = "Gc" if tile_i < 2 else "Gx"
            sn = "Sc" if tile_i < 2 else "Sx"
            S.dma("sp", xt[b][:], K.X[tile_i * 128:(tile_i + 1) * 128, :], reads=[("X", tile_i), ("X", "all")],
                  writes=[("xt", b)])
            S.act(junk[:], xt[b][:], AF.Square, [("xt", b)], ["junk", ("ss", b)], accum_out=ss[b][:])
            S.act(ss[b][:], ss[b][:], AF.Ln, [("ss", b)], [("ss", b)], scale=1.0 / D, bias=EPS_RMS)
            S.act(ss[b][:], ss[b][:], AF.Exp, [("ss", b)], [("ss", b)], scale=-0.5)
            S.stt("dve", hn[:], xt[b][:], ss[b][:, 0:1], G[:], ALU.mult, ALU.mult, [("xt", b), ("ss", b), gn], ["hn"])
            S.tt("pool", hb[b][:], hn[:], Sh[:], ALU.add, ["hn", sn], [("hb", b)])
            for half in range(2):
                for k in range(8):
                    kk = half * 8 + k
                    S.tr(ptr[half][:, k * 128:(k + 1) * 128], hb[b][:, kk * 128:(kk + 1) * 128], identb[:],
                         [("hb", b), "identb"], [("ptr", half)], signal=(k == 7))
                S.cp("act" if half == 0 else "dve", hT[:, half * 8:half * 8 + 8, ti * 128:(ti + 1) * 128],
                     ptr[half][:].rearrange("p (k t) -> p k t", k=8), [("ptr", half)], [("hT", hbuf, ti)])

        def b_group(t0, nt, hbuf, c0, kind, drow, g0):
            hT = hT2[hbuf]
            ntok = nt * 128
            wb = cnt["w"] % 3
            cnt["w"] += 1
            S.dma("sp" if cnt["w"] % 2 == 0 else "act", wt[wb][:], Wb[g0 // 256], reads=[("wbin", l)], writes=[("wt", wb)])
            if kind in ("fm32", "fmbf"):
                for sub in range(2):
                    if kind == "fm32":
                        sg = stg[cnt["stg"] % 3]
                        sgn = ("stg", cnt["stg"] % 3)
                        cnt["stg"] += 1
                    else:
                        sg = stgb[cnt["stgb"] % 2]
                        sgn = ("stgb", cnt["stgb"] % 2)
                        cnt["stgb"] += 1
                    for ts0 in range(0, ntok, 512):
                        tsn = min(512, ntok - ts0)
                        pb = cnt["pp"] % 4
                        cnt["pp"] += 1
                        rtok = [("hT", hbuf, ti) for ti in range(ts0 // 128, (ts0 + tsn) // 128)]
                        for k in range(16):
                            S.mm(pp[pb][:, 0:tsn], wt[wb][:, k, sub * 128:(sub + 1) * 128], hT[:, k, ts0:ts0 + tsn],
                                 k == 0, k == 15, [("wt", wb)] + rtok, [("pp", pb)])
                        cnt["ev"] += 1
                        S.cp("act" if cnt["ev"] % 2 == 0 else "dve", sg[:, ts0:ts0 + tsn], pp[pb][:, 0:tsn], [("pp", pb)], [sgn])
                    r0 = drow + (g0 - c0) + sub * 128
                    dst = K.FT if kind == "fm32" else K.QK
                    S.dma("pool" if kind == "fm32" else "sp", dst[r0:r0 + 128, t0 * 128:t0 * 128 + ntok], sg[:, 0:ntok], reads=[sgn],
                          writes=[("FT" if kind == "fm32" else "QK", r0 // 128)])
            else:
                for ti in range(nt):
                    pb = cnt["pp"] % 4
                    cnt["pp"] += 1
                    for k in range(16):
                        S.mm(pp[pb][:, 0:256], hT[:, k, ti * 128:(ti + 1) * 128], wt[wb][:, k, :],
                             k == 0, k == 15, [("wt", wb), ("hT", hbuf, ti)], [("pp", pb)])
                    sb_ = cnt["stgt"] % 3
                    cnt["stgt"] += 1
                    cnt["ev"] += 1
                    S.cp("act" if cnt["ev"] % 2 == 0 else "dve", stgt[sb_][:], pp[pb][:, 0:256], [("pp", pb)], [("stgt", sb_)])
                    cc0 = g0 - c0
                    S.dma("pool", K.VB[(t0 + ti) * 128:(t0 + ti + 1) * 128, cc0:cc0 + 256], stgt[sb_][:],
                          reads=[("stgt", sb_)], writes=[("VB", t0 + ti)])

        blocks = _blocks()
        groups = [(c0, kind, drow, g0) for (c0, c1, kind, drow) in spec for g0 in range(c0, c1, 256)]
        t0, nt = blocks[0]
        for ti in range(nt):
            a_tile(t0 + ti, 0, ti)
        for bi, (t0, nt) in enumerate(blocks):
            hbuf = bi % 2
            nxt = blocks[bi + 1] if bi + 1 < len(blocks) else None
            pend = list(range(nxt[1])) if nxt else []
            every = max(1, (len(groups) - 2) // max(1, len(pend))) if pend else 0
            for gi, (c0, kind, drow, g0) in enumerate(groups):
                b_group(t0, nt, hbuf, c0, kind, drow, g0)
                if pend and (gi + 1) % every == 0:
                    ti = pend.pop(0)
                    a_tile(nxt[0] + ti, 1 - hbuf, ti)
            while pend:
                ti = pend.pop(0)
                a_tile(nxt[0] + ti, 1 - hbuf, ti)


def phase_out(K, l, last):
    S, I = K.S, K.I
    nf = D if l % 2 == 0 else DC
    nk = nf // 128
    Wb = K.Wb_out[l].rearrange("(k p) c -> p k c", p=128)
    YTv = K.YT[0:nf, :].rearrange("(k p) t -> p k t", p=128)
    with contextlib.ExitStack() as st:
        rows = load_bcast_rows(K, st, l, "post")
        wo = S.sb(st, "wo", [128, nk, D], BF16)
        for k in range(0, nk, 4):
            S.dma("sp" if (k // 4) % 2 == 0 else "act", wo[:, k:k + 4, :], Wb[:, k:k + 4, :], reads=[("wbout", l)], writes=["wo"])
        yt = [S.sb(st, "yt%d" % i, [128, nk, 128], BF16) for i in range(2)]
        xt = [S.sb(st, "xo%d" % i, [128, D]) for i in range(2)]
        pz = [S.ps(st, "pz%d" % i, [128, D]) for i in range(2)]
        junk = S.sb(st, "junko", [128, D], BF16)
        ssq = [S.sb(st, "sso%d" % i, [128, 1]) for i in range(2)]
        zz = S.sb(st, "zz", [128, D])
        xn = [S.sb(st, "xn%d" % i, [128, D]) for i in range(2)]
        tiles = list(range(NTILE))
        if last:
            tiles = list(range(2, NTILE))
        for n, ti in enumerate(tiles):
            b = n % 2
            GP = rows["c"] if ti < 2 else rows["x"]
            gpn = "GPc" if ti < 2 else "GPx"
            S.dma("sp", yt[b][:], YTv[:, :, ti * 128:(ti + 1) * 128], reads=[("YT", ti), ("YT", "all")], writes=[("yt", b)])
            S.dma("act", xt[b][:], K.X[ti * 128:(ti + 1) * 128, :], reads=[("X", ti), ("X", "all")], writes=[("xo", b)])
            for nn in range(4):
                for k in range(nk):
                    S.mm(pz[b][:, nn * 512:(nn + 1) * 512], yt[b][:, k, :], wo[:, k, nn * 512:(nn + 1) * 512],
                         k == 0, k == nk - 1, [("yt", b), "wo"], [("pz", b, nn)])
            pzr = [("pz", b, nn) for nn in range(4)]
            S.act(junk[:], pz[b][:], AF.Square, pzr, ["junko", ("sso", b)], accum_out=ssq[b][:])
            S.act(ssq[b][:], ssq[b][:], AF.Ln, [("sso", b)], [("sso", b)], scale=1.0 / D, bias=EPS_RMS)
            S.act(ssq[b][:], ssq[b][:], AF.Exp, [("sso", b)], [("sso", b)], scale=-0.5)
            S.stt("dve", zz[:], pz[b][:], ssq[b][:, 0:1], GP[:], ALU.mult, ALU.mult, pzr + [("sso", b), gpn], ["zz"])
            S.tt("pool", xn[b][:], zz[:], xt[b][:], ALU.add, ["zz", ("xo", b)], [("xn", b)])
            if last:
                S.dma("sp", K.out[(ti - 2) * 128:(ti - 1) * 128, :], xn[b][:], reads=[("xn", b)])
            else:
                S.dma("pool", K.X[ti * 128:(ti + 1) * 128, :], xn[b][:], reads=[("xn", b)], writes=[("X", ti)])


RG_TB = 128


def rg_rects():
    res = []
    for h in range(16):
        lo, hi = 160 * h, 160 * h + 160
        for q in range(lo // 128, (hi - 1) // 128 + 1):
            r0, r1 = max(lo, 128 * q), min(hi, 128 * q + 128)
            for q2 in range(lo // 128, (hi - 1) // 128 + 1):
                c0, c1 = max(lo, 128 * q2), min(hi, 128 * q2 + 128)
                res.append((h, q, r0, r1, q2, c0, c1))
    return res


def phase_rglru(K, l):
    S, I = K.S, K.I
    i = l // 2
    NB = T // RG_TB
    NCTX = CTX // RG_TB
    TB = RG_TB
    with contextlib.ExitStack() as st:
        pc = S.sb(st, "pc", [128, 11, NQ])
        S.dma("sp", pc[:], I["od_pc"][i], writes=["pc"])
        spl = S.sb(st, "spl", [128, 2, NQ])
        S.act(spl[:], pc[:, 9:11, :], AF.Exp, ["pc"], ["spl"], scale=-1.0)
        S.act(spl[:], spl[:], AF.Ln, ["spl"], ["spl"], bias=1.0)
        S.ts("dve", spl[:], spl[:], -8.0, None, ALU.mult, None, ["spl"], ["spl"])
        wz = {}
        for g in range(2):
            wz[g] = S.sb(st, "wz%d" % g, [128, NQ, 384], BF16)
        HALO = 4
        xr = [S.sb(st, "xr%d" % b, [128, NQ, TB + HALO]) for b in range(2)]
        gg = S.sb(st, "gg", [128, NQ, TB])
        hf = S.sb(st, "hf", [128, NQ, TB])
        u_ = [S.sb(st, "u%d" % q, [128, NQ, TB]) for q in range(2)]
        tmp_ = [S.sb(st, "rtmp%d" % q, [128, NQ, TB]) for q in range(2)]
        ub_ = [S.sb(st, "ub%d" % q, [128, NQ, TB], BF16) for q in range(2)]
        ga_1 = S.sb(st, "ga", [128, NQ, TB])
        ga_ = [ga_1, ga_1]
        gx_ = [S.sb(st, "gx%d" % q, [128, NQ, TB]) for q in range(2)]
        aa_ = [S.sb(st, "aa%d" % q, [128, NQ, TB]) for q in range(2)]
        bb_ = [S.sb(st, "bb%d" % q, [128, NQ, TB]) for q in range(2)]
        hh = S.sb(st, "hh", [128, NQ, TB])
        yb = S.sb(st, "yb", [128, NQ, TB], BF16)
        state = S.sb(st, "state", [128, NQ])
        pg = [S.ps(st, "pg%d" % b, [128, 2, 256]) for b in range(4)]
        FTx = K.FT[0:DC, :].rearrange("(q p) t -> p q t", p=128)
        FTg = K.FT[DC:2 * DC, :].rearrange("(q p) t -> p q t", p=128)
        HFv = K.HF.rearrange("(q p) t -> p q t", p=128)
        YTv = K.YT.rearrange("(q p) t -> p q t", p=128)

        def bc(col):
            return col.unsqueeze(2).to_broadcast([128, NQ, TB])
        n_pg = 0
        nblk = 0
        for d in range(2):
            S.barrier()
            for g in range(2):
                S.memset("pool", wz[g][:], 0.0, [("wz", g)])
                for n, (h, q, r0, r1, q2, c0, c1) in enumerate(rg_rects()):
                    S.dma("pool",
                          wz[g][r0 - 128 * q:r1 - 128 * q, q, (q2 - q + 1) * 128 + (c0 - 128 * q2):(q2 - q + 1) * 128 + (c1 - 128 * q2)],
                          I["od_gw"][i, g, d, h, r0 - 160 * h:r1 - 160 * h, c0 - 160 * h:c1 - 160 * h],
                          reads=[], writes=[("wz", g)])
            S.memset("dve", state[:], 0.0, ["state"])
            if d == 0:
                order = list(range(NB))
            else:
                order = list(range(NCTX - 1, -1, -1)) + list(range(NB - 1, NCTX - 1, -1))
            import os
            blks = order[:int(os.environ.get("RG_MAXB", "1000"))]

            def front(bi, b):
                nonlocal n_pg
                t0 = bi * TB
                u, ub, ga, gx, aa, bb = u_[b], ub_[b], ga_[b], gx_[b], aa_[b], bb_[b]
                seg0, seg1 = (0, CTX) if t0 < CTX else (CTX, T)
                lo = max(seg0, t0 - 2)
                hi = min(seg1, t0 + TB + 1)
                if lo > t0 - 2:
                    S.memset("pool", xr[b][:, :, 0:2], 0.0, [("xr", b)])
                if hi < t0 + TB + 1:
                    S.memset("pool", xr[b][:, :, TB + 2:TB + 3], 0.0, [("xr", b)])
                for qh in range(2):
                    qs = slice(qh * 10, qh * 10 + 10)
                    S.dma("sp", xr[b][:, qs, lo - (t0 - 2):hi - (t0 - 2)], FTx[:, qs, lo:hi],
                          reads=[("FT", "all")], writes=[("xr", b)])
                S.tt("dve", u[:], xr[b][:, :, 0:TB], bc(pc[:, 0, :]), ALU.mult, [("xr", b), "pc"], [("u", b)])
                for j in range(1, 4):
                    tmp = tmp_[j % 2]
                    S.tt("pool", tmp[:], xr[b][:, :, j:j + TB], bc(pc[:, j, :]), ALU.mult, [("xr", b), "pc"], [("rtmp", j % 2)])
                    S.tt("dve", u[:], u[:], tmp[:], ALU.add, [("u", b), ("rtmp", j % 2)], [("u", b)])
                S.tt("dve", u[:], u[:], bc(pc[:, 4, :]), ALU.add, [("u", b), "pc"], [("u", b)])
                S.cp("act", ub[:], u[:], [("u", b)], [("ub", b)])
                for q2 in range(NQ):
                    pb = n_pg % 4
                    n_pg += 1
                    qs_ = [q for q in (q2 - 1, q2, q2 + 1) if 0 <= q < NQ]
                    for g in range(2):
                        for n, q in enumerate(qs_):
                            slot = q2 - q + 1
                            S.mm(pg[pb][:, g, 0:TB], wz[g][:, q, slot * 128:(slot + 1) * 128], ub[:, q, :],
                                 n == 0, n == len(qs_) - 1, [("wz", g), ("ub", b)], [("pg", pb)])
                    S.act(ga[:, q2, :], pg[pb][:, 0, 0:TB], AF.Sigmoid, [("pg", pb), "pc"], [("ga", q2)], bias=pc[:, 5 + d, q2:q2 + 1])
                    S.act(gx[:, q2, :], pg[pb][:, 1, 0:TB], AF.Sigmoid, [("pg", pb), "pc"], [("gx", b, q2)], bias=pc[:, 7 + d, q2:q2 + 1])

            def front_b(bi, b):
                u, ub, ga, gx, aa, bb = u_[b], ub_[b], ga_[b], gx_[b], aa_[b], bb_[b]
                gaall = [("ga", q) for q in range(NQ)]
                gxall = [("gx", b, q) for q in range(NQ)]
                S.tt("pool", aa[:], ga[:], bc(spl[:, d, :]), ALU.mult, gaall + ["spl"], [("aa", b)])
                S.act(aa[:], aa[:], AF.Exp, [("aa", b)], [("aa", b)])
                S.tt("pool", bb[:], aa[:], aa[:], ALU.mult, [("aa", b)], [("bb", b)])
                S.act(bb[:], bb[:], AF.Ln, [("bb", b)], [("bb", b)], scale=-1.0, bias=1.0)
                S.act(bb[:], bb[:], AF.Exp, [("bb", b)], [("bb", b)], scale=0.5)
                S.tt("dve", gx[:], gx[:], u[:], ALU.mult, gxall + [("u", b)], gxall)
                S.tt("pool", bb[:], bb[:], gx[:], ALU.mult, [("bb", b)] + gxall, [("bb", b)])

            def back(bi, b):
                t0 = bi * TB
                aa, bb = aa_[b], bb_[b]
                if d == 1:
                    for qh in range(2):
                        qs = slice(qh * 10, qh * 10 + 10)
                        S.dma("sp", gg[:, qs, :], FTg[:, qs, t0:t0 + TB], reads=[("FT", "all")], writes=["gg"])
                        S.dma("sp", hf[:, qs, :], HFv[:, qs, t0:t0 + TB], reads=[("HF", bi)], writes=["hf"])
                for q in range(NQ):
                    if d == 0:
                        S.op("dve", (lambda e, q=q, aa=aa, bb=bb: e.tensor_tensor_scan(out=hh[:, q, :], data0=aa[:, q, :], data1=bb[:, q, :],
                                                                           initial=state[:, q:q + 1], op0=ALU.mult, op1=ALU.add)),
                             [("aa", b), ("bb", b), "state"], [("hh", q)])
                    else:
                        S.op("dve", (lambda e, q=q, aa=aa, bb=bb: e.tensor_tensor_scan(out=hh[:, q, ::-1], data0=aa[:, q, ::-1], data1=bb[:, q, ::-1],
                                                                           initial=state[:, q:q + 1], op0=ALU.mult, op1=ALU.add)),
                             [("aa", b), ("bb", b), "state"], [("hh", q)])
                hhall = [("hh", q) for q in range(NQ)]
                if d == 0:
                    S.cp("pool", state[:], hh[:, :, TB - 1], hhall + ["state"], ["state"])
                    for qh in range(2):
                        qs = slice(qh * 10, qh * 10 + 10)
                        S.dma("sp", HFv[:, qs, t0:t0 + TB], hh[:, qs, :], reads=hhall, writes=[("HF", bi)])
                else:
                    S.cp("pool", state[:], hh[:, :, 0], hhall + ["state"], ["state"])
                    S.tt("pool", hh[:], hh[:], hf[:], ALU.add, hhall + ["hf"], hhall)
                    S.act(hf[:], gg[:], AF.Sigmoid, ["gg"], ["hf"])
                    S.tt("dve", gg[:], gg[:], hf[:], ALU.mult, ["gg", "hf"], ["gg"])
                    S.tt("dve", yb[:], hh[:], gg[:], ALU.mult, hhall + ["gg"], ["yb"])
                    for qh in range(2):
                        qs = slice(qh * 10, qh * 10 + 10)
                        S.dma("sp", YTv[:, qs, t0:t0 + TB], yb[:, qs, :], reads=["yb"], writes=[("YT", "all")])

            for n, bi in enumerate(blks):
                front(bi, n % 2)
                if n >= 1:
                    back(blks[n - 1], (n - 1) % 2)
                front_b(bi, n % 2)
            if blks:
                back(blks[-1], (len(blks) - 1) % 2)


EVPC_N = 124


def phase_rwkv(K, l):
    S, I = K.S, K.I
    i = l // 2
    import os
    NB = 256
    NCH = NB // 64
    NP = 8
    maxblk = int(os.environ.get("RW_MAXB", "1000"))
    FT3 = K.FT[0:3072, :].rearrange("(part q p) t -> q p part t", part=3, q=8, p=128)
    FTc = K.FT[3072:3328, :].rearrange("(g p) t -> p g t", p=128)
    with contextlib.ExitStack() as st:
        pc = S.sb(st, "evpc", [128, EVPC_N])
        S.dma("sp", pc[:], I["ev_pc"][i], writes=["evpc"])
        mc = S.sb(st, "mc", [128, 26])
        for part in range(3):
            S.tt("dve", mc[:, part * 8:part * 8 + 8], pc[:, part * 16:part * 16 + 8], pc[:, part * 16 + 8:part * 16 + 16], ALU.add,
                 ["evpc"], ["mc"])
        S.tt("dve", mc[:, 24:25], pc[:, 48:49], pc[:, 49:50], ALU.add, ["evpc"], ["mc"])
        S.tt("dve", mc[:, 25:26], pc[:, 50:51], pc[:, 51:52], ALU.add, ["evpc"], ["mc"])
        S.ts("dve", mc[:], mc[:], -1.0, 1.0, ALU.mult, ALU.add, ["mc"], ["mc"])
        wup = S.sb(st, "wup", [128, 2, DA])
        S.dma("act", wup[:], I["ev_wup"][i].rearrange("g p c -> p g c"), writes=["wup"])
        masks = S.sb(st, "masks", [128, 5, 64])
        S.dma("sp", masks[:], I["c_masks"], writes=["masks"])
        ident = S.sb(st, "ident", [128, 128])
        S.dma("act", ident[:], I["c_ident"], writes=["ident"])
        bones = S.sb(st, "bones", [128, 128])
        S.dma("sp", bones[:], I["c_bones"], writes=["bones"])
        bones_s = S.sb(st, "bones_s", [128, 128])
        S.ts("dve", bones_s[:], bones[:], 1.0 / 64, None, ALU.mult, None, ["bones"], ["bones_s"])
        scanm = S.sb(st, "scanm", [128, 2, NB])
        S.dma("act", scanm[:], I["c_scanm"], writes=["scanm"])
        RT = S.sb(st, "RT", [128, NP, NB])
        KT = S.sb(st, "KT", [128, NP, NB])
        BT = S.sb(st, "BT", [128, NP, NB])
        AT = S.sb(st, "AT", [128, NP, NB])
        VT = S.sb(st, "VT", [128, NP, NB])
        KTt = S.sb(st, "KTt", [128, NCH, NP, 64])
        BTt = S.sb(st, "BTt", [128, NCH, NP, 64])
        VTt = S.sb(st, "VTt", [128, NCH, NP, 64])
        gCt = S.sb(st, "gCt", [128, NP, NCH])
        H = S.sb(st, "H", [128, NP, 64])
        XY = [S.sb(st, "XY%d" % m, [128, 2, 2, 64]) for m in range(NP)]
        Mt = [[S.sb(st, "M%d_%d" % (m, q), [128, 64]) for q in range(2)] for m in range(NP)]
        DE = [[S.sb(st, "DE%d_%d" % (m, q), [128, 3, 64]) for q in range(2)] for m in range(NP)]
        Wm = S.sb(st, "Wm", [128, NP, 64])
        U = S.sb(st, "U", [128, NP, 64])
        tmpH = S.sb(st, "tmpH", [128, NP, 64])
        Ost = S.sb(st, "Ost", [128, NP, NB])
        Ost2 = S.sb(st, "Ost2", [128, NP, NB])
        raw = [S.sb(st, "raw%d" % q, [128, 3, NB + 2]) for q in range(2)]
        craw = S.sb(st, "craw", [128, 2, NB + 2])
        cwT = S.sb(st, "cwT", [128, NB])
        caT = S.sb(st, "caT", [128, NB])
        tn = {}
        for nm in ("rs", "ks", "vs", "ld", "aa", "L", "eL", "eLn", "eLp", "kkr", "sq", "kk", "kd", "t1", "bvt", "t2"):
            tn[nm] = S.sb(st, "p_" + nm, [128, NB])
        psA = [S.ps(st, "psA%d" % q, [128, 512]) for q in range(2)]
        psL = [S.ps(st, "psL%d" % q, [128, 512]) for q in range(2)]
        psB = [S.ps(st, "psB%d" % q, [128, 512]) for q in range(2)]
        ppre = [S.ps(st, "ppre%d" % q, [128, 512]) for q in range(2)]
        I2 = S.sb(st, "I2", [128, 64])
        for h in range(2):
            S.cp("pool", I2[64 * h:64 * h + 64, :], ident[64 * h:64 * h + 64, 64 * h:64 * h + 64], ["ident"], ["I2"])

        def shift(eng, out, outn, src, srcn, c0, c1, cm):
            S.ts(eng, out, src[:, 1:NB + 1], cm, None, ALU.mult, None, [srcn, "evpc", "mc"], [outn])
            S.stt(eng, out, src[:, 0:NB], c0, out, ALU.mult, ALU.add, [srcn, "evpc", outn], [outn])
            S.stt(eng, out, src[:, 2:NB + 2], c1, out, ALU.mult, ALU.add, [srcn, "evpc", outn], [outn])

        nraw = 0
        n_pa = 0
        n_pl = 0
        n_pp = 0
        n_ev = 0
        for d in range(2):
            S.memset("pool", H[:], 0.0, ["H"])
            if d == 0:
                order = list(range(0, T, NB))
            else:
                order = [0] + list(range(T - NB, 0, -NB))
            for t0 in order[:maxblk]:
                lz = (t0 == 0 or t0 == CTX)
                rz = (t0 + NB == CTX or t0 + NB == T)
                lo = t0 if lz else t0 - 1
                hi = t0 + NB if rz else t0 + NB + 1
                c_lo = lo - (t0 - 1)
                c_hi = hi - (t0 - 1)
                if lz:
                    S.memset("pool", craw[:, :, 0:1], 0.0, ["craw"])
                if rz:
                    S.memset("pool", craw[:, :, NB + 1:NB + 2], 0.0, ["craw"])
                S.dma("sp", craw[:, :, c_lo:c_hi], FTc[:, :, lo:hi], writes=["craw"])
                shift("dve", cwT[:], "cwT", craw[:, 0, :], "craw", pc[:, 48:49], pc[:, 49:50], mc[:, 24:25])
                S.act(cwT[:], cwT[:], AF.Tanh, ["cwT"], ["cwT"])
                shift("pool", caT[:], "caT", craw[:, 1, :], "craw", pc[:, 50:51], pc[:, 51:52], mc[:, 25:26])
                for m in range(NP):
                    rb = nraw % 2
                    nraw += 1
                    rw = raw[rb]
                    rwn = ("raw", rb)
                    if lz:
                        S.memset("pool", rw[:, :, 0:1], 0.0, [rwn])
                    if rz:
                        S.memset("pool", rw[:, :, NB + 1:NB + 2], 0.0, [rwn])
                    S.dma("sp" if m % 2 == 0 else "act", rw[:, :, c_lo:c_hi], FT3[m][:, :, lo:hi], writes=[rwn])
                    rs, ks, vs = tn["rs"], tn["ks"], tn["vs"]
                    shift("dve", rs[:], "rs", rw[:, 0, :], rwn, pc[:, 0 + m:1 + m], pc[:, 8 + m:9 + m], mc[:, m:m + 1])
                    shift("pool", ks[:], "ks", rw[:, 1, :], rwn, pc[:, 16 + m:17 + m], pc[:, 24 + m:25 + m], mc[:, 8 + m:9 + m])
                    shift("dve", vs[:], "vs", rw[:, 2, :], rwn, pc[:, 32 + m:33 + m], pc[:, 40 + m:41 + m], mc[:, 16 + m:17 + m])
                    db = 64 * d
                    pa = ppre[n_pp % 2]
                    pan = ("ppre", n_pp % 2)
                    n_pp += 1
                    S.mm(pa[:, 0:NB], wup[db:db + 64, 0, 128 * m:128 * m + 128], cwT[db:db + 64, :], True, True, ["wup", "cwT"], [pan])
                    ld = tn["ld"]
                    S.act(ld[:], pa[:, 0:NB], AF.Sigmoid, [pan, "evpc"], ["ld"], bias=pc[:, 52 + 8 * d + m:53 + 8 * d + m])
                    S.ts("dve", ld[:], ld[:], -0.6065306597126334, None, ALU.mult, None, ["ld"], ["ld"])
                    pa2 = ppre[n_pp % 2]
                    pan2 = ("ppre", n_pp % 2)
                    n_pp += 1
                    S.mm(pa2[:, 0:NB], wup[db:db + 64, 1, 128 * m:128 * m + 128], caT[db:db + 64, :], True, True, ["wup", "caT"], [pan2])
                    aa = tn["aa"]
                    S.act(aa[:], pa2[:, 0:NB], AF.Sigmoid, [pan2, "evpc"], ["aa"], bias=pc[:, 68 + 8 * d + m:69 + 8 * d + m])
                    L, eL, eLn, eLp = tn["L"], tn["eL"], tn["eLn"], tn["eLp"]
                    if d == 0:
                        S.op("dve", (lambda e: e.tensor_tensor_scan(out=L[:], data0=scanm[:, 0, :], data1=ld[:], initial=0.0,
                                                                     op0=ALU.mult, op1=ALU.add)), ["scanm", "ld"], ["L"])
                    else:
                        S.op("dve", (lambda e: e.tensor_tensor_scan(out=L[:, ::-1], data0=scanm[:, 1, ::-1], data1=ld[:, ::-1], initial=0.0,
                                                                     op0=ALU.mult, op1=ALU.add)), ["scanm", "ld"], ["L"])
                    S.act(eL[:], L[:], AF.Exp, ["L"], ["eL"])
                    S.act(eLn[:], L[:], AF.Exp, ["L"], ["eLn"], scale=-1.0)
                    S.tt("pool", eLp[:], L[:], ld[:], ALU.subtract, ["L", "ld"], ["eLp"])
                    S.act(eLp[:], eLp[:], AF.Exp, ["eLp"], ["eLp"])
                    kkr, sq, kk, kd, t1, t2, bvt = tn["kkr"], tn["sq"], tn["kk"], tn["kd"], tn["t1"], tn["t2"], tn["bvt"]
                    S.ts("pool", kkr[:], ks[:], pc[:, 84 + m:85 + m], None, ALU.mult, None, ["ks", "evpc"], ["kkr"])
                    S.tt("pool", sq[:], kkr[:], kkr[:], ALU.mult, ["kkr"], ["sq"])
                    pa3 = ppre[n_pp % 2]
                    pan3 = ("ppre", n_pp % 2)
                    n_pp += 1
                    S.mm(pa3[:, 0:NB], bones[:], sq[:], True, True, ["bones", "sq"], [pan3])
                    S.ts("dve", sq[:], pa3[:, 0:NB], 1e-12, None, ALU.add, None, [pan3], ["sq"])
                    S.act(sq[:], sq[:], AF.Ln, ["sq"], ["sq"])
                    S.act(sq[:], sq[:], AF.Exp, ["sq"], ["sq"], scale=-0.5)
                    S.tt("dve", kk[:], kkr[:], sq[:], ALU.mult, ["kkr", "sq"], ["kk"])
                    S.ts("pool", t1[:], aa[:], -1.0, pc[:, 92 + m:93 + m], ALU.add, ALU.mult, ["aa", "evpc"], ["t1"])
                    S.stt("dve", kd[:], t1[:], 1.0, ks[:], ALU.add, ALU.mult, ["t1", "ks"], ["kd"])
                    S.stt("pool", t2[:], rs[:], pc[:, 100 + m:101 + m], kd[:], ALU.mult, ALU.mult, ["rs", "evpc", "kd"], ["t2"])
                    pa4 = ppre[n_pp % 2]
                    pan4 = ("ppre", n_pp % 2)
                    n_pp += 1
                    S.mm(pa4[:, 0:NB], bones[:], t2[:], True, True, ["bones", "t2"], [pan4])
                    S.tt("dve", bvt[:], pa4[:, 0:NB], vs[:], ALU.mult, [pan4, "vs"], ["bvt"])
                    S.dma("act", K.OT[d, 1, 128 * m:128 * m + 128, t0:t0 + NB], bvt[:], reads=["bvt"], writes=[("OTb", d, m)])
                    rv = (lambda ap: ap[:, ::-1]) if d == 1 else (lambda ap: ap)
                    S.tt("dve", rv(RT[:, m, :]), rs[:], eL[:], ALU.mult, ["rs", "eL"], [("RT", m)])
                    S.tt("pool", rv(KT[:, m, :]), kd[:], eLn[:], ALU.mult, ["kd", "eLn"], [("KT", m)])
                    S.tt("pool", t1[:], kk[:], aa[:], ALU.mult, ["kk", "aa"], ["t1"])
                    S.tt("dve", rv(BT[:, m, :]), t1[:], eLn[:], ALU.mult, ["t1", "eLn"], [("BT", m)])
                    S.stt("pool", rv(AT[:, m, :]), kk[:], -1.0, eLp[:], ALU.mult, ALU.mult, ["kk", "eLp"], [("AT", m)])
                    S.cp("act", rv(VT[:, m, :]), vs[:], ["vs"], [("VT", m)])
                    if d == 0:
                        S.cp("pool", gCt[:, m, :], eL[:, 63::64], ["eL"], ["gCt"])
                    else:
                        S.cp("pool", gCt[:, m, :], eL[:, NB - 64::-64], ["eL"], ["gCt"])
                for c in range(NCH):
                    cs = slice(64 * c, 64 * c + 64)
                    for (src, srcn, dst, dstn) in ((KT, "KT", KTt, "KTt"), (BT, "BT", BTt, "BTt"), (VT, "VT", VTt, "VTt")):
                        pt = ppre[n_pp % 2]
                        ptn = ("ppre", n_pp % 2)
                        n_pp += 1
                        for m in range(NP):
                            for h in range(2):
                                pb = 64 * h
                                if h == 0:
                                    S.tr(pt[0:64, m * 64:(m + 1) * 64], src[0:64, m, cs], ident[0:64, 0:64],
                                         [(srcn, m), "ident"], [ptn], signal=False)
                                else:
                                    S.mm(pt[pb:pb + 64, m * 64:(m + 1) * 64], src[pb:pb + 64, m, cs], ident[pb:pb + 64, pb:pb + 64],
                                         True, True, [(srcn, m), "ident"], [ptn], signal=(m == NP - 1), tp=(pb, pb))
                        n_ev += 1
                        S.cp("act" if n_ev % 2 == 0 else "dve", dst[:, c, :, :], pt[:].rearrange("p (m k) -> p m k", k=64),
                             [ptn], [(dstn, c)])

                def stage_a(c):
                    nonlocal n_pa, n_pl
                    cs = slice(64 * c, 64 * c + 64)
                    par = c % 2
                    for m in range(NP):
                        pa_ = psA[n_pa % 2]
                        pn = ("psA", n_pa % 2)
                        n_pa += 1
                        pav = pa_[:, 0:320].rearrange("p (s w) -> p s w", w=64)
                        for h in range(2):
                            pb = 64 * h
                            ops = ((BT, "BT", AT, "AT"), (AT, "AT", BT, "BT"), (KT, "KT", AT, "AT"), (KT, "KT", RT, "RT"), (BT, "BT", RT, "RT"))
                            for sl, (la, lan, ra, ran) in enumerate(ops):
                                S.mm(pav[pb:pb + 64, sl, :], la[pb:pb + 64, m, cs], ra[pb:pb + 64, m, cs], True, True,
                                     [(lan, m), (ran, m)], [pn], signal=(h == 1 and sl == 4), tp=(pb, pb))
                        S.tt("dve", XY[m][:, 0, :, :], pav[:, 0:2, :], masks[:, 0:2, :], ALU.mult, [pn, "masks"], [("XY", m, 0)])
                        S.tt("dve", DE[m][par][:], pav[:, 2:5, :], masks[:, 2:5, :], ALU.mult, [pn, "masks"], [("DE", m, par)])
                        S.tt("pool", Mt[m][par][:], XY[m][:, 0, 0, :], I2[:], ALU.add, [("XY", m, 0), "I2"], [("M", m, par)])
                    for j in range(5):
                        cur, nxt = j % 2, 1 - (j % 2)
                        for m in range(NP):
                            pl = psL[n_pl % 2]
                            pln = ("psL", n_pl % 2)
                            n_pl += 1
                            plv = pl[:, 0:192].rearrange("p (s w) -> p s w", w=64)
                            for h in range(2):
                                pb = 64 * h
                                X_, Y_ = XY[m][pb:pb + 64, cur, 0, :], XY[m][pb:pb + 64, cur, 1, :]
                                if j < 4:
                                    S.mm(plv[pb:pb + 64, 0, :], Y_, X_, True, True, [("XY", m, cur)], [pln], signal=False, tp=(pb, pb))
                                S.mm(plv[pb:pb + 64, 1, :], X_, Y_, True, True, [("XY", m, cur)], [pln], signal=(h == 1), tp=(pb, pb))
                            if j < 4:
                                S.cp("act", XY[m][:, nxt, :, :], plv[:, 0:2, :], [pln], [("XY", m, nxt)])
                            else:
                                S.cp("act", XY[m][:, nxt, 1, :], plv[:, 1, :], [pln], [("XY", m, nxt)])
                            for h in range(2):
                                pb = 64 * h
                                S.mm(plv[pb:pb + 64, 2, :], XY[m][pb:pb + 64, nxt, 1, :], Mt[m][par][pb:pb + 64, :], True, True,
                                     [("XY", m, nxt), ("M", m, par)], [pln], signal=(h == 1), tp=(pb, pb))
                            S.tt("dve", Mt[m][par][:], plv[:, 2, :], Mt[m][par][:], ALU.add, [pln, ("M", m, par)], [("M", m, par)])

                def stage_b(c):
                    cs = slice(64 * c, 64 * c + 64)
                    par = c % 2
                    pw = psB[0][:].rearrange("p (m k) -> p m k", k=64)
                    pu = psB[1][:].rearrange("p (m k) -> p m k", k=64)
                    for m in range(NP):
                        for h in range(2):
                            pb = 64 * h
                            S.mm(pw[pb:pb + 64, m, :], AT[pb:pb + 64, m, cs], H[pb:pb + 64, m, :], True, False,
                                 [("AT", m), "H"], [("psB", 0)], signal=False, tp=(pb, pb))
                            S.mm(pw[pb:pb + 64, m, :], DE[m][par][pb:pb + 64, 0, :], VTt[pb:pb + 64, c, m, :], False, True,
                                 [("DE", m, par), ("VTt", c)], [("psB", 0)], signal=(m == NP - 1 and h == 1), tp=(pb, pb))
                    S.cp("act", Wm[:], pw, [("psB", 0)], ["Wm"])
                    for m in range(NP):
                        for h in range(2):
                            pb = 64 * h
                            S.mm(pu[pb:pb + 64, m, :], Mt[m][par][pb:pb + 64, :], Wm[pb:pb + 64, m, :], True, True,
                                 [("M", m, par), "Wm"], [("psB", 1)], signal=(m == NP - 1 and h == 1), tp=(pb, pb))
                    S.cp("act", U[:], pu, [("psB", 1)], ["U"])
                    for m in range(NP):
                        for h in range(2):
                            pb = 64 * h
                            S.mm(pw[pb:pb + 64, m, :], H[pb:pb + 64, m, :], RT[pb:pb + 64, m, cs], True, False,
                                 ["H", ("RT", m)], [("psB", 0)], signal=False, tp=(pb, pb))
                            S.mm(pw[pb:pb + 64, m, :], U[pb:pb + 64, m, :], DE[m][par][pb:pb + 64, 2, :], False, False,
                                 ["U", ("DE", m, par)], [("psB", 0)], signal=False, tp=(pb, pb))
                            S.mm(pw[pb:pb + 64, m, :], VTt[pb:pb + 64, c, m, :], DE[m][par][pb:pb + 64, 1, :], False, True,
                                 [("VTt", c), ("DE", m, par)], [("psB", 0)], signal=(m == NP - 1 and h == 1), tp=(pb, pb))
                    S.cp("act", Ost[:, :, cs], pw, [("psB", 0)], ["Ost"])
                    for m in range(NP):
                        for h in range(2):
                            pb = 64 * h
                            S.mm(pu[pb:pb + 64, m, :], BTt[pb:pb + 64, c, m, :], U[pb:pb + 64, m, :], True, False,
                                 [("BTt", c), "U"], [("psB", 1)], signal=False, tp=(pb, pb))
                            S.mm(pu[pb:pb + 64, m, :], KTt[pb:pb + 64, c, m, :], VTt[pb:pb + 64, c, m, :], False, True,
                                 [("KTt", c), ("VTt", c)], [("psB", 1)], signal=(m == NP - 1 and h == 1), tp=(pb, pb))
                    S.tt("dve", tmpH[:], pu, H[:], ALU.add, [("psB", 1), "H"], ["tmpH"])
                    S.tt("pool", H[:], tmpH[:], gCt[:, :, c:c + 1].to_broadcast([128, NP, 64]), ALU.mult, ["tmpH", "gCt"], ["H"])

                for c in range(NCH):
                    stage_a(c)
                    if c >= 1:
                        stage_b(c - 1)
                stage_b(NCH - 1)
                OTv = K.OT[d, 0].rearrange("(m p) t -> p m t", p=128)
                if d == 0:
                    S.dma("sp", OTv[:, :, t0:t0 + NB], Ost[:], reads=["Ost"], writes=[("OTo", d)])
                else:
                    S.cp("pool", Ost2[:, :, ::-1], Ost[:], ["Ost"], ["Ost2"])
                    S.dma("sp", OTv[:, :, t0:t0 + NB], Ost2[:], reads=["Ost2"], writes=[("OTo", d)])
        S.barrier()
        ro = {}
        for nm in ("o0", "o1", "b0", "b1", "gat", "cen", "sq2", "sg"):
            ro[nm] = [S.sb(st, "ro_%s%d" % (nm, q), [128, NB]) for q in range(2)]
        y16 = [S.sb(st, "ro_y%d" % q, [128, NB], BF16) for q in range(2)]
        nro = 0
        for m in range(NP):
            for t0 in range(0, T, NB):
                q = nro % 2
                nro += 1
                o0, o1, b0, b1, gat, cen, sq2, sg = (ro[nm][q] for nm in ("o0", "o1", "b0", "b1", "gat", "cen", "sq2", "sg"))
                tk = lambda nm: ("ro", nm, q)
                rsl = slice(128 * m, 128 * m + 128)
                S.dma("sp", o0[:], K.OT[0, 0, rsl, t0:t0 + NB], writes=[tk("o0")])
                S.dma("act", o1[:], K.OT[1, 0, rsl, t0:t0 + NB], writes=[tk("o1")])
                S.dma("sp", b0[:], K.OT[0, 1, rsl, t0:t0 + NB], writes=[tk("b0")])
                S.dma("act", b1[:], K.OT[1, 1, rsl, t0:t0 + NB], writes=[tk("b1")])
                S.dma("sp", gat[:], K.FT[3328 + 128 * m:3328 + 128 * m + 128, t0:t0 + NB], writes=[tk("gat")])
                S.tt("pool", o0[:], o0[:], o1[:], ALU.add, [tk("o0"), tk("o1")], [tk("o0")])
                S.tt("pool", b0[:], b0[:], b1[:], ALU.add, [tk("b0"), tk("b1")], [tk("b0")])
                pm_ = psA[q]
                S.mm(pm_[:, 0:NB], bones_s[:], o0[:], True, True, ["bones_s", tk("o0")], [("psA", q)])
                S.tt("dve", cen[:], o0[:], pm_[:, 0:NB], ALU.subtract, [tk("o0"), ("psA", q)], [tk("cen")])
                S.tt("pool", sq2[:], cen[:], cen[:], ALU.mult, [tk("cen")], [tk("sq2")])
                pv_ = psL[q]
                S.mm(pv_[:, 0:NB], bones_s[:], sq2[:], True, True, ["bones_s", tk("sq2")], [("psL", q)])
                S.ts("dve", sq2[:], pv_[:, 0:NB], GN_EPS, None, ALU.add, None, [("psL", q)], [tk("sq2")])
                S.act(sq2[:], sq2[:], AF.Ln, [tk("sq2")], [tk("sq2")])
                S.act(sq2[:], sq2[:], AF.Exp, [tk("sq2")], [tk("sq2")], scale=-0.5)
                S.tt("dve", cen[:], cen[:], sq2[:], ALU.mult, [tk("cen"), tk("sq2")], [tk("cen")])
                S.ts("dve", cen[:], cen[:], pc[:, 108 + m:109 + m], pc[:, 116 + m:117 + m], ALU.mult, ALU.add, [tk("cen"), "evpc"], [tk("cen")])
                S.tt("pool", cen[:], cen[:], b0[:], ALU.add, [tk("cen"), tk("b0")], [tk("cen")])
                S.act(sg[:], gat[:], AF.Sigmoid, [tk("gat")], [tk("sg")])
                S.tt("pool", gat[:], gat[:], sg[:], ALU.mult, [tk("gat"), tk("sg")], [tk("gat")])
                S.tt("dve", y16[q][:], cen[:], gat[:], ALU.mult, [tk("cen"), tk("gat")], [("roy", q)])
                S.dma("act", K.YT[rsl, t0:t0 + NB], y16[q][:], reads=[("roy", q)], writes=[("YT", "all")])


def phase_na(K, l):
    S, I = K.S, K.I
    i = l // 2
    import os
    maxrows = int(os.environ.get("NA_MAXROWS", "1000"))
    npairs = int(os.environ.get("NA_PAIRS", "8"))
    with contextlib.ExitStack() as st:
        ones = S.sb(st, "ones", [128, 64], BF16)
        S.memset("pool", ones[:], 1.0, ["ones"])
        q2 = [S.sb(st, "q2_%d" % b, [128, T], BF16) for b in range(2)]
        k2 = [S.sb(st, "k2_%d" % b, [128, T], BF16) for b in range(2)]
        Ve = [S.sb(st, "Ve%d" % b, [128, NTILE, 128], BF16) for b in range(2)]
        Vo = [S.sb(st, "Vo%d" % b, [128, NTILE - 1, 128], BF16) for b in range(2)]
        tb2 = [S.sb(st, "tb2_%d" % b, [128, 2, 14, 64]) for b in range(2)]
        gb = S.sb(st, "gb", [128, T])
        sgb = S.sb(st, "sgb", [128, T])
        ybT = S.sb(st, "ybT", [128, T])
        yb16 = S.sb(st, "yb16", [128, T], BF16)
        sT = [S.sb(st, "sT%d" % b, [128, 2, 256]) for b in range(2)]
        pT = [S.sb(st, "pT%d" % b, [128, 2, 384], BF16) for b in range(2)]
        rden = [S.sb(st, "rden%d" % b, [128, 64]) for b in range(2)]
        pss = [S.ps(st, "pss%d" % b, [128, 2, 512]) for b in range(2)]
        pso = [S.ps(st, "pso%d" % b, [128, 512]) for b in range(2)]
        VBe = K.VB.rearrange("(i p) c -> p i c", p=128)
        VBo = K.VB[64:64 + (NTILE - 1) * 128, :].rearrange("(i p) c -> p i c", p=128)
        nrow = 0
        for m in range(npairs):
            b = m % 2
            S.dma("sp", q2[b][:], K.QK[128 * m:128 * m + 128, :], writes=[("q2", b)])
            S.dma("act", k2[b][:], K.QK[1024 + 128 * m:1024 + 128 * m + 128, :], writes=[("k2", b)])
            for hh in range(2):
                i0, i1 = hh * 17, hh * 17 + 17
                S.dma("sp", Ve[b][:, i0:i1, :], VBe[:, i0:i1, 128 * m:128 * m + 128], writes=[("Ve", b)])
                j1 = min(i1, NTILE - 1)
                S.dma("act", Vo[b][:, i0:j1, :], VBo[:, i0:j1, 128 * m:128 * m + 128], writes=[("Vo", b)])
            S.dma("sp", tb2[b][:], I["ev_tb"][i, 2 * m:2 * m + 2].rearrange("h p s w -> p h s w"), writes=[("tb2", b)])
            S.dma("act", gb[:], K.FT[4352 + 128 * m:4352 + 128 * m + 128, :], writes=["gb"])
            rows = [("c", r) for r in range(CTX // 64)] + [("x", y) for y in range(64)]
            rows = rows[:maxrows]

            def rowinfo(kind, y):
                if kind == "x":
                    y0 = min(max(y - 4, 0), 56)
                    p_ = y - y0
                    tok0 = CTX + 64 * y0
                    q0 = CTX + 64 * y
                    chunks = [tok0 + 128 * c for c in range(4)] + [0, 128]
                else:
                    p_ = 0
                    q0 = 64 * y
                    chunks = [0, 128]
                return p_, q0, chunks

            def s1(kind, y, rb):
                p_, q0, chunks = rowinfo(kind, y)
                nch = len(chunks)
                for h in range(2):
                    pb = 64 * h
                    for c, kt0 in enumerate(chunks):
                        S.mm(pss[rb][:, h, c * 64:(c + 1) * 64], k2[b][pb:pb + 64, kt0:kt0 + 128], q2[b][pb:pb + 64, q0:q0 + 64],
                             True, True, [("k2", b), ("q2", b)], [("pss", rb, h)], signal=(c == nch - 1))

            def s2(kind, y, rb):
                p_, q0, chunks = rowinfo(kind, y)
                nch = len(chunks)
                psr = [("pss", rb, 0), ("pss", rb, 1)]
                if kind == "x":
                    for h in range(2):
                        S.stt("dve", sT[rb][:, h, :].rearrange("p (c w) -> p c w", w=64),
                              pss[rb][:, h, 0:256].rearrange("p (c w) -> p c w", w=64), 0.125,
                              tb2[b][:, h, 7 - p_:7 - p_ + 7:2, :], ALU.mult, ALU.add, [("pss", rb, h), ("tb2", b)], [("sT", rb)])
                    S.act(pT[rb][:, :, 0:256], sT[rb][:], AF.Exp, [("sT", rb)], [("pT", rb)])
                    S.act(pT[rb][:, :, 256:384], pss[rb][:, :, 256:384], AF.Exp, psr, [("pT", rb)], scale=0.125)
                else:
                    S.act(pT[rb][:, :, 0:128], pss[rb][:, :, 0:128], AF.Exp, psr, [("pT", rb)], scale=0.125)
                for h in range(2):
                    pb = 64 * h
                    for c, kt0 in enumerate(chunks):
                        if kt0 % 128 == 0:
                            vch = Ve[b][:, kt0 // 128, pb:pb + 64]
                            vr = ("Ve", b)
                        else:
                            vch = Vo[b][:, (kt0 - 64) // 128, pb:pb + 64]
                            vr = ("Vo", b)
                        S.mm(pso[rb][pb:pb + 64, 0:64], vch, pT[rb][:, h, c * 64:(c + 1) * 64], c == 0, c == nch - 1,
                             [vr, ("pT", rb)], [("pso", rb)], signal=False, tp=(0, pb))
                    for c in range(nch):
                        S.mm(pso[rb][pb:pb + 64, 64:128], ones[:], pT[rb][:, h, c * 64:(c + 1) * 64], c == 0, c == nch - 1,
                             ["ones", ("pT", rb)], [("pso", rb)], signal=(h == 1 and c == nch - 1), tp=(0, pb))
                S.op("dve", (lambda e, rb=rb: e.reciprocal(out=rden[rb][:], in_=pso[rb][:, 64:128])), [("pso", rb)], [("rden", rb)])
                S.tt("dve", ybT[:, q0:q0 + 64], pso[rb][:, 0:64], rden[rb][:], ALU.mult, [("pso", rb), ("rden", rb)], ["ybT"])

            if rows:
                s1(rows[0][0], rows[0][1], nrow % 2)
            for ri, (kind, y) in enumerate(rows):
                rb = nrow % 2
                nrow += 1
                if ri + 1 < len(rows):
                    s1(rows[ri + 1][0], rows[ri + 1][1], nrow % 2)
                s2(kind, y, rb)
            S.act(sgb[:], gb[:], AF.Sigmoid, ["gb"], ["sgb"])
            S.tt("pool", sgb[:], sgb[:], gb[:], ALU.mult, ["sgb", "gb"], ["sgb"])
            S.tt("pool", yb16[:], ybT[:], sgb[:], ALU.mult, ["ybT", "sgb"], ["yb16"])
            S.dma("sp", K.YT[1024 + 128 * m:1024 + 128 * m + 128, :], yb16[:], reads=["yb16"], writes=[("YT", "all")])


def _col(v, n=128):
    v = np.asarray(v, np.float32)
    return np.ascontiguousarray(v.reshape(-1, n).T)


def host_constants():
    import ml_dtypes
    c = {}
    c["c_ident"] = np.eye(128, dtype=np.float32)
    c["c_identb"] = np.eye(128, dtype=np.float32).astype(ml_dtypes.bfloat16)
    m = np.zeros((128, 5, 64), np.float32)
    su = np.triu(np.ones((64, 64), np.float32), 1)
    iu = np.triu(np.ones((64, 64), np.float32), 0)
    for h in range(2):
        m[h * 64:(h + 1) * 64, 0] = su
        m[h * 64:(h + 1) * 64, 1] = su.T
        m[h * 64:(h + 1) * 64, 2] = su
        m[h * 64:(h + 1) * 64, 3] = iu
        m[h * 64:(h + 1) * 64, 4] = iu
    c["c_masks"] = m
    bo = np.zeros((128, 128), np.float32)
    bo[:64, :64] = 1.0
    bo[64:, 64:] = 1.0
    c["c_bones"] = bo
    sm = np.ones((128, 2, 256), np.float32)
    sm[:, 0, 0::64] = 0.0
    sm[:, 1, 63::64] = 0.0
    c["c_scanm"] = sm
    return c


def host_layout(inputs, b):
    f = lambda k: np.asarray(inputs[k], np.float32)
    m = {}
    m["xin"] = np.ascontiguousarray(np.concatenate([f("ctx")[b], f("x")[b]], axis=0))
    ccm = np.stack([f("c")[b], f("c_ctx")], axis=-1)
    m["cc"] = np.ascontiguousarray(ccm.reshape(16, 128, 2).transpose(1, 0, 2))
    for k in ("mod_w", "mod_b", "norm_pre", "norm_post", "ev_w_in", "ev_w_out", "od_w_in", "od_w_out"):
        m[k] = f(k)
    m["od_gw"] = np.ascontiguousarray(np.stack([f("od_gate_a_w"), f("od_gate_x_w")], axis=1))
    pcs = []
    for i in range(2):
        cols = [_col(f("od_conv_w")[i, j]) for j in range(4)]
        cols.append(_col(f("od_conv_b")[i]))
        cols += [_col(f("od_gate_a_b")[i, d]) for d in range(2)]
        cols += [_col(f("od_gate_x_b")[i, d]) for d in range(2)]
        cols += [_col(f("od_lambda")[i, d]) for d in range(2)]
        pcs.append(np.stack(cols, axis=1))
    m["od_pc"] = np.ascontiguousarray(np.stack(pcs, axis=0))
    evs = []
    for i in range(2):
        mu = f("ev_mu")[i]
        cols = []
        for part in range(3):
            for j in range(2):
                cols.append(_col(mu[j, part * DA:(part + 1) * DA]))
        cols.append(_col(mu[0, 3 * DA:3 * DA + 128]))
        cols.append(_col(mu[1, 3 * DA:3 * DA + 128]))
        cols.append(_col(mu[0, 3 * DA + 128:3 * DA + 256]))
        cols.append(_col(mu[1, 3 * DA + 128:3 * DA + 256]))
        for d in range(2):
            cols.append(_col(f("ev_w0")[i, d]))
        for d in range(2):
            cols.append(_col(f("ev_a0")[i, d]))
        cols.append(_col(f("ev_k_k")[i]))
        cols.append(_col(f("ev_k_a")[i]))
        cols.append(_col(f("ev_r_k")[i].reshape(-1)))
        cols.append(_col(f("ev_gn_w")[i]))
        cols.append(_col(f("ev_gn_b")[i]))
        ev = np.concatenate(cols, axis=1)
        assert ev.shape[1] == EVPC_N, ev.shape
        evs.append(ev)
    m["ev_pc"] = np.ascontiguousarray(np.stack(evs, axis=0))
    wup = np.stack([f("ev_w_up").reshape(2, 128, DA), f("ev_a_up").reshape(2, 128, DA)], axis=1)
    m["ev_wup"] = np.ascontiguousarray(wup)
    rpb = f("ev_rpb")
    cols_ = np.arange(GRID_W)
    cstart = np.clip(cols_ - 8, 0, GRID_W - 16)
    tb = np.full((2, 16, 128, 14, 64), -60.0, np.float32)
    cc_, ww_ = np.meshgrid(np.arange(64), np.arange(64), indexing="ij")
    valid = (cc_ >= cstart[ww_]) & (cc_ < cstart[ww_] + 16)
    dx = np.clip(cc_ - ww_ + 15, 0, 30)
    for half in range(2):
        for slot in range(14):
            g = rpb[:, :, slot + half, :][:, :, dx]
            tb[:, :, half * 64:(half + 1) * 64, slot, :] = np.where(valid[None, None], g, np.float32(-60.0))
    m["ev_tb"] = tb
    m.update(host_constants())
    return m


_CACHE = {}


def kernel(**inputs):
    layers = (0, 1, 2, 3)
    key = ("full", layers)
    if key not in _CACHE:
        _CACHE[key] = build_program(list(layers))
    nc = _CACHE[key]
    n = 4
    in_maps = [host_layout(inputs, b) for b in range(n)]
    res = run_bass_kernel_spmd(nc, in_maps, core_ids=list(range(n)))
    out = np.stack([np.asarray(res.results[b]["out"], np.float32) for b in range(n)], axis=0)
    return out
```

```python
import contextlib
import numpy as np
import concourse.bass as bass
import concourse.mybir as mybir
from concourse.bass_utils import run_bass_kernel_spmd

F32 = mybir.dt.float32
BF16 = mybir.dt.bfloat16
ALU = mybir.AluOpType
AF = mybir.ActivationFunctionType
AX = mybir.AxisListType

ENGS = ("pe", "dve", "act", "pool", "sp")

D = 2048
CTX = 256
SEQ = 4096
T = CTX + SEQ
NTILE = T // 128
DEPTH = 4
A_SH = 3328
EV_IN = 8448
DA = 1024
DC = 2560
NQ = DC // 128
GRID_W = 64
EPS_RMS = 1e-6
GN_EPS = 64e-5


class Sched:
    def __init__(self, nc, stack, n_dma_sems=12):
        self.nc = nc
        self.stack = stack
        self.stream = {e: [] for e in ENGS}
        self.cnt = {e: 0 for e in ENGS}
        self.sem = {e: stack.enter_context(nc.semaphore("s_" + e)) for e in ENGS}
        self.seen = {e: {} for e in ENGS}
        self.lastw = {}
        self.readers = {}
        self.dsem = {}
        self.drr = {}
        for q in ("sp", "act", "pool"):
            self.dsem[q] = [[stack.enter_context(nc.semaphore("d_%s%d" % (q, i))), 0]
                            for i in range(n_dma_sems)]
            self.drr[q] = 0
        self.semkey = {}
        self.uid = 0
        self.ninst = 0

    def sb(self, st, name, shape, dt=F32):
        self.uid += 1
        return st.enter_context(self.nc.sbuf_tensor("%s_%d" % (name, self.uid), list(shape), dt))

    def ps(self, st, name, shape, dt=F32):
        self.uid += 1
        return st.enter_context(self.nc.psum_tensor("%s_%d" % (name, self.uid), list(shape), dt))

    def _deps(self, reads, writes):
        need = {}

        def add(sv):
            s, v = sv
            k = id(s)
            self.semkey[k] = s
            if need.get(k, 0) < v:
                need[k] = v
        for t in reads:
            if t in self.lastw:
                add(self.lastw[t])
        for t in writes:
            if t in self.lastw:
                add(self.lastw[t])
            for sv in self.readers.get(t, {}).values():
                add(sv)
        return need

    def _commit_waits(self, e, need, skip_own=False):
        waits = []
        seen = self.seen[e]
        for k, v in need.items():
            if skip_own and k == id(self.sem[e]):
                continue
            if seen.get(k, 0) >= v:
                continue
            seen[k] = v
            waits.append((self.semkey[k], v))
        return waits

    def _mark(self, reads, writes, sv):
        s, v = sv
        for t in writes:
            self.lastw[t] = sv
            self.readers[t] = {}
        for t in reads:
            if t in writes:
                continue
            r = self.readers.setdefault(t, {})
            k = id(s)
            if k not in r or r[k][1] < v:
                r[k] = sv

    def op(self, e, fn, reads=(), writes=(), signal=True):
        need = self._deps(reads, writes)
        waits = self._commit_waits(e, need, skip_own=(e == "pe"))
        if signal:
            self.cnt[e] += 1
            v = self.cnt[e]
            inc = (self.sem[e], 1)
        else:
            v = self.cnt[e] + 1
            inc = None
        self.stream[e].append((waits, fn, inc))
        self.ninst += 1 + len(waits)
        self._mark(reads, writes, (self.sem[e], v))

    def dma(self, q, out, in_, reads=(), writes=(), **kw):
        need = self._deps(reads, writes)
        k = self.drr[q]
        self.drr[q] = (k + 1) % len(self.dsem[q])
        ent = self.dsem[q][k]
        s = ent[0]
        self.semkey[id(s)] = s
        if ent[1] > 0 and need.get(id(s), 0) < 16 * ent[1]:
            need[id(s)] = 16 * ent[1]
        waits = self._commit_waits(q, need)
        ent[1] += 1
        v = 16 * ent[1]
        self.stream[q].append((waits, (lambda eng: eng.dma_start(out=out, in_=in_, **kw)), (s, 16)))
        self.ninst += 1 + len(waits)
        self._mark(reads, writes, (s, v))

    def barrier(self):
        need = {}
        for q in self.dsem:
            for s, c in self.dsem[q]:
                if c > 0:
                    self.semkey[id(s)] = s
                    need[id(s)] = 16 * c
        for e in ENGS:
            if self.cnt[e] > 0:
                self.semkey[id(self.sem[e])] = self.sem[e]
                need[id(self.sem[e])] = self.cnt[e]
        for e in ENGS:
            waits = self._commit_waits(e, dict(need))
            if waits:
                self.stream[e].append((waits, None, None))
        self.lastw = {}
        self.readers = {}

    def emit(self):
        nc = self.nc
        st = self.stream

        def run(name, eng):
            for waits, fn, inc in st[name]:
                for s, v in waits:
                    eng.wait_ge(s, v)
                if fn is None:
                    continue
                ins = fn(eng)
                if inc is not None:
                    ins.then_inc(inc[0], inc[1])
        with nc.Block() as block:
            @block.tensor
            def _(e):
                run("pe", e)

            @block.vector
            def _(e):
                run("dve", e)

            @block.scalar
            def _(e):
                run("act", e)

            @block.gpsimd
            def _(e):
                run("pool", e)

            @block.sync
            def _(e):
                run("sp", e)

    def mm(self, out, lhsT, rhs, start, stop, reads, writes, signal=None, tp=None):
        if signal is None:
            signal = bool(stop)
        kw = {}
        if tp is not None:
            kw["tile_position"] = tp
        self.op("pe", lambda e: e.matmul(out, lhsT=lhsT, rhs=rhs, start=start, stop=stop, **kw),
                reads, writes, signal=signal)

    def tr(self, out, in_, ident, reads, writes, signal=True, tp=None):
        kw = {}
        if tp is not None:
            kw["tile_position"] = tp
        self.op("pe", lambda e: e.transpose(out, in_, ident, **kw), reads, writes, signal=signal)

    def tt(self, eng, out, in0, in1, op, reads, writes):
        self.op(eng, lambda e: e.tensor_tensor(out=out, in0=in0, in1=in1, op=op), reads, writes)

    def ts(self, eng, out, in0, s1, s2, op0, op1, reads, writes):
        if op1 is None:
            self.op(eng, lambda e: e.tensor_scalar(out=out, in0=in0, scalar1=s1, scalar2=None, op0=op0), reads, writes)
        else:
            self.op(eng, lambda e: e.tensor_scalar(out=out, in0=in0, scalar1=s1, scalar2=s2, op0=op0, op1=op1), reads, writes)

    def stt(self, eng, out, in0, scalar, in1, op0, op1, reads, writes):
        eng = "dve"
        self.op(eng, lambda e: e.scalar_tensor_tensor(out=out, in0=in0, scalar=scalar, in1=in1, op0=op0, op1=op1),
                reads, writes)

    def cp(self, eng, out, in_, reads, writes):
        if eng == "act":
            self.op(eng, lambda e: e.copy(out=out, in_=in_), reads, writes)
        else:
            self.op(eng, lambda e: e.tensor_copy(out=out, in_=in_), reads, writes)

    def act(self, out, in_, func, reads, writes, bias=None, scale=None, accum_out=None):
        kw = {}
        if bias is not None:
            kw["bias"] = bias
        if scale is not None:
            kw["scale"] = scale
        if accum_out is not None:
            kw["accum_out"] = accum_out
        self.op("act", lambda e: e.activation(out=out, in_=in_, func=func, **kw), reads, writes)

    def memset(self, eng, ap, val, writes):
        self.op(eng, lambda e: e.memset(ap, val), (), writes)


def _blocks():
    return [(0, 9), (9, 9), (18, 8), (26, 8)]


class Ctx:
    pass


def build_program(layers, final_out=True, debug_taps=(), stop_after=None):
    nc = bass.Bass("TRN2", target_bir_lowering=False)
    K = Ctx()
    K.nc = nc

    def din(name, shape, dt=F32):
        return nc.dram_tensor(name, list(shape), dt, kind="ExternalInput").ap()

    def dscr(name, shape, dt=F32):
        return nc.dram_tensor(name, list(shape), dt, kind="Internal").ap()

    I = {}
    I["xin"] = din("xin", [T, D])
    I["cc"] = din("cc", [128, 16, 2])
    I["mod_w"] = din("mod_w", [DEPTH, D, 3 * D])
    I["mod_b"] = din("mod_b", [DEPTH, 3 * D])
    I["norm_pre"] = din("norm_pre", [DEPTH, D])
    I["norm_post"] = din("norm_post", [DEPTH, D])
    I["ev_w_in"] = din("ev_w_in", [2, D, EV_IN])
    I["ev_w_out"] = din("ev_w_out", [2, D, D])
    I["od_w_in"] = din("od_w_in", [2, D, 2 * DC])
    I["od_w_out"] = din("od_w_out", [2, DC, D])
    I["od_gw"] = din("od_gw", [2, 2, 2, 16, 160, 160])
    I["od_pc"] = din("od_pc", [2, 128, 11, NQ])
    I["ev_pc"] = din("ev_pc", [2, 128, EVPC_N])
    I["ev_wup"] = din("ev_wup", [2, 2, 128, DA])
    I["ev_tb"] = din("ev_tb", [2, 16, 128, 14, 64])
    I["c_ident"] = din("c_ident", [128, 128])
    I["c_identb"] = din("c_identb", [128, 128], BF16)
    I["c_masks"] = din("c_masks", [128, 5, 64])
    I["c_bones"] = din("c_bones", [128, 128])
    I["c_scanm"] = din("c_scanm", [128, 2, 256])
    out = nc.dram_tensor("out", [SEQ, D], F32, kind="ExternalOutput").ap()
    K.I = I
    K.out = out
    K.X = dscr("X", [T, D])
    K.modrow = dscr("modrow", [2, 3 * D])
    K.FT = dscr("FT", [5376, T])
    K.QK = dscr("QK", [2048, T], BF16)
    K.VB = dscr("VB", [T, 1024], BF16)
    K.YT = dscr("YT", [DC, T], BF16)
    K.HF = dscr("HF", [DC, T])
    K.OT = dscr("OT", [2, 2, DA, T])
    K.Wb_in = {}
    K.Wb_out = {}
    for l in layers:
        i = l // 2
        if l % 2 == 0:
            K.Wb_in[l] = dscr("wbin%d" % l, [EV_IN // 256, 128, 16, 256], BF16)
            K.Wb_out[l] = dscr("wbout%d" % l, [D, D], BF16)
        else:
            K.Wb_in[l] = dscr("wbin%d" % l, [2 * DC // 256, 128, 16, 256], BF16)
            K.Wb_out[l] = dscr("wbout%d" % l, [DC, D], BF16)
    taps = {}
    for name, shape in debug_taps:
        taps[name] = nc.dram_tensor("tap_" + name, list(shape), F32, kind="ExternalOutput").ap()
    K.taps = taps

    with contextlib.ExitStack() as st:
        S = Sched(nc, st)
        K.S = S
        for l in layers:
            i = l // 2
            win = I["ev_w_in"][i] if l % 2 == 0 else I["od_w_in"][i]
            wout = I["ev_w_out"][i] if l % 2 == 0 else I["od_w_out"][i]
            winv = win.rearrange("(k p) c -> p k c", p=128)
            for g in range(K.Wb_in[l].shape[0]):
                S.dma("pool", K.Wb_in[l][g], winv[:, :, g * 256:(g + 1) * 256], writes=[("wbin", l)])
            nrow = D if l % 2 == 0 else DC
            for r in range(0, nrow, 256):
                S.dma("pool", K.Wb_out[l][r:r + 256, :], wout[r:r + 256, :], writes=[("wbout", l)])
        for r in range(0, T, 544):
            S.dma("sp", K.X[r:r + 544, :], I["xin"][r:r + 544, :], writes=[("X", "all")])
        S.barrier()
        stopped = stop_after == "init"
        for li, l in enumerate(layers):
            if stopped:
                break
            last = (li == len(layers) - 1) and final_out
            import os
            skip = os.environ.get("SKIP_PH", "").split(",")
            if "mod" not in skip:
                phase_mod(K, l)
                S.barrier()
            if stop_after == "mod":
                stopped = True
                break
            if "proj" not in skip:
                phase_proj(K, l)
                S.barrier()
            if stop_after == "proj":
                stopped = True
                break
            if l % 2 == 0:
                phase_rwkv(K, l)
                S.barrier()
                phase_na(K, l)
                S.barrier()
            else:
                phase_rglru(K, l)
                S.barrier()
            if stop_after == "mixer":
                stopped = True
                break
            phase_out(K, l, last)
            S.barrier()
        if stopped:
            final_out = False
        if not final_out:
            for r in range(0, SEQ, 512):
                S.dma("sp", out[r:r + 512, :], K.X[CTX + r:CTX + r + 512, :])
        S.barrier()
        print("instructions (incl waits):", S.ninst, {e: len(S.stream[e]) for e in ENGS})
        S.emit()
    return nc


def phase_mod(K, l):
    S, I = K.S, K.I
    with contextlib.ExitStack() as st:
        cc = S.sb(st, "cc", [128, 16, 2])
        sc = S.sb(st, "sc", [128, 16, 2])
        S.dma("sp", cc[:], I["cc"], writes=["cc"])
        S.act(sc[:], cc[:], AF.Sigmoid, ["cc"], ["sc"])
        S.tt("dve", sc[:], sc[:], cc[:], ALU.mult, ["sc", "cc"], ["sc"])
        mw = [S.sb(st, "mw%d" % i, [128, 16, 512]) for i in range(2)]
        pm = [S.ps(st, "pm%d" % i, [2, 512]) for i in range(2)]
        msb = S.sb(st, "msb", [2, 3 * D])
        mb = S.sb(st, "mb", [2, 3 * D])
        S.dma("act", mb[:], I["mod_b"][l:l + 1, :].partition_broadcast(2), writes=["mb"])
        src = I["mod_w"][l].rearrange("(k p) c -> p k c", p=128)
        for g in range(12):
            b = g % 2
            for h in range(2):
                S.dma("sp" if h == 0 else "act", mw[b][:, 8 * h:8 * h + 8, :], src[:, 8 * h:8 * h + 8, g * 512:(g + 1) * 512],
                      writes=[("mw", b, h)])
            for k in range(16):
                S.mm(pm[b][:], sc[:, k, :], mw[b][:, k, :], k == 0, k == 15,
                     ["sc", ("mw", b, k // 8)], [("pm", b)])
            S.tt("dve", msb[:, g * 512:(g + 1) * 512], pm[b][:], mb[:, g * 512:(g + 1) * 512], ALU.add,
                 [("pm", b), "mb"], ["msb"])
        S.dma("sp", K.modrow, msb[:], reads=["msb"], writes=["modrow"])


def load_bcast_rows(K, st, l, which):
    S, I = K.S, K.I
    res = {}
    tmp = S.sb(st, "bt_n", [128, D])
    if which == "pre":
        S.dma("act", tmp[:], I["norm_pre"][l:l + 1, :].partition_broadcast(128), writes=["bt_n"])
        for s, nm in ((0, "x"), (1, "c")):
            G = S.sb(st, "G" + nm, [128, D])
            Sh = S.sb(st, "S" + nm, [128, D])
            S.dma("sp", G[:], K.modrow[s:s + 1, D:2 * D].partition_broadcast(128), reads=["modrow"], writes=["G" + nm])
            S.dma("act", Sh[:], K.modrow[s:s + 1, 0:D].partition_broadcast(128), reads=["modrow"], writes=["S" + nm])
            S.stt("dve", G[:], G[:], 1.0, tmp[:], ALU.add, ALU.mult, ["G" + nm, "bt_n"], ["G" + nm])
            res[nm] = (G, Sh)
    else:
        S.dma("act", tmp[:], I["norm_post"][l:l + 1, :].partition_broadcast(128), writes=["bt_n"])
        for s, nm in ((0, "x"), (1, "c")):
            G = S.sb(st, "GP" + nm, [128, D])
            S.dma("sp", G[:], K.modrow[s:s + 1, 2 * D:3 * D].partition_broadcast(128), reads=["modrow"], writes=["GP" + nm])
            S.tt("dve", G[:], G[:], tmp[:], ALU.mult, ["GP" + nm, "bt_n"], ["GP" + nm])
            res[nm] = G
    return res


def proj_spec(l):
    if l % 2 == 0:
        return [(0, 4352, "fm32", 0), (4352, 6400, "fmbf", 0), (6400, 7424, "tm", 0), (7424, 8448, "fm32", 4352)]
    return [(0, 2 * DC, "fm32", 0)]


def phase_proj(K, l):
    S, I = K.S, K.I
    spec = proj_spec(l)
    Wb = K.Wb_in[l]
    with contextlib.ExitStack() as st:
        rows = load_bcast_rows(K, st, l, "pre")
        identb = S.sb(st, "identb", [128, 128], BF16)
        S.dma("sp", identb[:], I["c_identb"], writes=["identb"])
        xt = [S.sb(st, "xt%d" % i, [128, D]) for i in range(2)]
        junk = S.sb(st, "junk", [128, D], BF16)
        ss = [S.sb(st, "ss%d" % i, [128, 1]) for i in range(2)]
        hn = S.sb(st, "hn", [128, D])
        hb = [S.sb(st, "hb%d" % i, [128, D], BF16) for i in range(2)]
        hT2 = [S.sb(st, "hT%d" % i, [128, 16, 9 * 128], BF16) for i in range(2)]
        ptr = [S.ps(st, "ptr%d" % i, [128, 1024], BF16) for i in range(2)]
        pp = [S.ps(st, "pp%d" % i, [128, 512]) for i in range(4)]
        wt = [S.sb(st, "wt%d" % i, [128, 16, 256], BF16) for i in range(3)]
        stg = [S.sb(st, "stg%d" % i, [128, 9 * 128]) for i in range(3)]
        stgb = [S.sb(st, "stgb%d" % i, [128, 9 * 128], BF16) for i in range(2)]
        stgt = [S.sb(st, "stgt%d" % i, [128, 256], BF16) for i in range(3)]
        cnt = {"x": 0, "pp": 0, "w": 0, "stg": 0, "stgb": 0, "stgt": 0, "ev": 0}

        def a_tile(tile_i, hbuf, ti):
            hT = hT2[hbuf]
            b = cnt["x"] % 2
            cnt["x"] += 1
            G, Sh = rows["c"] if tile_i < 2 else rows["x"]
            gn = "Gc" if tile_i < 2 else "Gx"
            sn = "Sc" if tile_i < 2 else "Sx"
            S.dma("sp", xt[b][:], K.X[tile_i * 128:(tile_i + 1) * 128, :], reads=[("X", tile_i), ("X", "all")],
                  writes=[("xt", b)])
            S.act(junk[:], xt[b][:], AF.Square, [("xt", b)], ["junk", ("ss", b)], accum_out=ss[b][:])
            S.act(ss[b][:], ss[b][:], AF.Ln, [("ss", b)], [("ss", b)], scale=1.0 / D, bias=EPS_RMS)
            S.act(ss[b][:], ss[b][:], AF.Exp, [("ss", b)], [("ss", b)], scale=-0.5)
            S.stt("dve", hn[:], xt[b][:], ss[b][:, 0:1], G[:], ALU.mult, ALU.mult, [("xt", b), ("ss", b), gn], ["hn"])
            S.tt("pool", hb[b][:], hn[:], Sh[:], ALU.add, ["hn", sn], [("hb", b)])
            for half in range(2):
                for k in range(8):
                    kk = half * 8 + k
                    S.tr(ptr[half][:, k * 128:(k + 1) * 128], hb[b][:, kk * 128:(kk + 1) * 128], identb[:],
                         [("hb", b), "identb"], [("ptr", half)], signal=(k == 7))
                S.cp("act" if half == 0 else "dve", hT[:, half * 8:half * 8 + 8, ti * 128:(ti + 1) * 128],
                     ptr[half][:].rearrange("p (k t) -> p k t", k=8), [("ptr", half)], [("hT", hbuf, ti)])

        def b_group(t0, nt, hbuf, c0, kind, drow, g0):
            hT = hT2[hbuf]
            ntok = nt * 128
            wb = cnt["w"] % 3
            cnt["w"] += 1
            S.dma("sp" if cnt["w"] % 2 == 0 else "act", wt[wb][:], Wb[g0 // 256], reads=[("wbin", l)], writes=[("wt", wb)])
            if kind in ("fm32", "fmbf"):
                for sub in range(2):
                    if kind == "fm32":
                        sg = stg[cnt["stg"] % 3]
                        sgn = ("stg", cnt["stg"] % 3)
                        cnt["stg"] += 1
                    else:
                        sg = stgb[cnt["stgb"] % 2]
                        sgn = ("stgb", cnt["stgb"] % 2)
                        cnt["stgb"] += 1
                    for ts0 in range(0, ntok, 512):
                        tsn = min(512, ntok - ts0)
                        pb = cnt["pp"] % 4
                        cnt["pp"] += 1
                        rtok = [("hT", hbuf, ti) for ti in range(ts0 // 128, (ts0 + tsn) // 128)]
                        for k in range(16):
                            S.mm(pp[pb][:, 0:tsn], wt[wb][:, k, sub * 128:(sub + 1) * 128], hT[:, k, ts0:ts0 + tsn],
                                 k == 0, k == 15, [("wt", wb)] + rtok, [("pp", pb)])
                        cnt["ev"] += 1
                        S.cp("act" if cnt["ev"] % 2 == 0 else "dve", sg[:, ts0:ts0 + tsn], pp[pb][:, 0:tsn], [("pp", pb)], [sgn])
                    r0 = drow + (g0 - c0) + sub * 128
                    dst = K.FT if kind == "fm32" else K.QK
                    S.dma("pool" if kind == "fm32" else "sp", dst[r0:r0 + 128, t0 * 128:t0 * 128 + ntok], sg[:, 0:ntok], reads=[sgn],
                          writes=[("FT" if kind == "fm32" else "QK", r0 // 128)])
            else:
                for ti in range(nt):
                    pb = cnt["pp"] % 4
                    cnt["pp"] += 1
                    for k in range(16):
                        S.mm(pp[pb][:, 0:256], hT[:, k, ti * 128:(ti + 1) * 128], wt[wb][:, k, :],
                             k == 0, k == 15, [("wt", wb), ("hT", hbuf, ti)], [("pp", pb)])
                    sb_ = cnt["stgt"] % 3
                    cnt["stgt"] += 1
                    cnt["ev"] += 1
                    S.cp("act" if cnt["ev"] % 2 == 0 else "dve", stgt[sb_][:], pp[pb][:, 0:256], [("pp", pb)], [("stgt", sb_)])
                    cc0 = g0 - c0
                    S.dma("pool", K.VB[(t0 + ti) * 128:(t0 + ti + 1) * 128, cc0:cc0 + 256], stgt[sb_][:],
                          reads=[("stgt", sb_)], writes=[("VB", t0 + ti)])

        blocks = _blocks()
        groups = [(c0, kind, drow, g0) for (c0, c1, kind, drow) in spec for g0 in range(c0, c1, 256)]
        t0, nt = blocks[0]
        for ti in range(nt):
            a_tile(t0 + ti, 0, ti)
        for bi, (t0, nt) in enumerate(blocks):
            hbuf = bi % 2
            nxt = blocks[bi + 1] if bi + 1 < len(blocks) else None
            pend = list(range(nxt[1])) if nxt else []
            every = max(1, (len(groups) - 2) // max(1, len(pend))) if pend else 0
            for gi, (c0, kind, drow, g0) in enumerate(groups):
                b_group(t0, nt, hbuf, c0, kind, drow, g0)
                if pend and (gi + 1) % every == 0:
                    ti = pend.pop(0)
                    a_tile(nxt[0] + ti, 1 - hbuf, ti)
            while pend:
                ti = pend.pop(0)
                a_tile(nxt[0] + ti, 1 - hbuf, ti)


def phase_out(K, l, last):
    S, I = K.S, K.I
    nf = D if l % 2 == 0 else DC
    nk = nf // 128
    Wb = K.Wb_out[l].rearrange("(k p) c -> p k c", p=128)
    YTv = K.YT[0:nf, :].rearrange("(k p) t -> p k t", p=128)
    with contextlib.ExitStack() as st:
        rows = load_bcast_rows(K, st, l, "post")
        wo = S.sb(st, "wo", [128, nk, D], BF16)
        for k in range(0, nk, 4):
            S.dma("sp" if (k // 4) % 2 == 0 else "act", wo[:, k:k + 4, :], Wb[:, k:k + 4, :], reads=[("wbout", l)], writes=["wo"])
        yt = [S.sb(st, "yt%d" % i, [128, nk, 128], BF16) for i in range(2)]
        xt = [S.sb(st, "xo%d" % i, [128, D]) for i in range(2)]
        pz = [S.ps(st, "pz%d" % i, [128, D]) for i in range(2)]
        junk = S.sb(st, "junko", [128, D], BF16)
        ssq = [S.sb(st, "sso%d" % i, [128, 1]) for i in range(2)]
        zz = S.sb(st, "zz", [128, D])
        xn = [S.sb(st, "xn%d" % i, [128, D]) for i in range(2)]
        tiles = list(range(NTILE))
        if last:
            tiles = list(range(2, NTILE))
        for n, ti in enumerate(tiles):
            b = n % 2
            GP = rows["c"] if ti < 2 else rows["x"]
            gpn = "GPc" if ti < 2 else "GPx"
            S.dma("sp", yt[b][:], YTv[:, :, ti * 128:(ti + 1) * 128], reads=[("YT", ti), ("YT", "all")], writes=[("yt", b)])
            S.dma("act", xt[b][:], K.X[ti * 128:(ti + 1) * 128, :], reads=[("X", ti), ("X", "all")], writes=[("xo", b)])
            for nn in range(4):
                for k in range(nk):
                    S.mm(pz[b][:, nn * 512:(nn + 1) * 512], yt[b][:, k, :], wo[:, k, nn * 512:(nn + 1) * 512],
                         k == 0, k == nk - 1, [("yt", b), "wo"], [("pz", b, nn)])
            pzr = [("pz", b, nn) for nn in range(4)]
            S.act(junk[:], pz[b][:], AF.Square, pzr, ["junko", ("sso", b)], accum_out=ssq[b][:])
            S.act(ssq[b][:], ssq[b][:], AF.Ln, [("sso", b)], [("sso", b)], scale=1.0 / D, bias=EPS_RMS)
            S.act(ssq[b][:], ssq[b][:], AF.Exp, [("sso", b)], [("sso", b)], scale=-0.5)
            S.stt("dve", zz[:], pz[b][:], ssq[b][:, 0:1], GP[:], ALU.mult, ALU.mult, pzr + [("sso", b), gpn], ["zz"])
            S.tt("pool", xn[b][:], zz[:], xt[b][:], ALU.add, ["zz", ("xo", b)], [("xn", b)])
            if last:
                S.dma("sp", K.out[(ti - 2) * 128:(ti - 1) * 128, :], xn[b][:], reads=[("xn", b)])
            else:
                S.dma("pool", K.X[ti * 128:(ti + 1) * 128, :], xn[b][:], reads=[("xn", b)], writes=[("X", ti)])


RG_TB = 128


def rg_rects():
    res = []
    for h in range(16):
        lo, hi = 160 * h, 160 * h + 160
        for q in range(lo // 128, (hi - 1) // 128 + 1):
            r0, r1 = max(lo, 128 * q), min(hi, 128 * q + 128)
            for q2 in range(lo // 128, (hi - 1) // 128 + 1):
                c0, c1 = max(lo, 128 * q2), min(hi, 128 * q2 + 128)
                res.append((h, q, r0, r1, q2, c0, c1))
    return res


def phase_rglru(K, l):
    S, I = K.S, K.I
    i = l // 2
    NB = T // RG_TB
    NCTX = CTX // RG_TB
    TB = RG_TB
    with contextlib.ExitStack() as st:
        pc = S.sb(st, "pc", [128, 11, NQ])
        S.dma("sp", pc[:], I["od_pc"][i], writes=["pc"])
        spl = S.sb(st, "spl", [128, 2, NQ])
        S.act(spl[:], pc[:, 9:11, :], AF.Exp, ["pc"], ["spl"], scale=-1.0)
        S.act(spl[:], spl[:], AF.Ln, ["spl"], ["spl"], bias=1.0)
        S.ts("dve", spl[:], spl[:], -8.0, None, ALU.mult, None, ["spl"], ["spl"])
        wz = {}
        for g in range(2):
            wz[g] = S.sb(st, "wz%d" % g, [128, NQ, 384], BF16)
        HALO = 4
        xr = [S.sb(st, "xr%d" % b, [128, NQ, TB + HALO]) for b in range(2)]
        gg = S.sb(st, "gg", [128, NQ, TB])
        hf = S.sb(st, "hf", [128, NQ, TB])
        u_ = [S.sb(st, "u%d" % q, [128, NQ, TB]) for q in range(2)]
        tmp_ = [S.sb(st, "rtmp%d" % q, [128, NQ, TB]) for q in range(2)]
        ub_ = [S.sb(st, "ub%d" % q, [128, NQ, TB], BF16) for q in range(2)]
        ga_1 = S.sb(st, "ga", [128, NQ, TB])
        ga_ = [ga_1, ga_1]
        gx_ = [S.sb(st, "gx%d" % q, [128, NQ, TB]) for q in range(2)]
        aa_ = [S.sb(st, "aa%d" % q, [128, NQ, TB]) for q in range(2)]
        bb_ = [S.sb(st, "bb%d" % q, [128, NQ, TB]) for q in range(2)]
        hh = S.sb(st, "hh", [128, NQ, TB])
        yb = S.sb(st, "yb", [128, NQ, TB], BF16)
        state = S.sb(st, "state", [128, NQ])
        pg = [S.ps(st, "pg%d" % b, [128, 2, 256]) for b in range(4)]
        FTx = K.FT[0:DC, :].rearrange("(q p) t -> p q t", p=128)
        FTg = K.FT[DC:2 * DC, :].rearrange("(q p) t -> p q t", p=128)
        HFv = K.HF.rearrange("(q p) t -> p q t", p=128)
        YTv = K.YT.rearrange("(q p) t -> p q t", p=128)

        def bc(col):
            return col.unsqueeze(2).to_broadcast([128, NQ, TB])
        n_pg = 0
        nblk = 0
        for d in range(2):
            S.barrier()
            for g in range(2):
                S.memset("pool", wz[g][:], 0.0, [("wz", g)])
                for n, (h, q, r0, r1, q2, c0, c1) in enumerate(rg_rects()):
                    S.dma("pool",
                          wz[g][r0 - 128 * q:r1 - 128 * q, q, (q2 - q + 1) * 128 + (c0 - 128 * q2):(q2 - q + 1) * 128 + (c1 - 128 * q2)],
                          I["od_gw"][i, g, d, h, r0 - 160 * h:r1 - 160 * h, c0 - 160 * h:c1 - 160 * h],
                          reads=[], writes=[("wz", g)])
            S.memset("dve", state[:], 0.0, ["state"])
            if d == 0:
                order = list(range(NB))
            else:
                order = list(range(NCTX - 1, -1, -1)) + list(range(NB - 1, NCTX - 1, -1))
            import os
            blks = order[:int(os.environ.get("RG_MAXB", "1000"))]

            def front(bi, b):
                nonlocal n_pg
                t0 = bi * TB
                u, ub, ga, gx, aa, bb = u_[b], ub_[b], ga_[b], gx_[b], aa_[b], bb_[b]
                seg0, seg1 = (0, CTX) if t0 < CTX else (CTX, T)
                lo = max(seg0, t0 - 2)
                hi = min(seg1, t0 + TB + 1)
                if lo > t0 - 2:
                    S.memset("pool", xr[b][:, :, 0:2], 0.0, [("xr", b)])
                if hi < t0 + TB + 1:
                    S.memset("pool", xr[b][:, :, TB + 2:TB + 3], 0.0, [("xr", b)])
                for qh in range(2):
                    qs = slice(qh * 10, qh * 10 + 10)
                    S.dma("sp", xr[b][:, qs, lo - (t0 - 2):hi - (t0 - 2)], FTx[:, qs, lo:hi],
                          reads=[("FT", "all")], writes=[("xr", b)])
                S.tt("dve", u[:], xr[b][:, :, 0:TB], bc(pc[:, 0, :]), ALU.mult, [("xr", b), "pc"], [("u", b)])
                for j in range(1, 4):
                    tmp = tmp_[j % 2]
                    S.tt("pool", tmp[:], xr[b][:, :, j:j + TB], bc(pc[:, j, :]), ALU.mult, [("xr", b), "pc"], [("rtmp", j % 2)])
                    S.tt("dve", u[:], u[:], tmp[:], ALU.add, [("u", b), ("rtmp", j % 2)], [("u", b)])
                S.tt("dve", u[:], u[:], bc(pc[:, 4, :]), ALU.add, [("u", b), "pc"], [("u", b)])
                S.cp("act", ub[:], u[:], [("u", b)], [("ub", b)])
                for q2 in range(NQ):
                    pb = n_pg % 4
                    n_pg += 1
                    qs_ = [q for q in (q2 - 1, q2, q2 + 1) if 0 <= q < NQ]
                    for g in range(2):
                        for n, q in enumerate(qs_):
                            slot = q2 - q + 1
                            S.mm(pg[pb][:, g, 0:TB], wz[g][:, q, slot * 128:(slot + 1) * 128], ub[:, q, :],
                                 n == 0, n == len(qs_) - 1, [("wz", g), ("ub", b)], [("pg", pb)])
                    S.act(ga[:, q2, :], pg[pb][:, 0, 0:TB], AF.Sigmoid, [("pg", pb), "pc"], [("ga", q2)], bias=pc[:, 5 + d, q2:q2 + 1])
                    S.act(gx[:, q2, :], pg[pb][:, 1, 0:TB], AF.Sigmoid, [("pg", pb), "pc"], [("gx", b, q2)], bias=pc[:, 7 + d, q2:q2 + 1])

            def front_b(bi, b):
                u, ub, ga, gx, aa, bb = u_[b], ub_[b], ga_[b], gx_[b], aa_[b], bb_[b]
                gaall = [("ga", q) for q in range(NQ)]
                gxall = [("gx", b, q) for q in range(NQ)]
                S.tt("pool", aa[:], ga[:], bc(spl[:, d, :]), ALU.mult, gaall + ["spl"], [("aa", b)])
                S.act(aa[:], aa[:], AF.Exp, [("aa", b)], [("aa", b)])
                S.tt("pool", bb[:], aa[:], aa[:], ALU.mult, [("aa", b)], [("bb", b)])
                S.act(bb[:], bb[:], AF.Ln, [("bb", b)], [("bb", b)], scale=-1.0, bias=1.0)
                S.act(bb[:], bb[:], AF.Exp, [("bb", b)], [("bb", b)], scale=0.5)
                S.tt("dve", gx[:], gx[:], u[:], ALU.mult, gxall + [("u", b)], gxall)
                S.tt("pool", bb[:], bb[:], gx[:], ALU.mult, [("bb", b)] + gxall, [("bb", b)])

            def back(bi, b):
                t0 = bi * TB
                aa, bb = aa_[b], bb_[b]
                if d == 1:
                    for qh in range(2):
                        qs = slice(qh * 10, qh * 10 + 10)
                        S.dma("sp", gg[:, qs, :], FTg[:, qs, t0:t0 + TB], reads=[("FT", "all")], writes=["gg"])
                        S.dma("sp", hf[:, qs, :], HFv[:, qs, t0:t0 + TB], reads=[("HF", bi)], writes=["hf"])
                for q in range(NQ):
                    if d == 0:
                        S.op("dve", (lambda e, q=q, aa=aa, bb=bb: e.tensor_tensor_scan(out=hh[:, q, :], data0=aa[:, q, :], data1=bb[:, q, :],
                                                                           initial=state[:, q:q + 1], op0=ALU.mult, op1=ALU.add)),
                             [("aa", b), ("bb", b), "state"], [("hh", q)])
                    else:
                        S.op("dve", (lambda e, q=q, aa=aa, bb=bb: e.tensor_tensor_scan(out=hh[:, q, ::-1], data0=aa[:, q, ::-1], data1=bb[:, q, ::-1],
                                                                           initial=state[:, q:q + 1], op0=ALU.mult, op1=ALU.add)),
                             [("aa", b), ("bb", b), "state"], [("hh", q)])
                hhall = [("hh", q) for q in range(NQ)]
                if d == 0:
                    S.cp("pool", state[:], hh[:, :, TB - 1], hhall + ["state"], ["state"])
                    for qh in range(2):
                        qs = slice(qh * 10, qh * 10 + 10)
                        S.dma("sp", HFv[:, qs, t0:t0 + TB], hh[:, qs, :], reads=hhall, writes=[("HF", bi)])
                else:
                    S.cp("pool", state[:], hh[:, :, 0], hhall + ["state"], ["state"])
                    S.tt("pool", hh[:], hh[:], hf[:], ALU.add, hhall + ["hf"], hhall)
                    S.act(hf[:], gg[:], AF.Sigmoid, ["gg"], ["hf"])
                    S.tt("dve", gg[:], gg[:], hf[:], ALU.mult, ["gg", "hf"], ["gg"])
                    S.tt("dve", yb[:], hh[:], gg[:], ALU.mult, hhall + ["gg"], ["yb"])
                    for qh in range(2):
                        qs = slice(qh * 10, qh * 10 + 10)
                        S.dma("sp", YTv[:, qs, t0:t0 + TB], yb[:, qs, :], reads=["yb"], writes=[("YT", "all")])

            for n, bi in enumerate(blks):
                front(bi, n % 2)
                if n >= 1:
                    back(blks[n - 1], (n - 1) % 2)
                front_b(bi, n % 2)
            if blks:
                back(blks[-1], (len(blks) - 1) % 2)


EVPC_N = 124


def phase_rwkv(K, l):
    S, I = K.S, K.I
    i = l // 2
    import os
    NB = 256
    NCH = NB // 64
    NP = 8
    maxblk = int(os.environ.get("RW_MAXB", "1000"))
    FT3 = K.FT[0:3072, :].rearrange("(part q p) t -> q p part t", part=3, q=8, p=128)
    FTc = K.FT[3072:3328, :].rearrange("(g p) t -> p g t", p=128)
    with contextlib.ExitStack() as st:
        pc = S.sb(st, "evpc", [128, EVPC_N])
        S.dma("sp", pc[:], I["ev_pc"][i], writes=["evpc"])
        mc = S.sb(st, "mc", [128, 26])
        for part in range(3):
            S.tt("dve", mc[:, part * 8:part * 8 + 8], pc[:, part * 16:part * 16 + 8], pc[:, part * 16 + 8:part * 16 + 16], ALU.add,
                 ["evpc"], ["mc"])
        S.tt("dve", mc[:, 24:25], pc[:, 48:49], pc[:, 49:50], ALU.add, ["evpc"], ["mc"])
        S.tt("dve", mc[:, 25:26], pc[:, 50:51], pc[:, 51:52], ALU.add, ["evpc"], ["mc"])
        S.ts("dve", mc[:], mc[:], -1.0, 1.0, ALU.mult, ALU.add, ["mc"], ["mc"])
        wup = S.sb(st, "wup", [128, 2, DA])
        S.dma("act", wup[:], I["ev_wup"][i].rearrange("g p c -> p g c"), writes=["wup"])
        masks = S.sb(st, "masks", [128, 5, 64])
        S.dma("sp", masks[:], I["c_masks"], writes=["masks"])
        ident = S.sb(st, "ident", [128, 128])
        S.dma("act", ident[:], I["c_ident"], writes=["ident"])
        bones = S.sb(st, "bones", [128, 128])
        S.dma("sp", bones[:], I["c_bones"], writes=["bones"])
        bones_s = S.sb(st, "bones_s", [128, 128])
        S.ts("dve", bones_s[:], bones[:], 1.0 / 64, None, ALU.mult, None, ["bones"], ["bones_s"])
        scanm = S.sb(st, "scanm", [128, 2, NB])
        S.dma("act", scanm[:], I["c_scanm"], writes=["scanm"])
        RT = S.sb(st, "RT", [128, NP, NB])
        KT = S.sb(st, "KT", [128, NP, NB])
        BT = S.sb(st, "BT", [128, NP, NB])
        AT = S.sb(st, "AT", [128, NP, NB])
        VT = S.sb(st, "VT", [128, NP, NB])
        KTt = S.sb(st, "KTt", [128, NCH, NP, 64])
        BTt = S.sb(st, "BTt", [128, NCH, NP, 64])
        VTt = S.sb(st, "VTt", [128, NCH, NP, 64])
        gCt = S.sb(st, "gCt", [128, NP, NCH])
        H = S.sb(st, "H", [128, NP, 64])
        XY = [S.sb(st, "XY%d" % m, [128, 2, 2, 64]) for m in range(NP)]
        Mt = [[S.sb(st, "M%d_%d" % (m, q), [128, 64]) for q in range(2)] for m in range(NP)]
        DE = [[S.sb(st, "DE%d_%d" % (m, q), [128, 3, 64]) for q in range(2)] for m in range(NP)]
        Wm = S.sb(st, "Wm", [128, NP, 64])
        U = S.sb(st, "U", [128, NP, 64])
        tmpH = S.sb(st, "tmpH", [128, NP, 64])
        Ost = S.sb(st, "Ost", [128, NP, NB])
        Ost2 = S.sb(st, "Ost2", [128, NP, NB])
        raw = [S.sb(st, "raw%d" % q, [128, 3, NB + 2]) for q in range(2)]
        craw = S.sb(st, "craw", [128, 2, NB + 2])
        cwT = S.sb(st, "cwT", [128, NB])
        caT = S.sb(st, "caT", [128, NB])
        tn = {}
        for nm in ("rs", "ks", "vs", "ld", "aa", "L", "eL", "eLn", "eLp", "kkr", "sq", "kk", "kd", "t1", "bvt", "t2"):
            tn[nm] = S.sb(st, "p_" + nm, [128, NB])
        psA = [S.ps(st, "psA%d" % q, [128, 512]) for q in range(2)]
        psL = [S.ps(st, "psL%d" % q, [128, 512]) for q in range(2)]
        psB = [S.ps(st, "psB%d" % q, [128, 512]) for q in range(2)]
        ppre = [S.ps(st, "ppre%d" % q, [128, 512]) for q in range(2)]
        I2 = S.sb(st, "I2", [128, 64])
        for h in range(2):
            S.cp("pool", I2[64 * h:64 * h + 64, :], ident[64 * h:64 * h + 64, 64 * h:64 * h + 64], ["ident"], ["I2"])

        def shift(eng, out, outn, src, srcn, c0, c1, cm):
            S.ts(eng, out, src[:, 1:NB + 1], cm, None, ALU.mult, None, [srcn, "evpc", "mc"], [outn])
            S.stt(eng, out, src[:, 0:NB], c0, out, ALU.mult, ALU.add, [srcn, "evpc", outn], [outn])
            S.stt(eng, out, src[:, 2:NB + 2], c1, out, ALU.mult, ALU.add, [srcn, "evpc", outn], [outn])

        nraw = 0
        n_pa = 0
        n_pl = 0
        n_pp = 0
        n_ev = 0
        for d in range(2):
            S.memset("pool", H[:], 0.0, ["H"])
            if d == 0:
                order = list(range(0, T, NB))
            else:
                order = [0] + list(range(T - NB, 0, -NB))
            for t0 in order[:maxblk]:
                lz = (t0 == 0 or t0 == CTX)
                rz = (t0 + NB == CTX or t0 + NB == T)
                lo = t0 if lz else t0 - 1
                hi = t0 + NB if rz else t0 + NB + 1
                c_lo = lo - (t0 - 1)
                c_hi = hi - (t0 - 1)
                if lz:
                    S.memset("pool", craw[:, :, 0:1], 0.0, ["craw"])
                if rz:
                    S.memset("pool", craw[:, :, NB + 1:NB + 2], 0.0, ["craw"])
                S.dma("sp", craw[:, :, c_lo:c_hi], FTc[:, :, lo:hi], writes=["craw"])
                shift("dve", cwT[:], "cwT", craw[:, 0, :], "craw", pc[:, 48:49], pc[:, 49:50], mc[:, 24:25])
                S.act(cwT[:], cwT[:], AF.Tanh, ["cwT"], ["cwT"])
                shift("pool", caT[:], "caT", craw[:, 1, :], "craw", pc[:, 50:51], pc[:, 51:52], mc[:, 25:26])
                for m in range(NP):
                    rb = nraw % 2
                    nraw += 1
                    rw = raw[rb]
                    rwn = ("raw", rb)
                    if lz:
                        S.memset("pool", rw[:, :, 0:1], 0.0, [rwn])
                    if rz:
                        S.memset("pool", rw[:, :, NB + 1:NB + 2], 0.0, [rwn])
                    S.dma("sp" if m % 2 == 0 else "act", rw[:, :, c_lo:c_hi], FT3[m][:, :, lo:hi], writes=[rwn])
                    rs, ks, vs = tn["rs"], tn["ks"], tn["vs"]
                    shift("dve", rs[:], "rs", rw[:, 0, :], rwn, pc[:, 0 + m:1 + m], pc[:, 8 + m:9 + m], mc[:, m:m + 1])
                    shift("pool", ks[:], "ks", rw[:, 1, :], rwn, pc[:, 16 + m:17 + m], pc[:, 24 + m:25 + m], mc[:, 8 + m:9 + m])
                    shift("dve", vs[:], "vs", rw[:, 2, :], rwn, pc[:, 32 + m:33 + m], pc[:, 40 + m:41 + m], mc[:, 16 + m:17 + m])
                    db = 64 * d
                    pa = ppre[n_pp % 2]
                    pan = ("ppre", n_pp % 2)
                    n_pp += 1
                    S.mm(pa[:, 0:NB], wup[db:db + 64, 0, 128 * m:128 * m + 128], cwT[db:db + 64, :], True, True, ["wup", "cwT"], [pan])
                    ld = tn["ld"]
                    S.act(ld[:], pa[:, 0:NB], AF.Sigmoid, [pan, "evpc"], ["ld"], bias=pc[:, 52 + 8 * d + m:53 + 8 * d + m])
                    S.ts("dve", ld[:], ld[:], -0.6065306597126334, None, ALU.mult, None, ["ld"], ["ld"])
                    pa2 = ppre[n_pp % 2]
                    pan2 = ("ppre", n_pp % 2)
                    n_pp += 1
                    S.mm(pa2[:, 0:NB], wup[db:db + 64, 1, 128 * m:128 * m + 128], caT[db:db + 64, :], True, True, ["wup", "caT"], [pan2])
                    aa = tn["aa"]
                    S.act(aa[:], pa2[:, 0:NB], AF.Sigmoid, [pan2, "evpc"], ["aa"], bias=pc[:, 68 + 8 * d + m:69 + 8 * d + m])
                    L, eL, eLn, eLp = tn["L"], tn["eL"], tn["eLn"], tn["eLp"]
                    if d == 0:
                        S.op("dve", (lambda e: e.tensor_tensor_scan(out=L[:], data0=scanm[:, 0, :], data1=ld[:], initial=0.0,
                                                                     op0=ALU.mult, op1=ALU.add)), ["scanm", "ld"], ["L"])
                    else:
                        S.op("dve", (lambda e: e.tensor_tensor_scan(out=L[:, ::-1], data0=scanm[:, 1, ::-1], data1=ld[:, ::-1], initial=0.0,
                                                                     op0=ALU.mult, op1=ALU.add)), ["scanm", "ld"], ["L"])
                    S.act(eL[:], L[:], AF.Exp, ["L"], ["eL"])
                    S.act(eLn[:], L[:], AF.Exp, ["L"], ["eLn"], scale=-1.0)
                    S.tt("pool", eLp[:], L[:], ld[:], ALU.subtract, ["L", "ld"], ["eLp"])
                    S.act(eLp[:], eLp[:], AF.Exp, ["eLp"], ["eLp"])
                    kkr, sq, kk, kd, t1, t2, bvt = tn["kkr"], tn["sq"], tn["kk"], tn["kd"], tn["t1"], tn["t2"], tn["bvt"]
                    S.ts("pool", kkr[:], ks[:], pc[:, 84 + m:85 + m], None, ALU.mult, None, ["ks", "evpc"], ["kkr"])
                    S.tt("pool", sq[:], kkr[:], kkr[:], ALU.mult, ["kkr"], ["sq"])
                    pa3 = ppre[n_pp % 2]
                    pan3 = ("ppre", n_pp % 2)
                    n_pp += 1
                    S.mm(pa3[:, 0:NB], bones[:], sq[:], True, True, ["bones", "sq"], [pan3])
                    S.ts("dve", sq[:], pa3[:, 0:NB], 1e-12, None, ALU.add, None, [pan3], ["sq"])
                    S.act(sq[:], sq[:], AF.Ln, ["sq"], ["sq"])
                    S.act(sq[:], sq[:], AF.Exp, ["sq"], ["sq"], scale=-0.5)
                    S.tt("dve", kk[:], kkr[:], sq[:], ALU.mult, ["kkr", "sq"], ["kk"])
                    S.ts("pool", t1[:], aa[:], -1.0, pc[:, 92 + m:93 + m], ALU.add, ALU.mult, ["aa", "evpc"], ["t1"])
                    S.stt("dve", kd[:], t1[:], 1.0, ks[:], ALU.add, ALU.mult, ["t1", "ks"], ["kd"])
                    S.stt("pool", t2[:], rs[:], pc[:, 100 + m:101 + m], kd[:], ALU.mult, ALU.mult, ["rs", "evpc", "kd"], ["t2"])
                    pa4 = ppre[n_pp % 2]
                    pan4 = ("ppre", n_pp % 2)
                    n_pp += 1
                    S.mm(pa4[:, 0:NB], bones[:], t2[:], True, True, ["bones", "t2"], [pan4])
                    S.tt("dve", bvt[:], pa4[:, 0:NB], vs[:], ALU.mult, [pan4, "vs"], ["bvt"])
                    S.dma("act", K.OT[d, 1, 128 * m:128 * m + 128, t0:t0 + NB], bvt[:], reads=["bvt"], writes=[("OTb", d, m)])
                    rv = (lambda ap: ap[:, ::-1]) if d == 1 else (lambda ap: ap)
                    S.tt("dve", rv(RT[:, m, :]), rs[:], eL[:], ALU.mult, ["rs", "eL"], [("RT", m)])
                    S.tt("pool", rv(KT[:, m, :]), kd[:], eLn[:], ALU.mult, ["kd", "eLn"], [("KT", m)])
                    S.tt("pool", t1[:], kk[:], aa[:], ALU.mult, ["kk", "aa"], ["t1"])
                    S.tt("dve", rv(BT[:, m, :]), t1[:], eLn[:], ALU.mult, ["t1", "eLn"], [("BT", m)])
                    S.stt("pool", rv(AT[:, m, :]), kk[:], -1.0, eLp[:], ALU.mult, ALU.mult, ["kk", "eLp"], [("AT", m)])
                    S.cp("act", rv(VT[:, m, :]), vs[:], ["vs"], [("VT", m)])
                    if d == 0:
                        S.cp("pool", gCt[:, m, :], eL[:, 63::64], ["eL"], ["gCt"])
                    else:
                        S.cp("pool", gCt[:, m, :], eL[:, NB - 64::-64], ["eL"], ["gCt"])
                for c in range(NCH):
                    cs = slice(64 * c, 64 * c + 64)
                    for (src, srcn, dst, dstn) in ((KT, "KT", KTt, "KTt"), (BT, "BT", BTt, "BTt"), (VT, "VT", VTt, "VTt")):
                        pt = ppre[n_pp % 2]
                        ptn = ("ppre", n_pp % 2)
                        n_pp += 1
                        for m in range(NP):
                            for h in range(2):
                                pb = 64 * h
                                S.mm(pt[pb:pb + 64, m * 64:(m + 1) * 64], src[pb:pb + 64, m, cs], ident[pb:pb + 64, pb:pb + 64],
                                     True, True, [(srcn, m), "ident"], [ptn], signal=(m == NP - 1 and h == 1), tp=(pb, pb))
                        n_ev += 1
                        S.cp("act" if n_ev % 2 == 0 else "dve", dst[:, c, :, :], pt[:].rearrange("p (m k) -> p m k", k=64),
                             [ptn], [(dstn, c)])

                def stage_a(c):
                    nonlocal n_pa, n_pl
                    cs = slice(64 * c, 64 * c + 64)
                    par = c % 2
                    for m in range(NP):
                        pa_ = psA[n_pa % 2]
                        pn = ("psA", n_pa % 2)
                        n_pa += 1
                        pav = pa_[:, 0:320].rearrange("p (s w) -> p s w", w=64)
                        for h in range(2):
                            pb = 64 * h
                            ops = ((BT, "BT", AT, "AT"), (AT, "AT", BT, "BT"), (KT, "KT", AT, "AT"), (KT, "KT", RT, "RT"), (BT, "BT", RT, "RT"))
                            for sl, (la, lan, ra, ran) in enumerate(ops):
                                S.mm(pav[pb:pb + 64, sl, :], la[pb:pb + 64, m, cs], ra[pb:pb + 64, m, cs], True, True,
                                     [(lan, m), (ran, m)], [pn], signal=(h == 1 and sl == 4), tp=(pb, pb))
                        S.tt("dve", XY[m][:, 0, :, :], pav[:, 0:2, :], masks[:, 0:2, :], ALU.mult, [pn, "masks"], [("XY", m, 0)])
                        S.tt("dve", DE[m][par][:], pav[:, 2:5, :], masks[:, 2:5, :], ALU.mult, [pn, "masks"], [("DE", m, par)])
                        S.tt("pool", Mt[m][par][:], XY[m][:, 0, 0, :], I2[:], ALU.add, [("XY", m, 0), "I2"], [("M", m, par)])
                    for j in range(5):
                        cur, nxt = j % 2, 1 - (j % 2)
                        for m in range(NP):
                            pl = psL[n_pl % 2]
                            pln = ("psL", n_pl % 2)
                            n_pl += 1
                            plv = pl[:, 0:192].rearrange("p (s w) -> p s w", w=64)
                            for h in range(2):
                                pb = 64 * h
                                X_, Y_ = XY[m][pb:pb + 64, cur, 0, :], XY[m][pb:pb + 64, cur, 1, :]
                                if j < 4:
                                    S.mm(plv[pb:pb + 64, 0, :], Y_, X_, True, True, [("XY", m, cur)], [pln], signal=False, tp=(pb, pb))
                                S.mm(plv[pb:pb + 64, 1, :], X_, Y_, True, True, [("XY", m, cur)], [pln], signal=(h == 1), tp=(pb, pb))
                            if j < 4:
                                S.cp("act", XY[m][:, nxt, :, :], plv[:, 0:2, :], [pln], [("XY", m, nxt)])
                            else:
                                S.cp("act", XY[m][:, nxt, 1, :], plv[:, 1, :], [pln], [("XY", m, nxt)])
                            for h in range(2):
                                pb = 64 * h
                                S.mm(plv[pb:pb + 64, 2, :], XY[m][pb:pb + 64, nxt, 1, :], Mt[m][par][pb:pb + 64, :], True, True,
                                     [("XY", m, nxt), ("M", m, par)], [pln], signal=(h == 1), tp=(pb, pb))
                            S.tt("dve", Mt[m][par][:], plv[:, 2, :], Mt[m][par][:], ALU.add, [pln, ("M", m, par)], [("M", m, par)])

                def stage_b(c):
                    cs = slice(64 * c, 64 * c + 64)
                    par = c % 2
                    pw = psB[0][:].rearrange("p (m k) -> p m k", k=64)
                    pu = psB[1][:].rearrange("p (m k) -> p m k", k=64)
                    for m in range(NP):
                        for h in range(2):
                            pb = 64 * h
                            S.mm(pw[pb:pb + 64, m, :], AT[pb:pb + 64, m, cs], H[pb:pb + 64, m, :], True, False,
                                 [("AT", m), "H"], [("psB", 0)], signal=False, tp=(pb, pb))
                            S.mm(pw[pb:pb + 64, m, :], DE[m][par][pb:pb + 64, 0, :], VTt[pb:pb + 64, c, m, :], False, True,
                                 [("DE", m, par), ("VTt", c)], [("psB", 0)], signal=(m == NP - 1 and h == 1), tp=(pb, pb))
                    S.cp("act", Wm[:], pw, [("psB", 0)], ["Wm"])
                    for m in range(NP):
                        for h in range(2):
                            pb = 64 * h
                            S.mm(pu[pb:pb + 64, m, :], Mt[m][par][pb:pb + 64, :], Wm[pb:pb + 64, m, :], True, True,
                                 [("M", m, par), "Wm"], [("psB", 1)], signal=(m == NP - 1 and h == 1), tp=(pb, pb))
                    S.cp("act", U[:], pu, [("psB", 1)], ["U"])
                    for m in range(NP):
                        for h in range(2):
                            pb = 64 * h
                            S.mm(pw[pb:pb + 64, m, :], H[pb:pb + 64, m, :], RT[pb:pb + 64, m, cs], True, False,
                                 ["H", ("RT", m)], [("psB", 0)], signal=False, tp=(pb, pb))
                            S.mm(pw[pb:pb + 64, m, :], U[pb:pb + 64, m, :], DE[m][par][pb:pb + 64, 2, :], False, False,
                                 ["U", ("DE", m, par)], [("psB", 0)], signal=False, tp=(pb, pb))
                            S.mm(pw[pb:pb + 64, m, :], VTt[pb:pb + 64, c, m, :], DE[m][par][pb:pb + 64, 1, :], False, True,
                                 [("VTt", c), ("DE", m, par)], [("psB", 0)], signal=(m == NP - 1 and h == 1), tp=(pb, pb))
                    S.cp("act", Ost[:, :, cs], pw, [("psB", 0)], ["Ost"])
                    for m in range(NP):
                        for h in range(2):
                            pb = 64 * h
                            S.mm(pu[pb:pb + 64, m, :], BTt[pb:pb + 64, c, m, :], U[pb:pb + 64, m, :], True, False,
                                 [("BTt", c), "U"], [("psB", 1)], signal=False, tp=(pb, pb))
                            S.mm(pu[pb:pb + 64, m, :], KTt[pb:pb + 64, c, m, :], VTt[pb:pb + 64, c, m, :], False, True,
                                 [("KTt", c), ("VTt", c)], [("psB", 1)], signal=(m == NP - 1 and h == 1), tp=(pb, pb))
                    S.tt("dve", tmpH[:], pu, H[:], ALU.add, [("psB", 1), "H"], ["tmpH"])
                    S.tt("pool", H[:], tmpH[:], gCt[:, :, c:c + 1].to_broadcast([128, NP, 64]), ALU.mult, ["tmpH", "gCt"], ["H"])

                for c in range(NCH):
                    stage_a(c)
                    if c >= 1:
                        stage_b(c - 1)
                stage_b(NCH - 1)
                OTv = K.OT[d, 0].rearrange("(m p) t -> p m t", p=128)
                if d == 0:
                    S.dma("sp", OTv[:, :, t0:t0 + NB], Ost[:], reads=["Ost"], writes=[("OTo", d)])
                else:
                    S.cp("pool", Ost2[:, :, ::-1], Ost[:], ["Ost"], ["Ost2"])
                    S.dma("sp", OTv[:, :, t0:t0 + NB], Ost2[:], reads=["Ost2"], writes=[("OTo", d)])
        S.barrier()
        ro = {}
        for nm in ("o0", "o1", "b0", "b1", "gat", "cen", "sq2", "sg"):
            ro[nm] = [S.sb(st, "ro_%s%d" % (nm, q), [128, NB]) for q in range(2)]
        y16 = [S.sb(st, "ro_y%d" % q, [128, NB], BF16) for q in range(2)]
        nro = 0
        for m in range(NP):
            for t0 in range(0, T, NB):
                q = nro % 2
                nro += 1
                o0, o1, b0, b1, gat, cen, sq2, sg = (ro[nm][q] for nm in ("o0", "o1", "b0", "b1", "gat", "cen", "sq2", "sg"))
                tk = lambda nm: ("ro", nm, q)
                rsl = slice(128 * m, 128 * m + 128)
                S.dma("sp", o0[:], K.OT[0, 0, rsl, t0:t0 + NB], writes=[tk("o0")])
                S.dma("act", o1[:], K.OT[1, 0, rsl, t0:t0 + NB], writes=[tk("o1")])
                S.dma("sp", b0[:], K.OT[0, 1, rsl, t0:t0 + NB], writes=[tk("b0")])
                S.dma("act", b1[:], K.OT[1, 1, rsl, t0:t0 + NB], writes=[tk("b1")])
                S.dma("sp", gat[:], K.FT[3328 + 128 * m:3328 + 128 * m + 128, t0:t0 + NB], writes=[tk("gat")])
                S.tt("pool", o0[:], o0[:], o1[:], ALU.add, [tk("o0"), tk("o1")], [tk("o0")])
                S.tt("pool", b0[:], b0[:], b1[:], ALU.add, [tk("b0"), tk("b1")], [tk("b0")])
                pm_ = psA[q]
                S.mm(pm_[:, 0:NB], bones_s[:], o0[:], True, True, ["bones_s", tk("o0")], [("psA", q)])
                S.tt("dve", cen[:], o0[:], pm_[:, 0:NB], ALU.subtract, [tk("o0"), ("psA", q)], [tk("cen")])
                S.tt("pool", sq2[:], cen[:], cen[:], ALU.mult, [tk("cen")], [tk("sq2")])
                pv_ = psL[q]
                S.mm(pv_[:, 0:NB], bones_s[:], sq2[:], True, True, ["bones_s", tk("sq2")], [("psL", q)])
                S.ts("dve", sq2[:], pv_[:, 0:NB], GN_EPS, None, ALU.add, None, [("psL", q)], [tk("sq2")])
                S.act(sq2[:], sq2[:], AF.Ln, [tk("sq2")], [tk("sq2")])
                S.act(sq2[:], sq2[:], AF.Exp, [tk("sq2")], [tk("sq2")], scale=-0.5)
                S.tt("dve", cen[:], cen[:], sq2[:], ALU.mult, [tk("cen"), tk("sq2")], [tk("cen")])
                S.ts("dve", cen[:], cen[:], pc[:, 108 + m:109 + m], pc[:, 116 + m:117 + m], ALU.mult, ALU.add, [tk("cen"), "evpc"], [tk("cen")])
                S.tt("pool", cen[:], cen[:], b0[:], ALU.add, [tk("cen"), tk("b0")], [tk("cen")])
                S.act(sg[:], gat[:], AF.Sigmoid, [tk("gat")], [tk("sg")])
                S.tt("pool", gat[:], gat[:], sg[:], ALU.mult, [tk("gat"), tk("sg")], [tk("gat")])
                S.tt("dve", y16[q][:], cen[:], gat[:], ALU.mult, [tk("cen"), tk("gat")], [("roy", q)])
                S.dma("act", K.YT[rsl, t0:t0 + NB], y16[q][:], reads=[("roy", q)], writes=[("YT", "all")])


def phase_na(K, l):
    S, I = K.S, K.I
    i = l // 2
    import os
    maxrows = int(os.environ.get("NA_MAXROWS", "1000"))
    npairs = int(os.environ.get("NA_PAIRS", "8"))
    with contextlib.ExitStack() as st:
        ones = S.sb(st, "ones", [128, 64], BF16)
        S.memset("pool", ones[:], 1.0, ["ones"])
        q2 = [S.sb(st, "q2_%d" % b, [128, T], BF16) for b in range(2)]
        k2 = [S.sb(st, "k2_%d" % b, [128, T], BF16) for b in range(2)]
        Ve = [S.sb(st, "Ve%d" % b, [128, NTILE, 128], BF16) for b in range(2)]
        Vo = [S.sb(st, "Vo%d" % b, [128, NTILE - 1, 128], BF16) for b in range(2)]
        tb2 = [S.sb(st, "tb2_%d" % b, [128, 2, 14, 64]) for b in range(2)]
        gb = S.sb(st, "gb", [128, T])
        sgb = S.sb(st, "sgb", [128, T])
        ybT = S.sb(st, "ybT", [128, T])
        yb16 = S.sb(st, "yb16", [128, T], BF16)
        sT = [S.sb(st, "sT%d" % b, [128, 2, 256]) for b in range(2)]
        pT = [S.sb(st, "pT%d" % b, [128, 2, 384], BF16) for b in range(2)]
        rden = [S.sb(st, "rden%d" % b, [128, 64]) for b in range(2)]
        pss = [S.ps(st, "pss%d" % b, [128, 2, 512]) for b in range(2)]
        pso = [S.ps(st, "pso%d" % b, [128, 512]) for b in range(2)]
        VBe = K.VB.rearrange("(i p) c -> p i c", p=128)
        VBo = K.VB[64:64 + (NTILE - 1) * 128, :].rearrange("(i p) c -> p i c", p=128)
        nrow = 0
        for m in range(npairs):
            b = m % 2
            S.dma("sp", q2[b][:], K.QK[128 * m:128 * m + 128, :], writes=[("q2", b)])
            S.dma("act", k2[b][:], K.QK[1024 + 128 * m:1024 + 128 * m + 128, :], writes=[("k2", b)])
            for hh in range(2):
                i0, i1 = hh * 17, hh * 17 + 17
                S.dma("sp", Ve[b][:, i0:i1, :], VBe[:, i0:i1, 128 * m:128 * m + 128], writes=[("Ve", b)])
                j1 = min(i1, NTILE - 1)
                S.dma("act", Vo[b][:, i0:j1, :], VBo[:, i0:j1, 128 * m:128 * m + 128], writes=[("Vo", b)])
            S.dma("sp", tb2[b][:], I["ev_tb"][i, 2 * m:2 * m + 2].rearrange("h p s w -> p h s w"), writes=[("tb2", b)])
            S.dma("act", gb[:], K.FT[4352 + 128 * m:4352 + 128 * m + 128, :], writes=["gb"])
            rows = [("c", r) for r in range(CTX // 64)] + [("x", y) for y in range(64)]
            rows = rows[:maxrows]

            def rowinfo(kind, y):
                if kind == "x":
                    y0 = min(max(y - 4, 0), 56)
                    p_ = y - y0
                    tok0 = CTX + 64 * y0
                    q0 = CTX + 64 * y
                    chunks = [tok0 + 128 * c for c in range(4)] + [0, 128]
                else:
                    p_ = 0
                    q0 = 64 * y
                    chunks = [0, 128]
                return p_, q0, chunks

            def s1(kind, y, rb):
                p_, q0, chunks = rowinfo(kind, y)
                nch = len(chunks)
                for h in range(2):
                    pb = 64 * h
                    for c, kt0 in enumerate(chunks):
                        S.mm(pss[rb][:, h, c * 64:(c + 1) * 64], k2[b][pb:pb + 64, kt0:kt0 + 128], q2[b][pb:pb + 64, q0:q0 + 64],
                             True, True, [("k2", b), ("q2", b)], [("pss", rb, h)], signal=(c == nch - 1))

            def s2(kind, y, rb):
                p_, q0, chunks = rowinfo(kind, y)
                nch = len(chunks)
                psr = [("pss", rb, 0), ("pss", rb, 1)]
                if kind == "x":
                    for h in range(2):
                        S.stt("dve", sT[rb][:, h, :].rearrange("p (c w) -> p c w", w=64),
                              pss[rb][:, h, 0:256].rearrange("p (c w) -> p c w", w=64), 0.125,
                              tb2[b][:, h, 7 - p_:7 - p_ + 7:2, :], ALU.mult, ALU.add, [("pss", rb, h), ("tb2", b)], [("sT", rb)])
                    S.act(pT[rb][:, :, 0:256], sT[rb][:], AF.Exp, [("sT", rb)], [("pT", rb)])
                    S.act(pT[rb][:, :, 256:384], pss[rb][:, :, 256:384], AF.Exp, psr, [("pT", rb)], scale=0.125)
                else:
                    S.act(pT[rb][:, :, 0:128], pss[rb][:, :, 0:128], AF.Exp, psr, [("pT", rb)], scale=0.125)
                for h in range(2):
                    pb = 64 * h
                    for c, kt0 in enumerate(chunks):
                        if kt0 % 128 == 0:
                            vch = Ve[b][:, kt0 // 128, pb:pb + 64]
                            vr = ("Ve", b)
                        else:
                            vch = Vo[b][:, (kt0 - 64) // 128, pb:pb + 64]
                            vr = ("Vo", b)
                        S.mm(pso[rb][pb:pb + 64, 0:64], vch, pT[rb][:, h, c * 64:(c + 1) * 64], c == 0, c == nch - 1,
                             [vr, ("pT", rb)], [("pso", rb)], signal=False, tp=(0, pb))
                    for c in range(nch):
                        S.mm(pso[rb][pb:pb + 64, 64:128], ones[:], pT[rb][:, h, c * 64:(c + 1) * 64], c == 0, c == nch - 1,
                             ["ones", ("pT", rb)], [("pso", rb)], signal=(h == 1 and c == nch - 1), tp=(0, pb))
                S.op("dve", (lambda e, rb=rb: e.reciprocal(out=rden[rb][:], in_=pso[rb][:, 64:128])), [("pso", rb)], [("rden", rb)])
                S.tt("dve", ybT[:, q0:q0 + 64], pso[rb][:, 0:64], rden[rb][:], ALU.mult, [("pso", rb), ("rden", rb)], ["ybT"])

            if rows:
                s1(rows[0][0], rows[0][1], nrow % 2)
            for ri, (kind, y) in enumerate(rows):
                rb = nrow % 2
                nrow += 1
                if ri + 1 < len(rows):
                    s1(rows[ri + 1][0], rows[ri + 1][1], nrow % 2)
                s2(kind, y, rb)
            S.act(sgb[:], gb[:], AF.Sigmoid, ["gb"], ["sgb"])
            S.tt("pool", sgb[:], sgb[:], gb[:], ALU.mult, ["sgb", "gb"], ["sgb"])
            S.tt("pool", yb16[:], ybT[:], sgb[:], ALU.mult, ["ybT", "sgb"], ["yb16"])
            S.dma("sp", K.YT[1024 + 128 * m:1024 + 128 * m + 128, :], yb16[:], reads=["yb16"], writes=[("YT", "all")])


def _col(v, n=128):
    v = np.asarray(v, np.float32)
    return np.ascontiguousarray(v.reshape(-1, n).T)


def host_constants():
    import ml_dtypes
    c = {}
    c["c_ident"] = np.eye(128, dtype=np.float32)
    c["c_identb"] = np.eye(128, dtype=np.float32).astype(ml_dtypes.bfloat16)
    m = np.zeros((128, 5, 64), np.float32)
    su = np.triu(np.ones((64, 64), np.float32), 1)
    iu = np.triu(np.ones((64, 64), np.float32), 0)
    for h in range(2):
        m[h * 64:(h + 1) * 64, 0] = su
        m[h * 64:(h + 1) * 64, 1] = su.T
        m[h * 64:(h + 1) * 64, 2] = su
        m[h * 64:(h + 1) * 64, 3] = iu
        m[h * 64:(h + 1) * 64, 4] = iu
    c["c_masks"] = m
    bo = np.zeros((128, 128), np.float32)
    bo[:64, :64] = 1.0
    bo[64:, 64:] = 1.0
    c["c_bones"] = bo
    sm = np.ones((128, 2, 256), np.float32)
    sm[:, 0, 0::64] = 0.0
    sm[:, 1, 63::64] = 0.0
    c["c_scanm"] = sm
    return c


def host_layout(inputs, b):
    f = lambda k: np.asarray(inputs[k], np.float32)
    m = {}
    m["xin"] = np.ascontiguousarray(np.concatenate([f("ctx")[b], f("x")[b]], axis=0))
    ccm = np.stack([f("c")[b], f("c_ctx")], axis=-1)
    m["cc"] = np.ascontiguousarray(ccm.reshape(16, 128, 2).transpose(1, 0, 2))
    for k in ("mod_w", "mod_b", "norm_pre", "norm_post", "ev_w_in", "ev_w_out", "od_w_in", "od_w_out"):
        m[k] = f(k)
    m["od_gw"] = np.ascontiguousarray(np.stack([f("od_gate_a_w"), f("od_gate_x_w")], axis=1))
    pcs = []
    for i in range(2):
        cols = [_col(f("od_conv_w")[i, j]) for j in range(4)]
        cols.append(_col(f("od_conv_b")[i]))
        cols += [_col(f("od_gate_a_b")[i, d]) for d in range(2)]
        cols += [_col(f("od_gate_x_b")[i, d]) for d in range(2)]
        cols += [_col(f("od_lambda")[i, d]) for d in range(2)]
        pcs.append(np.stack(cols, axis=1))
    m["od_pc"] = np.ascontiguousarray(np.stack(pcs, axis=0))
    evs = []
    for i in range(2):
        mu = f("ev_mu")[i]
        cols = []
        for part in range(3):
            for j in range(2):
                cols.append(_col(mu[j, part * DA:(part + 1) * DA]))
        cols.append(_col(mu[0, 3 * DA:3 * DA + 128]))
        cols.append(_col(mu[1, 3 * DA:3 * DA + 128]))
        cols.append(_col(mu[0, 3 * DA + 128:3 * DA + 256]))
        cols.append(_col(mu[1, 3 * DA + 128:3 * DA + 256]))
        for d in range(2):
            cols.append(_col(f("ev_w0")[i, d]))
        for d in range(2):
            cols.append(_col(f("ev_a0")[i, d]))
        cols.append(_col(f("ev_k_k")[i]))
        cols.append(_col(f("ev_k_a")[i]))
        cols.append(_col(f("ev_r_k")[i].reshape(-1)))
        cols.append(_col(f("ev_gn_w")[i]))
        cols.append(_col(f("ev_gn_b")[i]))
        ev = np.concatenate(cols, axis=1)
        assert ev.shape[1] == EVPC_N, ev.shape
        evs.append(ev)
    m["ev_pc"] = np.ascontiguousarray(np.stack(evs, axis=0))
    wup = np.stack([f("ev_w_up").reshape(2, 128, DA), f("ev_a_up").reshape(2, 128, DA)], axis=1)
    m["ev_wup"] = np.ascontiguousarray(wup)
    rpb = f("ev_rpb")
    cols_ = np.arange(GRID_W)
    cstart = np.clip(cols_ - 8, 0, GRID_W - 16)
    tb = np.full((2, 16, 128, 14, 64), -60.0, np.float32)
    cc_, ww_ = np.meshgrid(np.arange(64), np.arange(64), indexing="ij")
    valid = (cc_ >= cstart[ww_]) & (cc_ < cstart[ww_] + 16)
    dx = np.clip(cc_ - ww_ + 15, 0, 30)
    for half in range(2):
        for slot in range(14):
            g = rpb[:, :, slot + half, :][:, :, dx]
            tb[:, :, half * 64:(half + 1) * 64, slot, :] = np.where(valid[None, None], g, np.float32(-60.0))
    m["ev_tb"] = tb
    m.update(host_constants())
    return m


_CACHE = {}


def kernel(**inputs):
    layers = (0, 1, 2, 3)
    key = ("full", layers)
    if key not in _CACHE:
        _CACHE[key] = build_program(list(layers))
    nc = _CACHE[key]
    n = 4
    in_maps = [host_layout(inputs, b) for b in range(n)]
    res = run_bass_kernel_spmd(nc, in_maps, core_ids=list(range(n)))
    out = np.stack([np.asarray(res.results[b]["out"], np.float32) for b in range(n)], axis=0)
    return out
```

```python
import contextlib
import numpy as np
import concourse.bass as bass
import concourse.mybir as mybir
from concourse.bass_utils import run_bass_kernel_spmd

F32 = mybir.dt.float32
BF16 = mybir.dt.bfloat16
ALU = mybir.AluOpType
AF = mybir.ActivationFunctionType
AX = mybir.AxisListType

ENGS = ("pe", "dve", "act", "pool", "sp")

D = 2048
CTX = 256
SEQ = 4096
T = CTX + SEQ
NTILE = T // 128
DEPTH = 4
A_SH = 3328
EV_IN = 8448
DA = 1024
DC = 2560
NQ = DC // 128
GRID_W = 64
EPS_RMS = 1e-6
GN_EPS = 64e-5


class Sched:
    def __init__(self, nc, stack, n_dma_sems=12):
        self.nc = nc
        self.stack = stack
        self.stream = {e: [] for e in ENGS}
        self.cnt = {e: 0 for e in ENGS}
        self.sem = {e: stack.enter_context(nc.semaphore("s_" + e)) for e in ENGS}
        self.seen = {e: {} for e in ENGS}
        self.lastw = {}
        self.readers = {}
        self.dsem = {}
        self.drr = {}
        for q in ("sp", "act", "pool"):
            self.dsem[q] = [[stack.enter_context(nc.semaphore("d_%s%d" % (q, i))), 0]
                            for i in range(n_dma_sems)]
            self.drr[q] = 0
        self.semkey = {}
        self.uid = 0
        self.ninst = 0

    def sb(self, st, name, shape, dt=F32):
        self.uid += 1
        return st.enter_context(self.nc.sbuf_tensor("%s_%d" % (name, self.uid), list(shape), dt))

    def ps(self, st, name, shape, dt=F32):
        self.uid += 1
        return st.enter_context(self.nc.psum_tensor("%s_%d" % (name, self.uid), list(shape), dt))

    def _deps(self, reads, writes):
        need = {}

        def add(sv):
            s, v = sv
            k = id(s)
            self.semkey[k] = s
            if need.get(k, 0) < v:
                need[k] = v
        for t in reads:
            if t in self.lastw:
                add(self.lastw[t])
        for t in writes:
            if t in self.lastw:
                add(self.lastw[t])
            for sv in self.readers.get(t, {}).values():
                add(sv)
        return need

    def _commit_waits(self, e, need, skip_own=False):
        waits = []
        seen = self.seen[e]
        for k, v in need.items():
            if skip_own and k == id(self.sem[e]):
                continue
            if seen.get(k, 0) >= v:
                continue
            seen[k] = v
            waits.append((self.semkey[k], v))
        return waits

    def _mark(self, reads, writes, sv):
        s, v = sv
        for t in writes:
            self.lastw[t] = sv
            self.readers[t] = {}
        for t in reads:
            if t in writes:
                continue
            r = self.readers.setdefault(t, {})
            k = id(s)
            if k not in r or r[k][1] < v:
                r[k] = sv

    def op(self, e, fn, reads=(), writes=(), signal=True):
        need = self._deps(reads, writes)
        waits = self._commit_waits(e, need, skip_own=(e == "pe"))
        if signal:
            self.cnt[e] += 1
            v = self.cnt[e]
            inc = (self.sem[e], 1)
        else:
            v = self.cnt[e] + 1
            inc = None
        self.stream[e].append((waits, fn, inc))
        self.ninst += 1 + len(waits)
        self._mark(reads, writes, (self.sem[e], v))

    def dma(self, q, out, in_, reads=(), writes=(), **kw):
        need = self._deps(reads, writes)
        k = self.drr[q]
        self.drr[q] = (k + 1) % len(self.dsem[q])
        ent = self.dsem[q][k]
        s = ent[0]
        self.semkey[id(s)] = s
        if ent[1] > 0 and need.get(id(s), 0) < 16 * ent[1]:
            need[id(s)] = 16 * ent[1]
        waits = self._commit_waits(q, need)
        ent[1] += 1
        v = 16 * ent[1]
        self.stream[q].append((waits, (lambda eng: eng.dma_start(out=out, in_=in_, **kw)), (s, 16)))
        self.ninst += 1 + len(waits)
        self._mark(reads, writes, (s, v))

    def barrier(self):
        need = {}
        for q in self.dsem:
            for s, c in self.dsem[q]:
                if c > 0:
                    self.semkey[id(s)] = s
                    need[id(s)] = 16 * c
        for e in ENGS:
            if self.cnt[e] > 0:
                self.semkey[id(self.sem[e])] = self.sem[e]
                need[id(self.sem[e])] = self.cnt[e]
        for e in ENGS:
            waits = self._commit_waits(e, dict(need))
            if waits:
                self.stream[e].append((waits, None, None))
        self.lastw = {}
        self.readers = {}

    def emit(self):
        nc = self.nc
        st = self.stream

        def run(name, eng):
            for waits, fn, inc in st[name]:
                for s, v in waits:
                    eng.wait_ge(s, v)
                if fn is None:
                    continue
                ins = fn(eng)
                if inc is not None:
                    ins.then_inc(inc[0], inc[1])
        with nc.Block() as block:
            @block.tensor
            def _(e):
                run("pe", e)

            @block.vector
            def _(e):
                run("dve", e)

            @block.scalar
            def _(e):
                run("act", e)

            @block.gpsimd
            def _(e):
                run("pool", e)

            @block.sync
            def _(e):
                run("sp", e)

    def mm(self, out, lhsT, rhs, start, stop, reads, writes, signal=None, tp=None):
        if signal is None:
            signal = bool(stop)
        kw = {}
        if tp is not None:
            kw["tile_position"] = tp
        self.op("pe", lambda e: e.matmul(out, lhsT=lhsT, rhs=rhs, start=start, stop=stop, **kw),
                reads, writes, signal=signal)

    def tr(self, out, in_, ident, reads, writes, signal=True, tp=None):
        kw = {}
        if tp is not None:
            kw["tile_position"] = tp
        self.op("pe", lambda e: e.transpose(out, in_, ident, **kw), reads, writes, signal=signal)

    def tt(self, eng, out, in0, in1, op, reads, writes):
        self.op(eng, lambda e: e.tensor_tensor(out=out, in0=in0, in1=in1, op=op), reads, writes)

    def ts(self, eng, out, in0, s1, s2, op0, op1, reads, writes):
        if op1 is None:
            self.op(eng, lambda e: e.tensor_scalar(out=out, in0=in0, scalar1=s1, scalar2=None, op0=op0), reads, writes)
        else:
            self.op(eng, lambda e: e.tensor_scalar(out=out, in0=in0, scalar1=s1, scalar2=s2, op0=op0, op1=op1), reads, writes)

    def stt(self, eng, out, in0, scalar, in1, op0, op1, reads, writes):
        eng = "dve"
        self.op(eng, lambda e: e.scalar_tensor_tensor(out=out, in0=in0, scalar=scalar, in1=in1, op0=op0, op1=op1),
                reads, writes)

    def cp(self, eng, out, in_, reads, writes):
        if eng == "act":
            self.op(eng, lambda e: e.copy(out=out, in_=in_), reads, writes)
        else:
            self.op(eng, lambda e: e.tensor_copy(out=out, in_=in_), reads, writes)

    def act(self, out, in_, func, reads, writes, bias=None, scale=None, accum_out=None):
        kw = {}
        if bias is not None:
            kw["bias"] = bias
        if scale is not None:
            kw["scale"] = scale
        if accum_out is not None:
            kw["accum_out"] = accum_out
        self.op("act", lambda e: e.activation(out=out, in_=in_, func=func, **kw), reads, writes)

    def memset(self, eng, ap, val, writes):
        self.op(eng, lambda e: e.memset(ap, val), (), writes)


def _blocks():
    return [(0, 9), (9, 9), (18, 8), (26, 8)]


class Ctx:
    pass


def build_program(layers, final_out=True, debug_taps=(), stop_after=None):
    nc = bass.Bass("TRN2", target_bir_lowering=False)
    K = Ctx()
    K.nc = nc

    def din(name, shape, dt=F32):
        return nc.dram_tensor(name, list(shape), dt, kind="ExternalInput").ap()

    def dscr(name, shape, dt=F32):
        return nc.dram_tensor(name, list(shape), dt, kind="Internal").ap()

    I = {}
    I["xin"] = din("xin", [T, D])
    I["cc"] = din("cc", [128, 16, 2])
    I["mod_w"] = din("mod_w", [DEPTH, D, 3 * D])
    I["mod_b"] = din("mod_b", [DEPTH, 3 * D])
    I["norm_pre"] = din("norm_pre", [DEPTH, D])
    I["norm_post"] = din("norm_post", [DEPTH, D])
    I["ev_w_in"] = din("ev_w_in", [2, D, EV_IN])
    I["ev_w_out"] = din("ev_w_out", [2, D, D])
    I["od_w_in"] = din("od_w_in", [2, D, 2 * DC])
    I["od_w_out"] = din("od_w_out", [2, DC, D])
    I["od_gw"] = din("od_gw", [2, 2, 2, 16, 160, 160])
    I["od_pc"] = din("od_pc", [2, 128, 11, NQ])
    I["ev_pc"] = din("ev_pc", [2, 128, EVPC_N])
    I["ev_wup"] = din("ev_wup", [2, 2, 128, DA])
    I["ev_tb"] = din("ev_tb", [2, 16, 128, 14, 64])
    I["c_ident"] = din("c_ident", [128, 128])
    I["c_identb"] = din("c_identb", [128, 128], BF16)
    I["c_masks"] = din("c_masks", [128, 5, 64])
    I["c_bones"] = din("c_bones", [128, 128])
    I["c_scanm"] = din("c_scanm", [128, 2, 256])
    out = nc.dram_tensor("out", [SEQ, D], F32, kind="ExternalOutput").ap()
    K.I = I
    K.out = out
    K.X = dscr("X", [T, D])
    K.modrow = dscr("modrow", [2, 3 * D])
    K.FT = dscr("FT", [5376, T])
    K.QK = dscr("QK", [2048, T], BF16)
    K.VB = dscr("VB", [T, 1024], BF16)
    K.YT = dscr("YT", [DC, T], BF16)
    K.HF = dscr("HF", [DC, T])
    K.OT = dscr("OT", [2, 2, DA, T])
    K.Wb_in = {}
    K.Wb_out = {}
    for l in layers:
        i = l // 2
        if l % 2 == 0:
            K.Wb_in[l] = dscr("wbin%d" % l, [EV_IN // 256, 128, 16, 256], BF16)
            K.Wb_out[l] = dscr("wbout%d" % l, [D, D], BF16)
        else:
            K.Wb_in[l] = dscr("wbin%d" % l, [2 * DC // 256, 128, 16, 256], BF16)
            K.Wb_out[l] = dscr("wbout%d" % l, [DC, D], BF16)
    taps = {}
    for name, shape in debug_taps:
        taps[name] = nc.dram_tensor("tap_" + name, list(shape), F32, kind="ExternalOutput").ap()
    K.taps = taps

    with contextlib.ExitStack() as st:
        S = Sched(nc, st)
        K.S = S
        def cast_weights(l):
            i = l // 2
            win = I["ev_w_in"][i] if l % 2 == 0 else I["od_w_in"][i]
            wout = I["ev_w_out"][i] if l % 2 == 0 else I["od_w_out"][i]
            winv = win.rearrange("(k p) c -> p k c", p=128)
            for g in range(K.Wb_in[l].shape[0]):
                S.dma("pool", K.Wb_in[l][g], winv[:, :, g * 256:(g + 1) * 256], writes=[("wbin", l)])
            nrow = D if l % 2 == 0 else DC
            for r in range(0, nrow, 256):
                S.dma("pool", K.Wb_out[l][r:r + 256, :], wout[r:r + 256, :], writes=[("wbout", l)])
        cast_weights(layers[0])
        for r in range(0, T, 544):
            S.dma("sp", K.X[r:r + 544, :], I["xin"][r:r + 544, :], writes=[("X", "all")])
        S.barrier()
        stopped = stop_after == "init"
        for li, l in enumerate(layers):
            if stopped:
                break
            last = (li == len(layers) - 1) and final_out
            import os
            skip = os.environ.get("SKIP_PH", "").split(",")
            if "mod" not in skip:
                phase_mod(K, l)
                S.barrier()
            if stop_after == "mod":
                stopped = True
                break
            if "proj" not in skip:
                phase_proj(K, l)
                S.barrier()
            if stop_after == "proj":
                stopped = True
                break
            if li + 1 < len(layers):
                cast_weights(layers[li + 1])
            if l % 2 == 0:
                phase_rwkv(K, l)
                S.barrier()
                phase_na(K, l)
                S.barrier()
            else:
                phase_rglru(K, l)
                S.barrier()
            if stop_after == "mixer":
                stopped = True
                break
            phase_out(K, l, last)
            S.barrier()
        if stopped:
            final_out = False
        if not final_out:
            for r in range(0, SEQ, 512):
                S.dma("sp", out[r:r + 512, :], K.X[CTX + r:CTX + r + 512, :])
        S.barrier()
        print("instructions (incl waits):", S.ninst, {e: len(S.stream[e]) for e in ENGS})
        S.emit()
    return nc


def phase_mod(K, l):
    S, I = K.S, K.I
    with contextlib.ExitStack() as st:
        cc = S.sb(st, "cc", [128, 16, 2])
        sc = S.sb(st, "sc", [128, 16, 2])
        S.dma("sp", cc[:], I["cc"], writes=["cc"])
        S.act(sc[:], cc[:], AF.Sigmoid, ["cc"], ["sc"])
        S.tt("dve", sc[:], sc[:], cc[:], ALU.mult, ["sc", "cc"], ["sc"])
        mw = [S.sb(st, "mw%d" % i, [128, 16, 512]) for i in range(2)]
        pm = [S.ps(st, "pm%d" % i, [2, 512]) for i in range(2)]
        msb = S.sb(st, "msb", [2, 3 * D])
        mb = S.sb(st, "mb", [2, 3 * D])
        S.dma("act", mb[:], I["mod_b"][l:l + 1, :].partition_broadcast(2), writes=["mb"])
        src = I["mod_w"][l].rearrange("(k p) c -> p k c", p=128)
        for g in range(12):
            b = g % 2
            for h in range(2):
                S.dma("sp" if h == 0 else "act", mw[b][:, 8 * h:8 * h + 8, :], src[:, 8 * h:8 * h + 8, g * 512:(g + 1) * 512],
                      writes=[("mw", b, h)])
            for k in range(16):
                S.mm(pm[b][:], sc[:, k, :], mw[b][:, k, :], k == 0, k == 15,
                     ["sc", ("mw", b, k // 8)], [("pm", b)])
            S.tt("dve", msb[:, g * 512:(g + 1) * 512], pm[b][:], mb[:, g * 512:(g + 1) * 512], ALU.add,
                 [("pm", b), "mb"], ["msb"])
        S.dma("sp", K.modrow, msb[:], reads=["msb"], writes=["modrow"])


def load_bcast_rows(K, st, l, which):
    S, I = K.S, K.I
    res = {}
    tmp = S.sb(st, "bt_n", [128, D])
    if which == "pre":
        S.dma("act", tmp[:], I["norm_pre"][l:l + 1, :].partition_broadcast(128), writes=["bt_n"])
        for s, nm in ((0, "x"), (1, "c")):
            G = S.sb(st, "G" + nm, [128, D])
            Sh = S.sb(st, "S" + nm, [128, D])
            S.dma("sp", G[:], K.modrow[s:s + 1, D:2 * D].partition_broadcast(128), reads=["modrow"], writes=["G" + nm])
            S.dma("act", Sh[:], K.modrow[s:s + 1, 0:D].partition_broadcast(128), reads=["modrow"], writes=["S" + nm])
            S.stt("dve", G[:], G[:], 1.0, tmp[:], ALU.add, ALU.mult, ["G" + nm, "bt_n"], ["G" + nm])
            res[nm] = (G, Sh)
    else:
        S.dma("act", tmp[:], I["norm_post"][l:l + 1, :].partition_broadcast(128), writes=["bt_n"])
        for s, nm in ((0, "x"), (1, "c")):
            G = S.sb(st, "GP" + nm, [128, D])
            S.dma("sp", G[:], K.modrow[s:s + 1, 2 * D:3 * D].partition_broadcast(128), reads=["modrow"], writes=["GP" + nm])
            S.tt("dve", G[:], G[:], tmp[:], ALU.mult, ["GP" + nm, "bt_n"], ["GP" + nm])
            res[nm] = G
    return res


def proj_spec(l):
    if l % 2 == 0:
        return [(0, 4352, "fm32", 0), (4352, 6400, "fmbf", 0), (6400, 7424, "tm", 0), (7424, 8448, "fm32", 4352)]
    return [(0, 2 * DC, "fm32", 0)]


def phase_proj(K, l):
    S, I = K.S, K.I
    spec = proj_spec(l)
    Wb = K.Wb_in[l]
    with contextlib.ExitStack() as st:
        rows = load_bcast_rows(K, st, l, "pre")
        identb = S.sb(st, "identb", [128, 128], BF16)
        S.dma("sp", identb[:], I["c_identb"], writes=["identb"])
        xt = [S.sb(st, "xt%d" % i, [128, D]) for i in range(2)]
        junk = S.sb(st, "junk", [128, D], BF16)
        ss = [S.sb(st, "ss%d" % i, [128, 1]) for i in range(2)]
        hn = S.sb(st, "hn", [128, D])
        hb = [S.sb(st, "hb%d" % i, [128, D], BF16) for i in range(2)]
        hT2 = [S.sb(st, "hT%d" % i, [128, 16, 9 * 128], BF16) for i in range(2)]
        ptr = [S.ps(st, "ptr%d" % i, [128, 1024], BF16) for i in range(2)]
        pp = [S.ps(st, "pp%d" % i, [128, 512]) for i in range(4)]
        wt = [S.sb(st, "wt%d" % i, [128, 16, 256], BF16) for i in range(3)]
        stg = [S.sb(st, "stg%d" % i, [128, 9 * 128]) for i in range(3)]
        stgb = [S.sb(st, "stgb%d" % i, [128, 9 * 128], BF16) for i in range(2)]
        stgt = [S.sb(st, "stgt%d" % i, [128, 256], BF16) for i in range(3)]
        cnt = {"x": 0, "pp": 0, "w": 0, "stg": 0, "stgb": 0, "stgt": 0, "ev": 0}

        def a_tile(tile_i, hbuf, ti):
            hT = hT2[hbuf]
            b = cnt["x"] % 2
            cnt["x"] += 1
            G, Sh = rows["c"] if tile_i < 2 else rows["x"]
            gn = "Gc" if tile_i < 2 else "Gx"
            sn = "Sc" if tile_i < 2 else "Sx"
            S.dma("sp", xt[b][:], K.X[tile_i * 128:(tile_i + 1) * 128, :], reads=[("X", tile_i), ("X", "all")],
                  writes=[("xt", b)])
            S.act(junk[:], xt[b][:], AF.Square, [("xt", b)], ["junk", ("ss", b)], accum_out=ss[b][:])
            S.act(ss[b][:], ss[b][:], AF.Ln, [("ss", b)], [("ss", b)], scale=1.0 / D, bias=EPS_RMS)
            S.act(ss[b][:], ss[b][:], AF.Exp, [("ss", b)], [("ss", b)], scale=-0.5)
            S.stt("dve", hn[:], xt[b][:], ss[b][:, 0:1], G[:], ALU.mult, ALU.mult, [("xt", b), ("ss", b), gn], ["hn"])
            S.tt("pool", hb[b][:], hn[:], Sh[:], ALU.add, ["hn", sn], [("hb", b)])
            for half in range(2):
                for k in range(8):
                    kk = half * 8 + k
                    S.tr(ptr[half][:, k * 128:(k + 1) * 128], hb[b][:, kk * 128:(kk + 1) * 128], identb[:],
                         [("hb", b), "identb"], [("ptr", half)], signal=(k == 7))
                S.cp("act" if half == 0 else "dve", hT[:, half * 8:half * 8 + 8, ti * 128:(ti + 1) * 128],
                     ptr[half][:].rearrange("p (k t) -> p k t", k=8), [("ptr", half)], [("hT", hbuf, ti)])

        def b_group(t0, nt, hbuf, c0, kind, drow, g0):
            hT = hT2[hbuf]
            ntok = nt * 128
            wb = cnt["w"] % 3
            cnt["w"] += 1
            S.dma("sp", wt[wb][:], Wb[g0 // 256], reads=[("wbin", l)], writes=[("wt", wb)])
            if kind in ("fm32", "fmbf"):
                for sub in range(2):
                    if kind == "fm32":
                        sg = stg[cnt["stg"] % 3]
                        sgn = ("stg", cnt["stg"] % 3)
                        cnt["stg"] += 1
                    else:
                        sg = stgb[cnt["stgb"] % 2]
                        sgn = ("stgb", cnt["stgb"] % 2)
                        cnt["stgb"] += 1
                    for ts0 in range(0, ntok, 512):
                        tsn = min(512, ntok - ts0)
                        pb = cnt["pp"] % 4
                        cnt["pp"] += 1
                        rtok = [("hT", hbuf, ti) for ti in range(ts0 // 128, (ts0 + tsn) // 128)]
                        for k in range(16):
                            S.mm(pp[pb][:, 0:tsn], wt[wb][:, k, sub * 128:(sub + 1) * 128], hT[:, k, ts0:ts0 + tsn],
                                 k == 0, k == 15, [("wt", wb)] + rtok, [("pp", pb)])
                        cnt["ev"] += 1
                        S.cp("act" if cnt["ev"] % 2 == 0 else "dve", sg[:, ts0:ts0 + tsn], pp[pb][:, 0:tsn], [("pp", pb)], [sgn])
                    r0 = drow + (g0 - c0) + sub * 128
                    dst = K.FT if kind == "fm32" else K.QK
                    S.dma("pool" if kind == "fm32" else "sp", dst[r0:r0 + 128, t0 * 128:t0 * 128 + ntok], sg[:, 0:ntok], reads=[sgn],
                          writes=[("FT" if kind == "fm32" else "QK", r0 // 128)])
            else:
                for ti in range(nt):
                    pb = cnt["pp"] % 4
                    cnt["pp"] += 1
                    for k in range(16):
                        S.mm(pp[pb][:, 0:256], hT[:, k, ti * 128:(ti + 1) * 128], wt[wb][:, k, :],
                             k == 0, k == 15, [("wt", wb), ("hT", hbuf, ti)], [("pp", pb)])
                    sb_ = cnt["stgt"] % 3
                    cnt["stgt"] += 1
                    cnt["ev"] += 1
                    S.cp("act" if cnt["ev"] % 2 == 0 else "dve", stgt[sb_][:], pp[pb][:, 0:256], [("pp", pb)], [("stgt", sb_)])
                    cc0 = g0 - c0
                    S.dma("pool", K.VB[(t0 + ti) * 128:(t0 + ti + 1) * 128, cc0:cc0 + 256], stgt[sb_][:],
                          reads=[("stgt", sb_)], writes=[("VB", t0 + ti)])

        blocks = _blocks()
        groups = [(c0, kind, drow, g0) for (c0, c1, kind, drow) in spec for g0 in range(c0, c1, 256)]
        t0, nt = blocks[0]
        for ti in range(nt):
            a_tile(t0 + ti, 0, ti)
        for bi, (t0, nt) in enumerate(blocks):
            hbuf = bi % 2
            nxt = blocks[bi + 1] if bi + 1 < len(blocks) else None
            pend = list(range(nxt[1])) if nxt else []
            every = max(1, (len(groups) - 2) // max(1, len(pend))) if pend else 0
            for gi, (c0, kind, drow, g0) in enumerate(groups):
                b_group(t0, nt, hbuf, c0, kind, drow, g0)
                if pend and (gi + 1) % every == 0:
                    ti = pend.pop(0)
                    a_tile(nxt[0] + ti, 1 - hbuf, ti)
            while pend:
                ti = pend.pop(0)
                a_tile(nxt[0] + ti, 1 - hbuf, ti)


def phase_out(K, l, last):
    S, I = K.S, K.I
    nf = D if l % 2 == 0 else DC
    nk = nf // 128
    Wb = K.Wb_out[l].rearrange("(k p) c -> p k c", p=128)
    YTv = K.YT[0:nf, :].rearrange("(k p) t -> p k t", p=128)
    with contextlib.ExitStack() as st:
        rows = load_bcast_rows(K, st, l, "post")
        wo = S.sb(st, "wo", [128, nk, D], BF16)
        for k in range(0, nk, 4):
            S.dma("sp" if (k // 4) % 2 == 0 else "act", wo[:, k:k + 4, :], Wb[:, k:k + 4, :], reads=[("wbout", l)], writes=["wo"])
        yt = [S.sb(st, "yt%d" % i, [128, nk, 128], BF16) for i in range(2)]
        xt = [S.sb(st, "xo%d" % i, [128, D]) for i in range(2)]
        pz = [S.ps(st, "pz%d" % i, [128, D]) for i in range(2)]
        junk = S.sb(st, "junko", [128, D], BF16)
        ssq = [S.sb(st, "sso%d" % i, [128, 1]) for i in range(2)]
        zz = S.sb(st, "zz", [128, D])
        xn = [S.sb(st, "xn%d" % i, [128, D]) for i in range(2)]
        tiles = list(range(NTILE))
        if last:
            tiles = list(range(2, NTILE))
        for n, ti in enumerate(tiles):
            b = n % 2
            GP = rows["c"] if ti < 2 else rows["x"]
            gpn = "GPc" if ti < 2 else "GPx"
            S.dma("sp", yt[b][:], YTv[:, :, ti * 128:(ti + 1) * 128], reads=[("YT", ti), ("YT", "all")], writes=[("yt", b)])
            S.dma("act", xt[b][:], K.X[ti * 128:(ti + 1) * 128, :], reads=[("X", ti), ("X", "all")], writes=[("xo", b)])
            for nn in range(4):
                for k in range(nk):
                    S.mm(pz[b][:, nn * 512:(nn + 1) * 512], yt[b][:, k, :], wo[:, k, nn * 512:(nn + 1) * 512],
                         k == 0, k == nk - 1, [("yt", b), "wo"], [("pz", b, nn)])
            pzr = [("pz", b, nn) for nn in range(4)]
            S.act(junk[:], pz[b][:], AF.Square, pzr, ["junko", ("sso", b)], accum_out=ssq[b][:])
            S.act(ssq[b][:], ssq[b][:], AF.Ln, [("sso", b)], [("sso", b)], scale=1.0 / D, bias=EPS_RMS)
            S.act(ssq[b][:], ssq[b][:], AF.Exp, [("sso", b)], [("sso", b)], scale=-0.5)
            S.stt("dve", zz[:], pz[b][:], ssq[b][:, 0:1], GP[:], ALU.mult, ALU.mult, pzr + [("sso", b), gpn], ["zz"])
            S.tt("pool", xn[b][:], zz[:], xt[b][:], ALU.add, ["zz", ("xo", b)], [("xn", b)])
            if last:
                S.dma("sp", K.out[(ti - 2) * 128:(ti - 1) * 128, :], xn[b][:], reads=[("xn", b)])
            else:
                S.dma("pool", K.X[ti * 128:(ti + 1) * 128, :], xn[b][:], reads=[("xn", b)], writes=[("X", ti)])


RG_TB = 128


def rg_rects():
    res = []
    for h in range(16):
        lo, hi = 160 * h, 160 * h + 160
        for q in range(lo // 128, (hi - 1) // 128 + 1):
            r0, r1 = max(lo, 128 * q), min(hi, 128 * q + 128)
            for q2 in range(lo // 128, (hi - 1) // 128 + 1):
                c0, c1 = max(lo, 128 * q2), min(hi, 128 * q2 + 128)
                res.append((h, q, r0, r1, q2, c0, c1))
    return res


def phase_rglru(K, l):
    S, I = K.S, K.I
    i = l // 2
    NB = T // RG_TB
    NCTX = CTX // RG_TB
    TB = RG_TB
    with contextlib.ExitStack() as st:
        pc = S.sb(st, "pc", [128, 11, NQ])
        S.dma("sp", pc[:], I["od_pc"][i], writes=["pc"])
        spl = S.sb(st, "spl", [128, 2, NQ])
        S.act(spl[:], pc[:, 9:11, :], AF.Exp, ["pc"], ["spl"], scale=-1.0)
        S.act(spl[:], spl[:], AF.Ln, ["spl"], ["spl"], bias=1.0)
        S.ts("dve", spl[:], spl[:], -8.0, None, ALU.mult, None, ["spl"], ["spl"])
        wz = {}
        for g in range(2):
            wz[g] = S.sb(st, "wz%d" % g, [128, NQ, 384], BF16)
        HALO = 4
        xr = [S.sb(st, "xr%d" % b, [128, NQ, TB + HALO]) for b in range(2)]
        gg = S.sb(st, "gg", [128, NQ, TB])
        hf = S.sb(st, "hf", [128, NQ, TB])
        u_ = [S.sb(st, "u%d" % q, [128, NQ, TB]) for q in range(2)]
        tmp_ = [S.sb(st, "rtmp%d" % q, [128, NQ, TB]) for q in range(2)]
        ub_ = [S.sb(st, "ub%d" % q, [128, NQ, TB], BF16) for q in range(2)]
        ga_1 = S.sb(st, "ga", [128, NQ, TB])
        ga_ = [ga_1, ga_1]
        gx_ = [S.sb(st, "gx%d" % q, [128, NQ, TB]) for q in range(2)]
        aa_ = [S.sb(st, "aa%d" % q, [128, NQ, TB]) for q in range(2)]
        bb_ = [S.sb(st, "bb%d" % q, [128, NQ, TB]) for q in range(2)]
        hh = S.sb(st, "hh", [128, NQ, TB])
        yb = S.sb(st, "yb", [128, NQ, TB], BF16)
        state = S.sb(st, "state", [128, NQ])
        pg = [S.ps(st, "pg%d" % b, [128, 2, 256]) for b in range(4)]
        FTx = K.FT[0:DC, :].rearrange("(q p) t -> p q t", p=128)
        FTg = K.FT[DC:2 * DC, :].rearrange("(q p) t -> p q t", p=128)
        HFv = K.HF.rearrange("(q p) t -> p q t", p=128)
        YTv = K.YT.rearrange("(q p) t -> p q t", p=128)

        def bc(col):
            return col.unsqueeze(2).to_broadcast([128, NQ, TB])
        n_pg = 0
        nblk = 0
        for d in range(2):
            S.barrier()
            for g in range(2):
                S.memset("pool", wz[g][:], 0.0, [("wz", g)])
                for n, (h, q, r0, r1, q2, c0, c1) in enumerate(rg_rects()):
                    S.dma("pool",
                          wz[g][r0 - 128 * q:r1 - 128 * q, q, (q2 - q + 1) * 128 + (c0 - 128 * q2):(q2 - q + 1) * 128 + (c1 - 128 * q2)],
                          I["od_gw"][i, g, d, h, r0 - 160 * h:r1 - 160 * h, c0 - 160 * h:c1 - 160 * h],
                          reads=[], writes=[("wz", g)])
            S.memset("dve", state[:], 0.0, ["state"])
            if d == 0:
                order = list(range(NB))
            else:
                order = list(range(NCTX - 1, -1, -1)) + list(range(NB - 1, NCTX - 1, -1))
            import os
            blks = order[:int(os.environ.get("RG_MAXB", "1000"))]

            def front(bi, b):
                nonlocal n_pg
                t0 = bi * TB
                u, ub, ga, gx, aa, bb = u_[b], ub_[b], ga_[b], gx_[b], aa_[b], bb_[b]
                seg0, seg1 = (0, CTX) if t0 < CTX else (CTX, T)
                lo = max(seg0, t0 - 2)
                hi = min(seg1, t0 + TB + 1)
                if lo > t0 - 2:
                    S.memset("pool", xr[b][:, :, 0:2], 0.0, [("xr", b)])
                if hi < t0 + TB + 1:
                    S.memset("pool", xr[b][:, :, TB + 2:TB + 3], 0.0, [("xr", b)])
                for qh in range(2):
                    qs = slice(qh * 10, qh * 10 + 10)
                    S.dma("sp", xr[b][:, qs, lo - (t0 - 2):hi - (t0 - 2)], FTx[:, qs, lo:hi],
                          reads=[("FT", "all")], writes=[("xr", b)])
                S.tt("dve", u[:], xr[b][:, :, 0:TB], bc(pc[:, 0, :]), ALU.mult, [("xr", b), "pc"], [("u", b)])
                for j in range(1, 4):
                    tmp = tmp_[j % 2]
                    S.tt("pool", tmp[:], xr[b][:, :, j:j + TB], bc(pc[:, j, :]), ALU.mult, [("xr", b), "pc"], [("rtmp", j % 2)])
                    S.tt("dve", u[:], u[:], tmp[:], ALU.add, [("u", b), ("rtmp", j % 2)], [("u", b)])
                S.tt("dve", u[:], u[:], bc(pc[:, 4, :]), ALU.add, [("u", b), "pc"], [("u", b)])
                S.cp("act", ub[:], u[:], [("u", b)], [("ub", b)])
                for q2 in range(NQ):
                    pb = n_pg % 4
                    n_pg += 1
                    qs_ = [q for q in (q2 - 1, q2, q2 + 1) if 0 <= q < NQ]
                    for g in range(2):
                        for n, q in enumerate(qs_):
                            slot = q2 - q + 1
                            S.mm(pg[pb][:, g, 0:TB], wz[g][:, q, slot * 128:(slot + 1) * 128], ub[:, q, :],
                                 n == 0, n == len(qs_) - 1, [("wz", g), ("ub", b)], [("pg", pb)])
                    S.act(ga[:, q2, :], pg[pb][:, 0, 0:TB], AF.Sigmoid, [("pg", pb), "pc"], [("ga", q2)], bias=pc[:, 5 + d, q2:q2 + 1])
                    S.act(gx[:, q2, :], pg[pb][:, 1, 0:TB], AF.Sigmoid, [("pg", pb), "pc"], [("gx", b, q2)], bias=pc[:, 7 + d, q2:q2 + 1])

            def front_b(bi, b):
                u, ub, ga, gx, aa, bb = u_[b], ub_[b], ga_[b], gx_[b], aa_[b], bb_[b]
                gaall = [("ga", q) for q in range(NQ)]
                gxall = [("gx", b, q) for q in range(NQ)]
                S.tt("pool", aa[:], ga[:], bc(spl[:, d, :]), ALU.mult, gaall + ["spl"], [("aa", b)])
                S.act(aa[:], aa[:], AF.Exp, [("aa", b)], [("aa", b)])
                S.tt("pool", bb[:], aa[:], aa[:], ALU.mult, [("aa", b)], [("bb", b)])
                S.act(bb[:], bb[:], AF.Ln, [("bb", b)], [("bb", b)], scale=-1.0, bias=1.0)
                S.act(bb[:], bb[:], AF.Exp, [("bb", b)], [("bb", b)], scale=0.5)
                S.tt("dve", gx[:], gx[:], u[:], ALU.mult, gxall + [("u", b)], gxall)
                S.tt("pool", bb[:], bb[:], gx[:], ALU.mult, [("bb", b)] + gxall, [("bb", b)])

            def back(bi, b):
                t0 = bi * TB
                aa, bb = aa_[b], bb_[b]
                if d == 1:
                    for qh in range(2):
                        qs = slice(qh * 10, qh * 10 + 10)
                        S.dma("sp", gg[:, qs, :], FTg[:, qs, t0:t0 + TB], reads=[("FT", "all")], writes=["gg"])
                        S.dma("sp", hf[:, qs, :], HFv[:, qs, t0:t0 + TB], reads=[("HF", bi)], writes=["hf"])
                for q in range(NQ):
                    if d == 0:
                        S.op("dve", (lambda e, q=q, aa=aa, bb=bb: e.tensor_tensor_scan(out=hh[:, q, :], data0=aa[:, q, :], data1=bb[:, q, :],
                                                                           initial=state[:, q:q + 1], op0=ALU.mult, op1=ALU.add)),
                             [("aa", b), ("bb", b), "state"], [("hh", q)])
                    else:
                        S.op("dve", (lambda e, q=q, aa=aa, bb=bb: e.tensor_tensor_scan(out=hh[:, q, ::-1], data0=aa[:, q, ::-1], data1=bb[:, q, ::-1],
                                                                           initial=state[:, q:q + 1], op0=ALU.mult, op1=ALU.add)),
                             [("aa", b), ("bb", b), "state"], [("hh", q)])
                hhall = [("hh", q) for q in range(NQ)]
                if d == 0:
                    S.cp("pool", state[:], hh[:, :, TB - 1], hhall + ["state"], ["state"])
                    for qh in range(2):
                        qs = slice(qh * 10, qh * 10 + 10)
                        S.dma("sp", HFv[:, qs, t0:t0 + TB], hh[:, qs, :], reads=hhall, writes=[("HF", bi)])
                else:
                    S.cp("pool", state[:], hh[:, :, 0], hhall + ["state"], ["state"])
                    S.tt("pool", hh[:], hh[:], hf[:], ALU.add, hhall + ["hf"], hhall)
                    S.act(hf[:], gg[:], AF.Sigmoid, ["gg"], ["hf"])
                    S.tt("dve", gg[:], gg[:], hf[:], ALU.mult, ["gg", "hf"], ["gg"])
                    S.tt("dve", yb[:], hh[:], gg[:], ALU.mult, hhall + ["gg"], ["yb"])
                    for qh in range(2):
                        qs = slice(qh * 10, qh * 10 + 10)
                        S.dma("sp", YTv[:, qs, t0:t0 + TB], yb[:, qs, :], reads=["yb"], writes=[("YT", "all")])

            for n, bi in enumerate(blks):
                front(bi, n % 2)
                if n >= 1:
                    back(blks[n - 1], (n - 1) % 2)
                front_b(bi, n % 2)
            if blks:
                back(blks[-1], (len(blks) - 1) % 2)


EVPC_N = 124


def phase_rwkv(K, l):
    S, I = K.S, K.I
    i = l // 2
    import os
    NB = 256
    NCH = NB // 64
    NP = 8
    maxblk = int(os.environ.get("RW_MAXB", "1000"))
    FT3 = K.FT[0:3072, :].rearrange("(part q p) t -> q p part t", part=3, q=8, p=128)
    FTc = K.FT[3072:3328, :].rearrange("(g p) t -> p g t", p=128)
    with contextlib.ExitStack() as st:
        pc = S.sb(st, "evpc", [128, EVPC_N])
        S.dma("sp", pc[:], I["ev_pc"][i], writes=["evpc"])
        mc = S.sb(st, "mc", [128, 26])
        for part in range(3):
            S.tt("dve", mc[:, part * 8:part * 8 + 8], pc[:, part * 16:part * 16 + 8], pc[:, part * 16 + 8:part * 16 + 16], ALU.add,
                 ["evpc"], ["mc"])
        S.tt("dve", mc[:, 24:25], pc[:, 48:49], pc[:, 49:50], ALU.add, ["evpc"], ["mc"])
        S.tt("dve", mc[:, 25:26], pc[:, 50:51], pc[:, 51:52], ALU.add, ["evpc"], ["mc"])
        S.ts("dve", mc[:], mc[:], -1.0, 1.0, ALU.mult, ALU.add, ["mc"], ["mc"])
        wup = S.sb(st, "wup", [128, 2, DA])
        S.dma("act", wup[:], I["ev_wup"][i].rearrange("g p c -> p g c"), writes=["wup"])
        masks = S.sb(st, "masks", [128, 5, 64])
        S.dma("sp", masks[:], I["c_masks"], writes=["masks"])
        ident = S.sb(st, "ident", [128, 128])
        S.dma("act", ident[:], I["c_ident"], writes=["ident"])
        bones = S.sb(st, "bones", [128, 128])
        S.dma("sp", bones[:], I["c_bones"], writes=["bones"])
        bones_s = S.sb(st, "bones_s", [128, 128])
        S.ts("dve", bones_s[:], bones[:], 1.0 / 64, None, ALU.mult, None, ["bones"], ["bones_s"])
        scanm = S.sb(st, "scanm", [128, 2, NB])
        S.dma("act", scanm[:], I["c_scanm"], writes=["scanm"])
        RT = S.sb(st, "RT", [128, NP, NB])
        KT = S.sb(st, "KT", [128, NP, NB])
        BT = S.sb(st, "BT", [128, NP, NB])
        AT = S.sb(st, "AT", [128, NP, NB])
        VT = S.sb(st, "VT", [128, NP, NB])
        KTt = S.sb(st, "KTt", [128, NCH, NP, 64])
        BTt = S.sb(st, "BTt", [128, NCH, NP, 64])
        VTt = S.sb(st, "VTt", [128, NCH, NP, 64])
        gCt = S.sb(st, "gCt", [128, NP, NCH])
        H = S.sb(st, "H", [128, NP, 64])
        XY = [S.sb(st, "XY%d" % m, [128, 2, 2, 64]) for m in range(NP)]
        Mt = [[S.sb(st, "M%d_%d" % (m, q), [128, 64]) for q in range(2)] for m in range(NP)]
        DE = [[S.sb(st, "DE%d_%d" % (m, q), [128, 3, 64]) for q in range(2)] for m in range(NP)]
        Wm = S.sb(st, "Wm", [128, NP, 64])
        U = S.sb(st, "U", [128, NP, 64])
        tmpH = S.sb(st, "tmpH", [128, NP, 64])
        Ost = S.sb(st, "Ost", [128, NP, NB])
        Ost2 = S.sb(st, "Ost2", [128, NP, NB])
        raw = [S.sb(st, "raw%d" % q, [128, 3, NB + 2]) for q in range(2)]
        craw = S.sb(st, "craw", [128, 2, NB + 2])
        cwT = S.sb(st, "cwT", [128, NB])
        caT = S.sb(st, "caT", [128, NB])
        tn = {}
        for nm in ("rs", "ks", "vs", "ld", "aa", "L", "eL", "eLn", "eLp", "kkr", "sq", "kk", "kd", "t1", "bvt", "t2"):
            tn[nm] = S.sb(st, "p_" + nm, [128, NB])
        psA = [S.ps(st, "psA%d" % q, [128, 512]) for q in range(2)]
        psL = [S.ps(st, "psL%d" % q, [128, 512]) for q in range(2)]
        psB = [S.ps(st, "psB%d" % q, [128, 512]) for q in range(2)]
        ppre = [S.ps(st, "ppre%d" % q, [128, 512]) for q in range(2)]
        I2 = S.sb(st, "I2", [128, 64])
        for h in range(2):
            S.cp("pool", I2[64 * h:64 * h + 64, :], ident[64 * h:64 * h + 64, 64 * h:64 * h + 64], ["ident"], ["I2"])

        def shift(eng, out, outn, src, srcn, c0, c1, cm):
            S.ts(eng, out, src[:, 1:NB + 1], cm, None, ALU.mult, None, [srcn, "evpc", "mc"], [outn])
            S.stt(eng, out, src[:, 0:NB], c0, out, ALU.mult, ALU.add, [srcn, "evpc", outn], [outn])
            S.stt(eng, out, src[:, 2:NB + 2], c1, out, ALU.mult, ALU.add, [srcn, "evpc", outn], [outn])

        nraw = 0
        n_pa = 0
        n_pl = 0
        n_pp = 0
        n_ev = 0
        for d in range(2):
            S.memset("pool", H[:], 0.0, ["H"])
            if d == 0:
                order = list(range(0, T, NB))
            else:
                order = [0] + list(range(T - NB, 0, -NB))
            for t0 in order[:maxblk]:
                lz = (t0 == 0 or t0 == CTX)
                rz = (t0 + NB == CTX or t0 + NB == T)
                lo = t0 if lz else t0 - 1
                hi = t0 + NB if rz else t0 + NB + 1
                c_lo = lo - (t0 - 1)
                c_hi = hi - (t0 - 1)
                if lz:
                    S.memset("pool", craw[:, :, 0:1], 0.0, ["craw"])
                if rz:
                    S.memset("pool", craw[:, :, NB + 1:NB + 2], 0.0, ["craw"])
                S.dma("sp", craw[:, :, c_lo:c_hi], FTc[:, :, lo:hi], writes=["craw"])
                shift("dve", cwT[:], "cwT", craw[:, 0, :], "craw", pc[:, 48:49], pc[:, 49:50], mc[:, 24:25])
                S.act(cwT[:], cwT[:], AF.Tanh, ["cwT"], ["cwT"])
                shift("pool", caT[:], "caT", craw[:, 1, :], "craw", pc[:, 50:51], pc[:, 51:52], mc[:, 25:26])
                for m in range(NP):
                    rb = nraw % 2
                    nraw += 1
                    rw = raw[rb]
                    rwn = ("raw", rb)
                    if lz:
                        S.memset("pool", rw[:, :, 0:1], 0.0, [rwn])
                    if rz:
                        S.memset("pool", rw[:, :, NB + 1:NB + 2], 0.0, [rwn])
                    S.dma("sp" if m % 2 == 0 else "act", rw[:, :, c_lo:c_hi], FT3[m][:, :, lo:hi], writes=[rwn])
                    rs, ks, vs = tn["rs"], tn["ks"], tn["vs"]
                    shift("dve", rs[:], "rs", rw[:, 0, :], rwn, pc[:, 0 + m:1 + m], pc[:, 8 + m:9 + m], mc[:, m:m + 1])
                    shift("pool", ks[:], "ks", rw[:, 1, :], rwn, pc[:, 16 + m:17 + m], pc[:, 24 + m:25 + m], mc[:, 8 + m:9 + m])
                    shift("dve", vs[:], "vs", rw[:, 2, :], rwn, pc[:, 32 + m:33 + m], pc[:, 40 + m:41 + m], mc[:, 16 + m:17 + m])
                    db = 64 * d
                    pa = ppre[n_pp % 2]
                    pan = ("ppre", n_pp % 2)
                    n_pp += 1
                    S.mm(pa[:, 0:NB], wup[db:db + 64, 0, 128 * m:128 * m + 128], cwT[db:db + 64, :], True, True, ["wup", "cwT"], [pan])
                    ld = tn["ld"]
                    S.act(ld[:], pa[:, 0:NB], AF.Sigmoid, [pan, "evpc"], ["ld"], bias=pc[:, 52 + 8 * d + m:53 + 8 * d + m])
                    S.ts("dve", ld[:], ld[:], -0.6065306597126334, None, ALU.mult, None, ["ld"], ["ld"])
                    pa2 = ppre[n_pp % 2]
                    pan2 = ("ppre", n_pp % 2)
                    n_pp += 1
                    S.mm(pa2[:, 0:NB], wup[db:db + 64, 1, 128 * m:128 * m + 128], caT[db:db + 64, :], True, True, ["wup", "caT"], [pan2])
                    aa = tn["aa"]
                    S.act(aa[:], pa2[:, 0:NB], AF.Sigmoid, [pan2, "evpc"], ["aa"], bias=pc[:, 68 + 8 * d + m:69 + 8 * d + m])
                    L, eL, eLn, eLp = tn["L"], tn["eL"], tn["eLn"], tn["eLp"]
                    if d == 0:
                        S.op("dve", (lambda e: e.tensor_tensor_scan(out=L[:], data0=scanm[:, 0, :], data1=ld[:], initial=0.0,
                                                                     op0=ALU.mult, op1=ALU.add)), ["scanm", "ld"], ["L"])
                    else:
                        S.op("dve", (lambda e: e.tensor_tensor_scan(out=L[:, ::-1], data0=scanm[:, 1, ::-1], data1=ld[:, ::-1], initial=0.0,
                                                                     op0=ALU.mult, op1=ALU.add)), ["scanm", "ld"], ["L"])
                    S.act(eL[:], L[:], AF.Exp, ["L"], ["eL"])
                    S.act(eLn[:], L[:], AF.Exp, ["L"], ["eLn"], scale=-1.0)
                    S.tt("pool", eLp[:], L[:], ld[:], ALU.subtract, ["L", "ld"], ["eLp"])
                    S.act(eLp[:], eLp[:], AF.Exp, ["eLp"], ["eLp"])
                    kkr, sq, kk, kd, t1, t2, bvt = tn["kkr"], tn["sq"], tn["kk"], tn["kd"], tn["t1"], tn["t2"], tn["bvt"]
                    S.ts("pool", kkr[:], ks[:], pc[:, 84 + m:85 + m], None, ALU.mult, None, ["ks", "evpc"], ["kkr"])
                    S.tt("pool", sq[:], kkr[:], kkr[:], ALU.mult, ["kkr"], ["sq"])
                    pa3 = ppre[n_pp % 2]
                    pan3 = ("ppre", n_pp % 2)
                    n_pp += 1
                    S.mm(pa3[:, 0:NB], bones[:], sq[:], True, True, ["bones", "sq"], [pan3])
                    S.ts("dve", sq[:], pa3[:, 0:NB], 1e-12, None, ALU.add, None, [pan3], ["sq"])
                    S.act(sq[:], sq[:], AF.Ln, ["sq"], ["sq"])
                    S.act(sq[:], sq[:], AF.Exp, ["sq"], ["sq"], scale=-0.5)
                    S.tt("dve", kk[:], kkr[:], sq[:], ALU.mult, ["kkr", "sq"], ["kk"])
                    S.ts("pool", t1[:], aa[:], -1.0, pc[:, 92 + m:93 + m], ALU.add, ALU.mult, ["aa", "evpc"], ["t1"])
                    S.stt("dve", kd[:], t1[:], 1.0, ks[:], ALU.add, ALU.mult, ["t1", "ks"], ["kd"])
                    S.stt("pool", t2[:], rs[:], pc[:, 100 + m:101 + m], kd[:], ALU.mult, ALU.mult, ["rs", "evpc", "kd"], ["t2"])
                    pa4 = ppre[n_pp % 2]
                    pan4 = ("ppre", n_pp % 2)
                    n_pp += 1
                    S.mm(pa4[:, 0:NB], bones[:], t2[:], True, True, ["bones", "t2"], [pan4])
                    S.tt("dve", bvt[:], pa4[:, 0:NB], vs[:], ALU.mult, [pan4, "vs"], ["bvt"])
                    S.dma("sp", K.OT[d, 1, 128 * m:128 * m + 128, t0:t0 + NB], bvt[:], reads=["bvt"], writes=[("OTb", d, m)])
                    rv = (lambda ap: ap[:, ::-1]) if d == 1 else (lambda ap: ap)
                    S.tt("dve", rv(RT[:, m, :]), rs[:], eL[:], ALU.mult, ["rs", "eL"], [("RT", m)])
                    S.tt("pool", rv(KT[:, m, :]), kd[:], eLn[:], ALU.mult, ["kd", "eLn"], [("KT", m)])
                    S.tt("pool", t1[:], kk[:], aa[:], ALU.mult, ["kk", "aa"], ["t1"])
                    S.tt("dve", rv(BT[:, m, :]), t1[:], eLn[:], ALU.mult, ["t1", "eLn"], [("BT", m)])
                    S.stt("pool", rv(AT[:, m, :]), kk[:], -1.0, eLp[:], ALU.mult, ALU.mult, ["kk", "eLp"], [("AT", m)])
                    S.cp("act", rv(VT[:, m, :]), vs[:], ["vs"], [("VT", m)])
                    if d == 0:
                        S.cp("pool", gCt[:, m, :], eL[:, 63::64], ["eL"], ["gCt"])
                    else:
                        S.cp("pool", gCt[:, m, :], eL[:, NB - 64::-64], ["eL"], ["gCt"])
                for c in range(NCH):
                    cs = slice(64 * c, 64 * c + 64)
                    for (src, srcn, dst, dstn) in ((KT, "KT", KTt, "KTt"), (BT, "BT", BTt, "BTt"), (VT, "VT", VTt, "VTt")):
                        pt = ppre[n_pp % 2]
                        ptn = ("ppre", n_pp % 2)
                        n_pp += 1
                        for m in range(NP):
                            for h in range(2):
                                pb = 64 * h
                                S.mm(pt[pb:pb + 64, m * 64:(m + 1) * 64], src[pb:pb + 64, m, cs], ident[pb:pb + 64, pb:pb + 64],
                                     True, True, [(srcn, m), "ident"], [ptn], signal=(m == NP - 1 and h == 1), tp=(pb, pb))
                        n_ev += 1
                        S.cp("act" if n_ev % 2 == 0 else "dve", dst[:, c, :, :], pt[:].rearrange("p (m k) -> p m k", k=64),
                             [ptn], [(dstn, c)])

                def stage_a(c):
                    nonlocal n_pa, n_pl
                    cs = slice(64 * c, 64 * c + 64)
                    par = c % 2
                    for m in range(NP):
                        pa_ = psA[n_pa % 2]
                        pn = ("psA", n_pa % 2)
                        n_pa += 1
                        pav = pa_[:, 0:320].rearrange("p (s w) -> p s w", w=64)
                        for h in range(2):
                            pb = 64 * h
                            ops = ((BT, "BT", AT, "AT"), (AT, "AT", BT, "BT"), (KT, "KT", AT, "AT"), (KT, "KT", RT, "RT"), (BT, "BT", RT, "RT"))
                            for sl, (la, lan, ra, ran) in enumerate(ops):
                                S.mm(pav[pb:pb + 64, sl, :], la[pb:pb + 64, m, cs], ra[pb:pb + 64, m, cs], True, True,
                                     [(lan, m), (ran, m)], [pn], signal=(h == 1 and sl == 4), tp=(pb, pb))
                        S.tt("dve", XY[m][:, 0, :, :], pav[:, 0:2, :], masks[:, 0:2, :], ALU.mult, [pn, "masks"], [("XY", m, 0)])
                        S.tt("dve", DE[m][par][:], pav[:, 2:5, :], masks[:, 2:5, :], ALU.mult, [pn, "masks"], [("DE", m, par)])
                        S.tt("pool", Mt[m][par][:], XY[m][:, 0, 0, :], I2[:], ALU.add, [("XY", m, 0), "I2"], [("M", m, par)])
                    for j in range(5):
                        cur, nxt = j % 2, 1 - (j % 2)
                        for m in range(NP):
                            pl = psL[n_pl % 2]
                            pln = ("psL", n_pl % 2)
                            n_pl += 1
                            plv = pl[:, 0:192].rearrange("p (s w) -> p s w", w=64)
                            for h in range(2):
                                pb = 64 * h
                                X_, Y_ = XY[m][pb:pb + 64, cur, 0, :], XY[m][pb:pb + 64, cur, 1, :]
                                if j < 4:
                                    S.mm(plv[pb:pb + 64, 0, :], Y_, X_, True, True, [("XY", m, cur)], [pln], signal=False, tp=(pb, pb))
                                S.mm(plv[pb:pb + 64, 1, :], X_, Y_, True, True, [("XY", m, cur)], [pln], signal=(h == 1), tp=(pb, pb))
                            if j < 4:
                                S.cp("act", XY[m][:, nxt, :, :], plv[:, 0:2, :], [pln], [("XY", m, nxt)])
                            else:
                                S.cp("act", XY[m][:, nxt, 1, :], plv[:, 1, :], [pln], [("XY", m, nxt)])
                            for h in range(2):
                                pb = 64 * h
                                S.mm(plv[pb:pb + 64, 2, :], XY[m][pb:pb + 64, nxt, 1, :], Mt[m][par][pb:pb + 64, :], True, True,
                                     [("XY", m, nxt), ("M", m, par)], [pln], signal=(h == 1), tp=(pb, pb))
                            S.tt("dve", Mt[m][par][:], plv[:, 2, :], Mt[m][par][:], ALU.add, [pln, ("M", m, par)], [("M", m, par)])

                def stage_b(c):
                    cs = slice(64 * c, 64 * c + 64)
                    par = c % 2
                    pw = psB[0][:].rearrange("p (m k) -> p m k", k=64)
                    pu = psB[1][:].rearrange("p (m k) -> p m k", k=64)
                    for m in range(NP):
                        for h in range(2):
                            pb = 64 * h
                            S.mm(pw[pb:pb + 64, m, :], AT[pb:pb + 64, m, cs], H[pb:pb + 64, m, :], True, False,
                                 [("AT", m), "H"], [("psB", 0)], signal=False, tp=(pb, pb))
                            S.mm(pw[pb:pb + 64, m, :], DE[m][par][pb:pb + 64, 0, :], VTt[pb:pb + 64, c, m, :], False, True,
                                 [("DE", m, par), ("VTt", c)], [("psB", 0)], signal=(m == NP - 1 and h == 1), tp=(pb, pb))
                    S.cp("act", Wm[:], pw, [("psB", 0)], ["Wm"])
                    for m in range(NP):
                        for h in range(2):
                            pb = 64 * h
                            S.mm(pu[pb:pb + 64, m, :], Mt[m][par][pb:pb + 64, :], Wm[pb:pb + 64, m, :], True, True,
                                 [("M", m, par), "Wm"], [("psB", 1)], signal=(m == NP - 1 and h == 1), tp=(pb, pb))
                    S.cp("act", U[:], pu, [("psB", 1)], ["U"])
                    for m in range(NP):
                        for h in range(2):
                            pb = 64 * h
                            S.mm(pw[pb:pb + 64, m, :], H[pb:pb + 64, m, :], RT[pb:pb + 64, m, cs], True, False,
                                 ["H", ("RT", m)], [("psB", 0)], signal=False, tp=(pb, pb))
                            S.mm(pw[pb:pb + 64, m, :], U[pb:pb + 64, m, :], DE[m][par][pb:pb + 64, 2, :], False, False,
                                 ["U", ("DE", m, par)], [("psB", 0)], signal=False, tp=(pb, pb))
                            S.mm(pw[pb:pb + 64, m, :], VTt[pb:pb + 64, c, m, :], DE[m][par][pb:pb + 64, 1, :], False, True,
                                 [("VTt", c), ("DE", m, par)], [("psB", 0)], signal=(m == NP - 1 and h == 1), tp=(pb, pb))
                    S.cp("act", Ost[:, :, cs], pw, [("psB", 0)], ["Ost"])
                    for m in range(NP):
                        for h in range(2):
                            pb = 64 * h
                            S.mm(pu[pb:pb + 64, m, :], BTt[pb:pb + 64, c, m, :], U[pb:pb + 64, m, :], True, False,
                                 [("BTt", c), "U"], [("psB", 1)], signal=False, tp=(pb, pb))
                            S.mm(pu[pb:pb + 64, m, :], KTt[pb:pb + 64, c, m, :], VTt[pb:pb + 64, c, m, :], False, True,
                                 [("KTt", c), ("VTt", c)], [("psB", 1)], signal=(m == NP - 1 and h == 1), tp=(pb, pb))
                    S.tt("dve", tmpH[:], pu, H[:], ALU.add, [("psB", 1), "H"], ["tmpH"])
                    S.tt("pool", H[:], tmpH[:], gCt[:, :, c:c + 1].to_broadcast([128, NP, 64]), ALU.mult, ["tmpH", "gCt"], ["H"])

                for c in range(NCH):
                    stage_a(c)
                    if c >= 1:
                        stage_b(c - 1)
                stage_b(NCH - 1)
                OTv = K.OT[d, 0].rearrange("(m p) t -> p m t", p=128)
                if d == 0:
                    S.dma("sp", OTv[:, :, t0:t0 + NB], Ost[:], reads=["Ost"], writes=[("OTo", d)])
                else:
                    S.cp("pool", Ost2[:, :, ::-1], Ost[:], ["Ost"], ["Ost2"])
                    S.dma("sp", OTv[:, :, t0:t0 + NB], Ost2[:], reads=["Ost2"], writes=[("OTo", d)])
        S.barrier()
        ro = {}
        for nm in ("o0", "o1", "b0", "b1", "gat", "cen", "sq2", "sg"):
            ro[nm] = [S.sb(st, "ro_%s%d" % (nm, q), [128, NB]) for q in range(2)]
        y16 = [S.sb(st, "ro_y%d" % q, [128, NB], BF16) for q in range(2)]
        nro = 0
        for m in range(NP):
            for t0 in range(0, T, NB):
                q = nro % 2
                nro += 1
                o0, o1, b0, b1, gat, cen, sq2, sg = (ro[nm][q] for nm in ("o0", "o1", "b0", "b1", "gat", "cen", "sq2", "sg"))
                tk = lambda nm: ("ro", nm, q)
                rsl = slice(128 * m, 128 * m + 128)
                S.dma("sp", o0[:], K.OT[0, 0, rsl, t0:t0 + NB], writes=[tk("o0")])
                S.dma("act", o1[:], K.OT[1, 0, rsl, t0:t0 + NB], writes=[tk("o1")])
                S.dma("sp", b0[:], K.OT[0, 1, rsl, t0:t0 + NB], writes=[tk("b0")])
                S.dma("act", b1[:], K.OT[1, 1, rsl, t0:t0 + NB], writes=[tk("b1")])
                S.dma("sp", gat[:], K.FT[3328 + 128 * m:3328 + 128 * m + 128, t0:t0 + NB], writes=[tk("gat")])
                S.tt("pool", o0[:], o0[:], o1[:], ALU.add, [tk("o0"), tk("o1")], [tk("o0")])
                S.tt("pool", b0[:], b0[:], b1[:], ALU.add, [tk("b0"), tk("b1")], [tk("b0")])
                pm_ = psA[q]
                S.mm(pm_[:, 0:NB], bones_s[:], o0[:], True, True, ["bones_s", tk("o0")], [("psA", q)])
                S.tt("dve", cen[:], o0[:], pm_[:, 0:NB], ALU.subtract, [tk("o0"), ("psA", q)], [tk("cen")])
                S.tt("pool", sq2[:], cen[:], cen[:], ALU.mult, [tk("cen")], [tk("sq2")])
                pv_ = psL[q]
                S.mm(pv_[:, 0:NB], bones_s[:], sq2[:], True, True, ["bones_s", tk("sq2")], [("psL", q)])
                S.ts("dve", sq2[:], pv_[:, 0:NB], GN_EPS, None, ALU.add, None, [("psL", q)], [tk("sq2")])
                S.act(sq2[:], sq2[:], AF.Ln, [tk("sq2")], [tk("sq2")])
                S.act(sq2[:], sq2[:], AF.Exp, [tk("sq2")], [tk("sq2")], scale=-0.5)
                S.tt("dve", cen[:], cen[:], sq2[:], ALU.mult, [tk("cen"), tk("sq2")], [tk("cen")])
                S.ts("dve", cen[:], cen[:], pc[:, 108 + m:109 + m], pc[:, 116 + m:117 + m], ALU.mult, ALU.add, [tk("cen"), "evpc"], [tk("cen")])
                S.tt("pool", cen[:], cen[:], b0[:], ALU.add, [tk("cen"), tk("b0")], [tk("cen")])
                S.act(sg[:], gat[:], AF.Sigmoid, [tk("gat")], [tk("sg")])
                S.tt("pool", gat[:], gat[:], sg[:], ALU.mult, [tk("gat"), tk("sg")], [tk("gat")])
                S.tt("dve", y16[q][:], cen[:], gat[:], ALU.mult, [tk("cen"), tk("gat")], [("roy", q)])
                S.dma("act", K.YT[rsl, t0:t0 + NB], y16[q][:], reads=[("roy", q)], writes=[("YT", "all")])


def phase_na(K, l):
    S, I = K.S, K.I
    i = l // 2
    import os
    maxrows = int(os.environ.get("NA_MAXROWS", "1000"))
    npairs = int(os.environ.get("NA_PAIRS", "8"))
    with contextlib.ExitStack() as st:
        ones = S.sb(st, "ones", [128, 64], BF16)
        S.memset("pool", ones[:], 1.0, ["ones"])
        q2 = [S.sb(st, "q2_%d" % b, [128, T], BF16) for b in range(2)]
        k2 = [S.sb(st, "k2_%d" % b, [128, T], BF16) for b in range(2)]
        Ve = [S.sb(st, "Ve%d" % b, [128, NTILE, 128], BF16) for b in range(2)]
        Vo = [S.sb(st, "Vo%d" % b, [128, NTILE - 1, 128], BF16) for b in range(2)]
        tb2 = [S.sb(st, "tb2_%d" % b, [128, 2, 14, 64]) for b in range(2)]
        gb = S.sb(st, "gb", [128, T])
        sgb = S.sb(st, "sgb", [128, T])
        ybT = S.sb(st, "ybT", [128, T])
        yb16 = S.sb(st, "yb16", [128, T], BF16)
        sT = [S.sb(st, "sT%d" % b, [128, 2, 256]) for b in range(2)]
        pT = [S.sb(st, "pT%d" % b, [128, 2, 384], BF16) for b in range(2)]
        rden = [S.sb(st, "rden%d" % b, [128, 64]) for b in range(2)]
        pss = [S.ps(st, "pss%d" % b, [128, 2, 512]) for b in range(2)]
        pso = [S.ps(st, "pso%d" % b, [128, 512]) for b in range(2)]
        VBe = K.VB.rearrange("(i p) c -> p i c", p=128)
        VBo = K.VB[64:64 + (NTILE - 1) * 128, :].rearrange("(i p) c -> p i c", p=128)
        nrow = 0
        for m in range(npairs):
            b = m % 2
            S.dma("sp", q2[b][:], K.QK[128 * m:128 * m + 128, :], writes=[("q2", b)])
            S.dma("act", k2[b][:], K.QK[1024 + 128 * m:1024 + 128 * m + 128, :], writes=[("k2", b)])
            for hh in range(2):
                i0, i1 = hh * 17, hh * 17 + 17
                S.dma("sp", Ve[b][:, i0:i1, :], VBe[:, i0:i1, 128 * m:128 * m + 128], writes=[("Ve", b)])
                j1 = min(i1, NTILE - 1)
                S.dma("act", Vo[b][:, i0:j1, :], VBo[:, i0:j1, 128 * m:128 * m + 128], writes=[("Vo", b)])
            S.dma("sp", tb2[b][:], I["ev_tb"][i, 2 * m:2 * m + 2].rearrange("h p s w -> p h s w"), writes=[("tb2", b)])
            S.dma("act", gb[:], K.FT[4352 + 128 * m:4352 + 128 * m + 128, :], writes=["gb"])
            rows = [("c", r) for r in range(CTX // 64)] + [("x", y) for y in range(64)]
            rows = rows[:maxrows]

            def rowinfo(kind, y):
                if kind == "x":
                    y0 = min(max(y - 4, 0), 56)
                    p_ = y - y0
                    tok0 = CTX + 64 * y0
                    q0 = CTX + 64 * y
                    chunks = [tok0 + 128 * c for c in range(4)] + [0, 128]
                else:
                    p_ = 0
                    q0 = 64 * y
                    chunks = [0, 128]
                return p_, q0, chunks

            def s1(kind, y, rb):
                p_, q0, chunks = rowinfo(kind, y)
                nch = len(chunks)
                for h in range(2):
                    pb = 64 * h
                    for c, kt0 in enumerate(chunks):
                        S.mm(pss[rb][:, h, c * 64:(c + 1) * 64], k2[b][pb:pb + 64, kt0:kt0 + 128], q2[b][pb:pb + 64, q0:q0 + 64],
                             True, True, [("k2", b), ("q2", b)], [("pss", rb, h)], signal=(c == nch - 1))

            def s2(kind, y, rb):
                p_, q0, chunks = rowinfo(kind, y)
                nch = len(chunks)
                psr = [("pss", rb, 0), ("pss", rb, 1)]
                if kind == "x":
                    for h in range(2):
                        S.stt("dve", sT[rb][:, h, :].rearrange("p (c w) -> p c w", w=64),
                              pss[rb][:, h, 0:256].rearrange("p (c w) -> p c w", w=64), 0.125,
                              tb2[b][:, h, 7 - p_:7 - p_ + 7:2, :], ALU.mult, ALU.add, [("pss", rb, h), ("tb2", b)], [("sT", rb)])
                    S.act(pT[rb][:, :, 0:256], sT[rb][:], AF.Exp, [("sT", rb)], [("pT", rb)])
                    S.act(pT[rb][:, :, 256:384], pss[rb][:, :, 256:384], AF.Exp, psr, [("pT", rb)], scale=0.125)
                else:
                    S.act(pT[rb][:, :, 0:128], pss[rb][:, :, 0:128], AF.Exp, psr, [("pT", rb)], scale=0.125)
                for h in range(2):
                    pb = 64 * h
                    for c, kt0 in enumerate(chunks):
                        if kt0 % 128 == 0:
                            vch = Ve[b][:, kt0 // 128, pb:pb + 64]
                            vr = ("Ve", b)
                        else:
                            vch = Vo[b][:, (kt0 - 64) // 128, pb:pb + 64]
                            vr = ("Vo", b)
                        S.mm(pso[rb][pb:pb + 64, 0:64], vch, pT[rb][:, h, c * 64:(c + 1) * 64], c == 0, c == nch - 1,
                             [vr, ("pT", rb)], [("pso", rb)], signal=False, tp=(0, pb))
                    for c in range(nch):
                        S.mm(pso[rb][pb:pb + 64, 64:128], ones[:], pT[rb][:, h, c * 64:(c + 1) * 64], c == 0, c == nch - 1,
                             ["ones", ("pT", rb)], [("pso", rb)], signal=(h == 1 and c == nch - 1), tp=(0, pb))
                S.op("dve", (lambda e, rb=rb: e.reciprocal(out=rden[rb][:], in_=pso[rb][:, 64:128])), [("pso", rb)], [("rden", rb)])
                S.tt("dve", ybT[:, q0:q0 + 64], pso[rb][:, 0:64], rden[rb][:], ALU.mult, [("pso", rb), ("rden", rb)], ["ybT"])

            if rows:
                s1(rows[0][0], rows[0][1], nrow % 2)
            for ri, (kind, y) in enumerate(rows):
                rb = nrow % 2
                nrow += 1
                if ri + 1 < len(rows):
                    s1(rows[ri + 1][0], rows[ri + 1][1], nrow % 2)
                s2(kind, y, rb)
            S.act(sgb[:], gb[:], AF.Sigmoid, ["gb"], ["sgb"])
            S.tt("pool", sgb[:], sgb[:], gb[:], ALU.mult, ["sgb", "gb"], ["sgb"])
            S.tt("pool", yb16[:], ybT[:], sgb[:], ALU.mult, ["ybT", "sgb"], ["yb16"])
            S.dma("sp", K.YT[1024 + 128 * m:1024 + 128 * m + 128, :], yb16[:], reads=["yb16"], writes=[("YT", "all")])


def _col(v, n=128):
    v = np.asarray(v, np.float32)
    return np.ascontiguousarray(v.reshape(-1, n).T)


def host_constants():
    import ml_dtypes
    c = {}
    c["c_ident"] = np.eye(128, dtype=np.float32)
    c["c_identb"] = np.eye(128, dtype=np.float32).astype(ml_dtypes.bfloat16)
    m = np.zeros((128, 5, 64), np.float32)
    su = np.triu(np.ones((64, 64), np.float32), 1)
    iu = np.triu(np.ones((64, 64), np.float32), 0)
    for h in range(2):
        m[h * 64:(h + 1) * 64, 0] = su
        m[h * 64:(h + 1) * 64, 1] = su.T
        m[h * 64:(h + 1) * 64, 2] = su
        m[h * 64:(h + 1) * 64, 3] = iu
        m[h * 64:(h + 1) * 64, 4] = iu
    c["c_masks"] = m
    bo = np.zeros((128, 128), np.float32)
    bo[:64, :64] = 1.0
    bo[64:, 64:] = 1.0
    c["c_bones"] = bo
    sm = np.ones((128, 2, 256), np.float32)
    sm[:, 0, 0::64] = 0.0
    sm[:, 1, 63::64] = 0.0
    c["c_scanm"] = sm
    return c


def host_layout(inputs, b):
    f = lambda k: np.asarray(inputs[k], np.float32)
    m = {}
    m["xin"] = np.ascontiguousarray(np.concatenate([f("ctx")[b], f("x")[b]], axis=0))
    ccm = np.stack([f("c")[b], f("c_ctx")], axis=-1)
    m["cc"] = np.ascontiguousarray(ccm.reshape(16, 128, 2).transpose(1, 0, 2))
    for k in ("mod_w", "mod_b", "norm_pre", "norm_post", "ev_w_in", "ev_w_out", "od_w_in", "od_w_out"):
        m[k] = f(k)
    m["od_gw"] = np.ascontiguousarray(np.stack([f("od_gate_a_w"), f("od_gate_x_w")], axis=1))
    pcs = []
    for i in range(2):
        cols = [_col(f("od_conv_w")[i, j]) for j in range(4)]
        cols.append(_col(f("od_conv_b")[i]))
        cols += [_col(f("od_gate_a_b")[i, d]) for d in range(2)]
        cols += [_col(f("od_gate_x_b")[i, d]) for d in range(2)]
        cols += [_col(f("od_lambda")[i, d]) for d in range(2)]
        pcs.append(np.stack(cols, axis=1))
    m["od_pc"] = np.ascontiguousarray(np.stack(pcs, axis=0))
    evs = []
    for i in range(2):
        mu = f("ev_mu")[i]
        cols = []
        for part in range(3):
            for j in range(2):
                cols.append(_col(mu[j, part * DA:(part + 1) * DA]))
        cols.append(_col(mu[0, 3 * DA:3 * DA + 128]))
        cols.append(_col(mu[1, 3 * DA:3 * DA + 128]))
        cols.append(_col(mu[0, 3 * DA + 128:3 * DA + 256]))
        cols.append(_col(mu[1, 3 * DA + 128:3 * DA + 256]))
        for d in range(2):
            cols.append(_col(f("ev_w0")[i, d]))
        for d in range(2):
            cols.append(_col(f("ev_a0")[i, d]))
        cols.append(_col(f("ev_k_k")[i]))
        cols.append(_col(f("ev_k_a")[i]))
        cols.append(_col(f("ev_r_k")[i].reshape(-1)))
        cols.append(_col(f("ev_gn_w")[i]))
        cols.append(_col(f("ev_gn_b")[i]))
        ev = np.concatenate(cols, axis=1)
        assert ev.shape[1] == EVPC_N, ev.shape
        evs.append(ev)
    m["ev_pc"] = np.ascontiguousarray(np.stack(evs, axis=0))
    wup = np.stack([f("ev_w_up").reshape(2, 128, DA), f("ev_a_up").reshape(2, 128, DA)], axis=1)
    m["ev_wup"] = np.ascontiguousarray(wup)
    rpb = f("ev_rpb")
    cols_ = np.arange(GRID_W)
    cstart = np.clip(cols_ - 8, 0, GRID_W - 16)
    tb = np.full((2, 16, 128, 14, 64), -60.0, np.float32)
    cc_, ww_ = np.meshgrid(np.arange(64), np.arange(64), indexing="ij")
    valid = (cc_ >= cstart[ww_]) & (cc_ < cstart[ww_] + 16)
    dx = np.clip(cc_ - ww_ + 15, 0, 30)
    for half in range(2):
        for slot in range(14):
            g = rpb[:, :, slot + half, :][:, :, dx]
            tb[:, :, half * 64:(half + 1) * 64, slot, :] = np.where(valid[None, None], g, np.float32(-60.0))
    m["ev_tb"] = tb
    m.update(host_constants())
    return m


_CACHE = {}


def kernel(**inputs):
    layers = (0, 1, 2, 3)
    key = ("full", layers)
    if key not in _CACHE:
        _CACHE[key] = build_program(list(layers))
    nc = _CACHE[key]
    n = 4
    in_maps = [host_layout(inputs, b) for b in range(n)]
    res = run_bass_kernel_spmd(nc, in_maps, core_ids=list(range(n)))
    out = np.stack([np.asarray(res.results[b]["out"], np.float32) for b in range(n)], axis=0)
    return out
```
